# Optimizing a Trainium2 kernel written in Bass

```python
import jax, jax.numpy as jnp
from jax import lax
import numpy as np

D_MODEL = 2048
BATCH = 4
SEQ = 2048
DEPTH = 4

EPS = 1e-6
N_EVEN = (DEPTH + 1) // 2
N_ODD = DEPTH // 2
D_CONV = D_MODEL // 2
D_POOL = D_MODEL - D_CONV
CONV_WIDTH = 31
POOL_WINDOWS = (2, 4, 8, 16)
POOL_GROUPS = len(POOL_WINDOWS)
POOL_GROUP_DIM = D_POOL // POOL_GROUPS
D_IN = 2 * D_CONV + D_POOL
FOURIER_GROUPS = 8
FOURIER_GROUP_DIM = D_MODEL // FOURIER_GROUPS
D_FF = 4 * D_MODEL
PLE_DIM = 256

kernel_name = "hybrid_conv_pool_fourier_encoder"


def rms_norm(x, g):
    x32 = x.astype(jnp.float32)
    y = x32 * lax.rsqrt(jnp.mean(x32 * x32, axis=-1, keepdims=True) + EPS)
    return (y * g.astype(jnp.float32)).astype(x.dtype)


def layer_norm(x, g, b):
    x32 = x.astype(jnp.float32)
    mu = jnp.mean(x32, axis=-1, keepdims=True)
    xc = x32 - mu
    y = xc * lax.rsqrt(jnp.mean(xc * xc, axis=-1, keepdims=True) + EPS)
    return (y * g.astype(jnp.float32) + b.astype(jnp.float32)).astype(x.dtype)


def conformer_conv(u, conv_w, conv_b, ln_g, ln_b):
    a, gate = jnp.split(u, 2, axis=-1)
    v = a * jax.nn.sigmoid(gate)
    half = CONV_WIDTH // 2
    v = lax.conv_general_dilated(
        v, conv_w[:, None, :], window_strides=(1,), padding=[(half, half)],
        dimension_numbers=("NWC", "WIO", "NWC"), feature_group_count=D_CONV) + conv_b
    return jax.nn.silu(layer_norm(v, ln_g, ln_b))


def multiscale_pool(u, pool_w, pool_scale):
    b, s, _ = u.shape
    ug = u.reshape(b, s, POOL_GROUPS, POOL_GROUP_DIM)
    ug32 = ug.astype(jnp.float32)
    cs = jnp.concatenate([jnp.zeros((b, 1, POOL_GROUPS, POOL_GROUP_DIM), jnp.float32),
                          jnp.cumsum(ug32, axis=1)], axis=1)
    t = np.arange(s)
    means = []
    for gi, win in enumerate(POOL_WINDOWS):
        lo = np.clip(t - win // 2, 0, s - 1)
        hi = np.clip(t + win // 2 - 1, 0, s - 1) + 1
        cnt = (hi - lo).astype(np.float32)[None, :, None]
        csg = cs[:, :, gi]
        means.append((csg[:, hi] - csg[:, lo]) / cnt)
    mixed = (jnp.stack(means, axis=2) - ug32).astype(u.dtype)
    y = jnp.einsum("bsgc,gcd->bsgd", mixed, pool_w).reshape(b, s, D_POOL)
    return y * pool_scale


def fourier_mix(h):
    b, s, d = h.shape
    hg = h.reshape(b, s, FOURIER_GROUPS, FOURIER_GROUP_DIM).astype(jnp.float32)
    f = jnp.fft.fftn(hg, axes=(1, 3), norm="ortho").real
    return f.reshape(b, s, d).astype(h.dtype)


def setup_inputs(seed: int = 0) -> dict:
    key = jax.random.key(seed)
    ks = iter(jax.random.split(key, 32))

    def w(shape, fan_in):
        return jax.random.normal(next(ks), shape, jnp.float32) * (fan_in ** -0.5)

    def gain(shape):
        return 1.0 + 0.1 * jax.random.normal(next(ks), shape, jnp.float32)

    def small(shape):
        return 0.02 * jax.random.normal(next(ks), shape, jnp.float32)

    return {
        "x": jax.random.normal(next(ks), (BATCH, SEQ, D_MODEL), jnp.float32),
        "p": jax.random.normal(next(ks), (DEPTH, BATCH, SEQ, PLE_DIM), jnp.float32),
        "ev_norm": gain((N_EVEN, D_MODEL)),
        "ev_w_in": w((N_EVEN, D_MODEL, D_IN), D_MODEL),
        "ev_conv_w": w((N_EVEN, CONV_WIDTH, D_CONV), CONV_WIDTH),
        "ev_conv_b": small((N_EVEN, D_CONV)),
        "ev_ln_g": gain((N_EVEN, D_CONV)),
        "ev_ln_b": small((N_EVEN, D_CONV)),
        "ev_pool_w": w((N_EVEN, POOL_GROUPS, POOL_GROUP_DIM, POOL_GROUP_DIM), POOL_GROUP_DIM),
        "ev_pool_scale": gain((N_EVEN, D_POOL)),
        "ev_w_out": w((N_EVEN, D_CONV + D_POOL, D_MODEL), D_CONV + D_POOL),
        "od_norm": gain((N_ODD, D_MODEL)),
        "od_w_out": w((N_ODD, D_MODEL, D_MODEL), D_MODEL),
        "mlp_norm": gain((DEPTH, D_MODEL)),
        "mlp_w1": w((DEPTH, D_MODEL, D_FF), D_MODEL),
        "mlp_w2": w((DEPTH, D_FF, D_MODEL), D_FF),
        "ple_norm": gain((DEPTH, D_MODEL)),
        "ple_w_gate": w((DEPTH, D_MODEL, D_MODEL), D_MODEL),
        "ple_w_proj": w((DEPTH, PLE_DIM, D_MODEL), PLE_DIM),
        "final_norm": gain((D_MODEL,)),
    }


def reference(x, p, ev_norm, ev_w_in, ev_conv_w, ev_conv_b, ev_ln_g, ev_ln_b,
              ev_pool_w, ev_pool_scale, ev_w_out, od_norm, od_w_out,
              mlp_norm, mlp_w1, mlp_w2, ple_norm, ple_w_gate, ple_w_proj, final_norm):
    for i in range(DEPTH):
        j = i // 2
        if i % 2 == 0:
            h = rms_norm(x, ev_norm[j])
            u = h @ ev_w_in[j]
            y_conv = conformer_conv(u[..., :2 * D_CONV], ev_conv_w[j], ev_conv_b[j],
                                    ev_ln_g[j], ev_ln_b[j])
            y_pool = multiscale_pool(u[..., 2 * D_CONV:], ev_pool_w[j], ev_pool_scale[j])
            x = x + jnp.concatenate([y_conv, y_pool], axis=-1) @ ev_w_out[j]
        else:
            h = rms_norm(x, od_norm[j])
            x = x + fourier_mix(h) @ od_w_out[j]
        h = rms_norm(x, mlp_norm[i])
        x = x + jnp.square(jax.nn.relu(h @ mlp_w1[i])) @ mlp_w2[i]
        g = jax.nn.sigmoid(rms_norm(x, ple_norm[i]) @ ple_w_gate[i])
        x = x + g * (p[i] @ ple_w_proj[i])
    return rms_norm(x, final_norm)
```

```python
import contextlib
import numpy as np
import ml_dtypes
import concourse.bass as bass
import concourse.mybir as mybir
from concourse.bass_utils import run_bass_kernel_spmd

F32 = mybir.dt.float32
BF16 = mybir.dt.bfloat16
AF = mybir.ActivationFunctionType
ALU = mybir.AluOpType

D = 2048
S = 2048
T = 1024
NT = 2
DEPTH = 4
DFF = 8192
PLE = 256
WINDOWS = (2, 4, 8, 16)
EPS = 1e-6
NCH = 16

PE, ACT, DVE, POOL, SP = "pe", "act", "dve", "pool", "sp"
SEM_ROT = 3000
N_DMA_SEMS = 12

NORM_COL = {}
_c = 0
for _n in ["ev0", "ev1", "od0", "od1", "mlp0", "mlp1", "mlp2", "mlp3", "ple0", "ple1", "ple2", "ple3", "fin"]:
    NORM_COL[_n] = _c
    _c += 16
CONVW_COL = _c; _c += 2 * 31 * 8
CONVB_COL = _c; _c += 16
LNG_COL = _c; _c += 16
LNB_COL = _c; _c += 16
PSC_COL = _c; _c += 16
NV = _c


class Op:
    __slots__ = ("eng", "fn", "deps", "kind", "signalled", "tok", "idx")

    def __init__(self, eng, fn, kind):
        self.eng = eng
        self.fn = fn
        self.kind = kind
        self.deps = set()
        self.signalled = False
        self.tok = None


class Prog:
    def __init__(self):
        self.ops = []
        self.streams = {e: [] for e in (PE, ACT, DVE, POOL, SP)}
        self.lastw = {}
        self.readers = {}

    def op(self, eng, fn, reads=(), writes=(), kind="c"):
        o = Op(eng, fn, kind)
        o.idx = len(self.ops)
        self.ops.append(o)
        self.streams[eng].append(o)
        deps = o.deps
        for k in reads:
            w = self.lastw.get(k)
            if w is not None:
                deps.add(w)
        for k in writes:
            w = self.lastw.get(k)
            if w is not None:
                deps.add(w)
            rs = self.readers.get(k)
            if rs:
                deps.update(rs.values())
        for k in reads:
            rs = self.readers.setdefault(k, {})
            rk = eng if kind == "c" else ("d", o.idx)
            rs[rk] = o.idx
        for k in writes:
            self.lastw[k] = o.idx
            self.readers[k] = {}
        deps.discard(o.idx)
        return o

    def emit(self, nc):
        ops = self.ops
        for o in ops:
            for d in o.deps:
                p = ops[d]
                if p.kind == "c" and p.eng == o.eng and p.eng == PE:
                    continue
                p.signalled = True
        engs = {PE: nc.tensor, ACT: nc.scalar, DVE: nc.vector, POOL: nc.gpsimd, SP: nc.sync}
        stack = contextlib.ExitStack()
        sem_lists = {}
        for e in engs:
            n_sig = sum(1 for o in self.streams[e] if o.kind == "c" and o.signalled)
            n = max(1, (n_sig + SEM_ROT - 1) // SEM_ROT)
            sem_lists[e] = [stack.enter_context(nc.semaphore(f"s_{e}_{i}")) for i in range(n)]
        dma_sems = {e: [stack.enter_context(nc.semaphore(f"d_{e}_{i}")) for i in range(N_DMA_SEMS)]
                    for e in (POOL, SP)}
        cc_sem = stack.enter_context(nc.semaphore("cc_sem"))
        cc_count = 0
        for e in engs:
            cnt = 0
            dma_i = 0
            dma_val = {}
            for o in self.streams[e]:
                if o.kind == "c":
                    if o.signalled:
                        o.tok = (sem_lists[e][cnt // SEM_ROT], cnt % SEM_ROT + 1)
                        cnt += 1
                elif o.kind == "d":
                    si = dma_i % N_DMA_SEMS
                    dma_i += 1
                    prev = dma_val.get(si, 0)
                    o.tok = (dma_sems[e][si], prev + 16, prev)
                    dma_val[si] = prev + 16
                elif o.kind == "cc":
                    cc_count += 1
                    o.tok = (cc_sem, cc_count)
        block = stack.enter_context(nc.Block())
        streams = self.streams

        def run_stream(e, eng):
            waited = {}
            for o in streams[e]:
                need = {}
                for d in o.deps:
                    p = ops[d]
                    if p.kind == "c" and p.eng == e and e == PE:
                        continue
                    s, v = p.tok[0], p.tok[1]
                    key = id(s)
                    if waited.get(key, 0) >= v:
                        continue
                    if key not in need or need[key][1] < v:
                        need[key] = (s, v)
                if o.kind == "d" and o.tok[2] > 0:
                    s, v = o.tok[0], o.tok[2]
                    key = id(s)
                    if waited.get(key, 0) < v and (key not in need or need[key][1] < v):
                        need[key] = (s, v)
                for key, (s, v) in need.items():
                    eng.wait_ge(s, v)
                    waited[key] = v
                ins = o.fn(eng)
                if o.kind == "c":
                    if o.signalled:
                        ins.then_inc(o.tok[0], 1)
                elif o.kind == "d":
                    ins.then_inc(o.tok[0], 16)
                else:
                    ins.then_inc(o.tok[0])
            dv = {}
            for o in streams[e]:
                if o.kind == "d":
                    dv[id(o.tok[0])] = (o.tok[0], o.tok[1])
            for s, v in dv.values():
                if waited.get(id(s), 0) < v:
                    eng.wait_ge(s, v)

        @block.tensor
        def _(eng):
            run_stream(PE, eng)

        @block.scalar
        def _(eng):
            run_stream(ACT, eng)

        @block.vector
        def _(eng):
            run_stream(DVE, eng)

        @block.gpsimd
        def _(eng):
            run_stream(POOL, eng)

        @block.sync
        def _(eng):
            run_stream(SP, eng)

        stack.close()


class Builder:
    def __init__(self, layers, do_final, x_from_dram_T=True):
        self.layers = list(layers)
        self.do_final = do_final
        self.nc = bass.Bass("TRN2", target_bir_lowering=False)
        self.P = Prog()
        self.bank_i = 0
        self.slot_i = 0
        self.cnt = {}
        self.prebuilt = None
        self.stats_ready = False
        self.pending_stats = []

    def din(self, name, shape, dt=F32):
        return self.nc.dram_tensor(name, list(shape), dt, kind="ExternalInput").ap()

    def next_bank(self):
        b = self.bank_i % 6
        self.bank_i += 1
        return b

    def next_slot(self):
        s = self.slot_i % 4
        self.slot_i += 1
        return s

    def ring(self, name, n):
        i = self.cnt.get(name, 0)
        self.cnt[name] = i + 1
        return i % n

    def r2keys(self, row, a, b):
        ks = []
        if a < 512:
            ks.append(("r2", row, 0))
        if b > 512:
            ks.append(("r2", row, 1))
        return ks

    def build(self):
        nc = self.nc
        P = self.P
        L = self.layers
        st = contextlib.ExitStack()
        self.st = st

        self.x_in = self.din("xT", [D, T])
        self.p_in = self.din("pT", [DEPTH, PLE, T])
        self.vecs_in = self.din("vecs", [128, NV])
        self.masks_in = self.din("masks", [128, 2])
        self.edge_in = self.din("edge", [128, 64])
        self.ccsc_in = self.din("ccsc", [256, 512])
        self.ident_in = self.din("ident", [128, 128])
        self.cs_in = self.din("cs", [2, 128, 4096], BF16)
        self.nss_in = self.din("nss", [2, 128, 4096], BF16)
        self.w = {}
        for i in L:
            j = i // 2
            if i % 2 == 0:
                self.w[f"ev_w_in{j}"] = self.din(f"ev_w_in{j}", [D, 3072])
                self.w[f"ev_pool_w{j}"] = self.din(f"ev_pool_w{j}", [1024, 256])
                self.w[f"ev_w_out{j}"] = self.din(f"ev_w_out{j}", [D, D])
            else:
                self.w[f"od_w_out{j}"] = self.din(f"od_w_out{j}", [D, D])
            self.w[f"mlp_w1{i}"] = self.din(f"mlp_w1{i}", [D, DFF])
            self.w[f"mlp_w2{i}"] = self.din(f"mlp_w2{i}", [DFF, D])
            self.w[f"ple_w_gate{i}"] = self.din(f"ple_w_gate{i}", [D, D])
            self.w[f"ple_w_proj{i}"] = self.din(f"ple_w_proj{i}", [PLE, D])
        self.out = nc.dram_tensor("outT", [D, T], F32, kind="ExternalOutput").ap()
        self.halo_in = [nc.dram_tensor(f"halo_in{k}", [128, 256], BF16) for k in range(2)]
        self.halo_out = [nc.dram_tensor(f"halo_out{k}", [256, 256], BF16) for k in range(2)]
        self.hsend = [nc.dram_tensor(f"hsend{q}", [128, 2 * T], BF16) for q in range(8)]
        self.hfull = [nc.dram_tensor(f"hfull{q}", [256, 2 * T], BF16) for q in range(8)]

        def sb(name, shape, dt):
            return st.enter_context(nc.sbuf_tensor(name, shape, dt))

        self.xT = sb("xT_sb", [128, NCH, T], F32)
        self.wring = [sb(f"wring{i}", [128, 8, 512], BF16) for i in range(4)]
        self.R1 = sb("R1", [128, NCH, T], BF16)
        R2 = sb("R2", [128, 16 * 1056], BF16)
        self.R2v = R2[:].rearrange("p (c t) -> p c t", t=1056)
        R3 = sb("R3", [128, 8192], BF16)
        self.R3b = R3[:].rearrange("p (c t) -> p c t", t=1024)
        self.R3f = R3[:].bitcast(F32).rearrange("p (c t) -> p c t", t=512)
        self.R3h = R3[:].rearrange("p (b k r t) -> p b k r t", b=2, k=2, r=2)
        self.wpT = R3[:, 0:4096].rearrange("p (k n) -> p k n", k=2)
        self.pT = R3[:, 4096:6144].rearrange("p (k t) -> p k t", k=2)
        R1f = self.R1[:].rearrange("p c t -> p (c t)").bitcast(F32)
        self.tA = R1f[:, 0:1040]
        self.tB = R1f[:, 1536:1536 + 1040]
        self.vecs = sb("vecs_sb", [128, NV], F32)
        self.masks = sb("masks_sb", [128, 2], F32)
        self.edge = sb("edge_sb", [128, 4, 16], F32)
        self.ccsc = sb("ccsc_sb", [128, 2, 512], BF16)
        self.poolw = sb("poolw_sb", [128, 8, 256], BF16)
        self.haloT = [sb(f"haloT_sb{k}", [128, 2, 256], BF16) for k in range(2)]
        self.ones_bf = sb("ones_bf", [128, 128], BF16)
        self.ones_ln = sb("ones_ln", [128, 128], BF16)
        self.eps = sb("eps_sb", [128, 1], F32)
        self.ident = sb("ident_sb", [128, 128], BF16)
        self.convw_bf = sb("convw_bf", [128, 2 * 31 * 8], BF16)
        self.sq = [sb(f"sq{i}", [128, 512], BF16) for i in range(4)]
        self.lnt = sb("lnt", [128, 512], F32)
        self.rstd = sb("rstd", [128, T], F32)
        self.f32tmp = [sb(f"f32tmp{i}", [128, 512], F32) for i in range(3)]
        self.edget = sb("edget", [128, 8], F32)
        self.edgeres = [sb(f"edgeres{i}", [128, 8], F32) for i in range(2)]
        self.ps = [st.enter_context(nc.psum_tensor(f"ps{i}", [128, 512], F32)) for i in range(8)]

        P.op(SP, lambda e: e.dma_start(out=self.vecs[:], in_=self.vecs_in), writes=["vecs"], kind="d")
        P.op(SP, lambda e: e.dma_start(out=self.masks[:], in_=self.masks_in), writes=["masks"], kind="d")
        P.op(SP, lambda e: e.dma_start(out=self.edge[:], in_=self.edge_in.rearrange("p (g t) -> p g t", t=16)),
             writes=["edge"], kind="d")
        P.op(POOL, lambda e: e.dma_start(out=self.ccsc[:], in_=self.ccsc_in.rearrange("(k p) n -> p k n", p=128)),
             writes=["ccsc"], kind="d")
        P.op(POOL, lambda e: e.dma_start(out=self.ident[:], in_=self.ident_in), writes=["ident"], kind="d")
        P.op(DVE, lambda e: e.memset(self.ones_bf[:], 1.0 / D), writes=["ones_bf"])
        P.op(DVE, lambda e: e.memset(self.ones_ln[:], 1.0 / 1024), writes=["ones_ln"])
        P.op(DVE, lambda e: e.memset(self.eps[:], EPS), writes=["eps"])
        P.op(DVE, lambda e: e.tensor_copy(out=self.convw_bf[:], in_=self.vecs[:, CONVW_COL:CONVW_COL + 2 * 31 * 8]),
             reads=["vecs"], writes=["convw_bf"])
        xin = self.x_in.rearrange("(c p) t -> p c t", p=128)
        for q in range(4):
            P.op(SP, (lambda q: lambda e: e.dma_start(out=self.xT[:, 4 * q:4 * q + 4, :], in_=xin[:, 4 * q:4 * q + 4, :]))(q),
                 writes=[("x", c, tt) for c in range(4 * q, 4 * q + 4) for tt in range(NT)], kind="d")

        for i in L:
            if i % 2 == 0:
                self.even_mixer(i)
            else:
                self.odd_mixer(i)
            if DEBUG_STOP in ("rms", "gather", "chdft", "seqdft", "mixer", "win", "halo", "pool", "poolw", "conv"):
                continue
            self.mlp(i)
            if DEBUG_STOP == "mlp":
                continue
            self.ple(i)
        if self.do_final:
            self.rmsnorm(NORM_COL["fin"], mode="final")
        outv = self.out.rearrange("(c p) t -> p c t", p=128)
        for q in range(4):
            P.op(SP, (lambda q: lambda e: e.dma_start(out=outv[:, 4 * q:4 * q + 4, :], in_=self.xT[:, 4 * q:4 * q + 4, :]))(q),
                 reads=[("x", c, tt) for c in range(4 * q, 4 * q + 4) for tt in range(NT)], kind="d")
        P.emit(nc)
        st.close()
        return nc

    def stat_update(self, n, tt, first, last):
        P = self.P
        sl = slice(tt * 512, (tt + 1) * 512)
        qi = self.ring("sq", 4)
        sq = self.sq[qi]
        P.op(ACT, lambda e: e.activation(out=sq[:], in_=self.xT[:, n, sl], func=AF.Square),
             reads=[("x", n, tt)], writes=[("sq", qi)])
        self.pending_stats.append((sq, qi, tt, first, last))

    def flush_stats(self, keep=0):
        P = self.P
        while len(self.pending_stats) > keep:
            sq, qi, tt, first, last = self.pending_stats.pop(0)
            P.op(PE, lambda e, sq=sq, tt=tt, first=first, last=last: e.matmul(
                self.ps[6 + tt][:], lhsT=self.ones_bf[:], rhs=sq[:], start=first, stop=last),
                reads=[("sq", qi), "ones_bf"], writes=[("ps", 6 + tt)])

    def scale_chunk(self, gcol, c, tt):
        P = self.P
        sl = slice(tt * 512, (tt + 1) * 512)
        g = self.vecs[:, gcol + c:gcol + c + 1]
        if c % 2 == 0:
            P.op(DVE, lambda e: e.tensor_scalar(out=self.R1[:, c, sl], in0=self.xT[:, c, sl], scalar1=g, scalar2=None, op0=ALU.mult),
                 reads=[("x", c, tt), "vecs"], writes=[("r1", c, tt)])
        else:
            P.op(ACT, lambda e: e.activation(out=self.R1[:, c, sl], in_=self.xT[:, c, sl], func=AF.Copy, scale=g),
                 reads=[("x", c, tt), "vecs"], writes=[("r1", c, tt)])

    def rmsnorm(self, gcol, mode="deferred", power=-0.5, prescaled=False):
        P = self.P
        xT, R1, vecs = self.xT, self.R1, self.vecs
        if not self.stats_ready:
            for tt in range(NT):
                for c in range(NCH):
                    self.stat_update(c, tt, c == 0, c == NCH - 1)
                    self.flush_stats(keep=1)
        self.flush_stats()
        self.stats_ready = False

        def rstd_tile(tt):
            sl = slice(tt * 512, (tt + 1) * 512)
            P.op(ACT, lambda e: e.activation(out=self.lnt[:], in_=self.ps[6 + tt][:], func=AF.Ln, bias=self.eps[:], scale=1.0),
                 reads=[("ps", 6 + tt), "eps"], writes=["lnt"])
            P.op(ACT, lambda e: e.activation(out=self.rstd[:, sl], in_=self.lnt[:], func=AF.Exp, scale=power),
                 reads=["lnt"], writes=[("rstd", tt)])
        if mode == "deferred" and prescaled:
            for tt in range(NT):
                rstd_tile(tt)
        elif mode == "deferred":
            for tt in range(NT):
                sl = slice(tt * 512, (tt + 1) * 512)
                for c in range(NCH):
                    g = vecs[:, gcol + c:gcol + c + 1]
                    if c % 2 == 0:
                        P.op(DVE, lambda e, c=c, sl=sl, g=g: e.tensor_scalar(out=R1[:, c, sl], in0=xT[:, c, sl], scalar1=g, scalar2=None, op0=ALU.mult),
                             reads=[("x", c, tt), "vecs"], writes=[("r1", c, tt)])
                    else:
                        P.op(ACT, lambda e, c=c, sl=sl, g=g: e.activation(out=R1[:, c, sl], in_=xT[:, c, sl], func=AF.Copy, scale=g),
                             reads=[("x", c, tt), "vecs"], writes=[("r1", c, tt)])
                rstd_tile(tt)
        else:
            for tt in range(NT):
                rstd_tile(tt)
            for c in range(NCH):
                for tt in range(NT):
                    sl = slice(tt * 512, (tt + 1) * 512)
                    g = vecs[:, gcol + c:gcol + c + 1]
                    if mode == "final":
                        P.op(DVE, lambda e, c=c, sl=sl, g=g: e.scalar_tensor_tensor(
                            out=xT[:, c, sl], in0=xT[:, c, sl], scalar=g, in1=self.rstd[:, sl], op0=ALU.mult, op1=ALU.mult),
                            reads=[("x", c, tt), ("rstd", tt), "vecs"], writes=[("x", c, tt)])
                    else:
                        P.op(DVE, lambda e, c=c, sl=sl, g=g: e.scalar_tensor_tensor(
                            out=R1[:, c, sl], in0=xT[:, c, sl], scalar=g, in1=self.rstd[:, sl], op0=ALU.mult, op1=ALU.mult),
                            reads=[("x", c, tt), ("rstd", tt), "vecs"], writes=[("r1", c, tt)])

    def dense(self, W, KC, col_blocks, rhs, epilogue, extra=None):
        P = self.P
        for (n0, wd) in col_blocks:
            nsl = (KC + 7) // 8
            slots = []
            for kh in range(nsl):
                slot = self.next_slot()
                slots.append(slot)
                wt = self.wring[slot]
                kn = min(8, KC - kh * 8)
                src = W[kh * 1024:kh * 1024 + kn * 128, n0:n0 + wd].rearrange("(kc p) n -> p kc n", p=128)
                P.op(POOL, (lambda wt, kn, src, wd: lambda e: e.dma_start(out=wt[:, 0:kn, 0:wd], in_=src))(wt, kn, src, wd),
                     writes=[("w", slot)], kind="d")
            for nci in range(wd // 128):
                n = n0 // 128 + nci
                for tt in range(NT):
                    bank = self.next_bank()
                    for kc in range(KC):
                        rap, rkeys = rhs(kc, tt)
                        slot = slots[kc // 8]
                        wt = self.wring[slot]
                        P.op(PE, (lambda wt, kc, nci, rap, bank: lambda e: e.matmul(
                            self.ps[bank][:], lhsT=wt[:, kc % 8, nci * 128:(nci + 1) * 128], rhs=rap,
                            start=(kc == 0), stop=(kc == KC - 1)))(wt, kc, nci, rap, bank),
                            reads=[("w", slot)] + rkeys, writes=[("ps", bank)])
                    epilogue(n, tt, bank)
                    self.flush_stats(keep=2)
        self.flush_stats()

    def rhs_r1(self, kc, tt):
        return self.R1[:, kc, tt * 512:(tt + 1) * 512], [("r1", kc, tt)]

    def rhs_a(self, kc, tt):
        return self.R2v[:, kc, tt * 512:(tt + 1) * 512], [("r2", kc, tt)]

    def make_ep_resid(self, stats=False, scaled=False, next_gcol=None):
        def ep(n, tt, bank):
            P = self.P
            sl = slice(tt * 512, (tt + 1) * 512)
            if scaled:
                fi = self.ring("f32tmp", 3)
                tmp = self.f32tmp[fi]
                P.op(DVE, lambda e: e.tensor_tensor(out=tmp[:], in0=self.ps[bank][:], in1=self.rstd[:, sl], op=ALU.mult),
                     reads=[("ps", bank), ("rstd", tt)], writes=[("f32tmp", fi)])
                P.op(DVE, lambda e: e.tensor_tensor(out=self.xT[:, n, sl], in0=tmp[:], in1=self.xT[:, n, sl], op=ALU.add),
                     reads=[("f32tmp", fi), ("x", n, tt)], writes=[("x", n, tt)])
            else:
                P.op(DVE, lambda e: e.tensor_tensor(out=self.xT[:, n, sl], in0=self.ps[bank][:], in1=self.xT[:, n, sl], op=ALU.add),
                     reads=[("ps", bank), ("x", n, tt)], writes=[("x", n, tt)])
            if stats:
                self.stat_update(n, tt, n == 0, n == NCH - 1)
            if next_gcol is not None:
                self.scale_chunk(next_gcol, n, tt)
        return ep

    def mlp(self, i):
        P = self.P
        P.op(POOL, lambda e: e.dma_start(out=self.pT, in_=self.p_in[i].rearrange("(k p) t -> p k t", p=128)),
             writes=[("r3", q) for q in range(8, 12)], kind="d")
        P.op(POOL, lambda e: e.dma_start(out=self.wpT, in_=self.w[f"ple_w_proj{i}"].rearrange("(k p) n -> p k n", p=128)),
             writes=[("r3", q) for q in range(0, 8)], kind="d")
        self.rmsnorm(NORM_COL[f"mlp{i}"], mode="deferred", power=-1.0)
        W1 = self.w[f"mlp_w1{i}"]
        W2 = self.w[f"mlp_w2{i}"]
        for fb in range(4):
            def ep1(n, tt, bank, fb=fb):
                nl = n - fb * 16
                ri = self.ring("sq", 4)
                rt = self.sq[ri]
                P.op(ACT, lambda e: e.activation(out=rt[:], in_=self.ps[bank][:], func=AF.Relu),
                     reads=[("ps", bank)], writes=[("sq", ri)])
                P.op(DVE, lambda e: e.scalar_tensor_tensor(
                    out=self.R2v[:, nl, tt * 512:(tt + 1) * 512], in0=self.ps[bank][:], scalar=0.0, in1=rt[:],
                    op0=ALU.max, op1=ALU.mult),
                    reads=[("ps", bank), ("sq", ri)], writes=[("r2", nl, tt)])
            self.dense(W1, 16, [(fb * 2048 + b * 512, 512) for b in range(4)], self.rhs_r1, ep1)
            self.dense(W2[fb * 2048:(fb + 1) * 2048, :], 16, [(b * 512, 512) for b in range(4)], self.rhs_a,
                       self.make_ep_resid(stats=(fb == 3), scaled=True,
                                          next_gcol=(NORM_COL[f"ple{i}"] if fb == 3 else None)))
        self.stats_ready = True

    def ple(self, i):
        P = self.P
        self.rmsnorm(NORM_COL[f"ple{i}"], mode="deferred", prescaled=True)

        def ep(n, tt, bank):
            sl = slice(tt * 512, (tt + 1) * 512)
            bank2 = self.next_bank()
            for kc in range(2):
                P.op(PE, (lambda kc: lambda e: e.matmul(self.ps[bank2][:], lhsT=self.wpT[:, kc, n * 128:(n + 1) * 128],
                                                        rhs=self.pT[:, kc, sl], start=(kc == 0), stop=(kc == 1)))(kc),
                     reads=[("r3", q) for q in range(12)], writes=[("ps", bank2)])
            fi = self.ring("f32tmp", 3)
            sg = self.f32tmp[fi]
            P.op(DVE, lambda e: e.tensor_tensor(out=sg[:], in0=self.ps[bank][:], in1=self.rstd[:, sl], op=ALU.mult),
                 reads=[("ps", bank), ("rstd", tt)], writes=[("f32tmp", fi)])
            P.op(ACT, lambda e: e.activation(out=sg[:], in_=sg[:], func=AF.Sigmoid),
                 reads=[("f32tmp", fi)], writes=[("f32tmp", fi)])
            P.op(DVE, lambda e: e.tensor_tensor(out=sg[:], in0=sg[:], in1=self.ps[bank2][:], op=ALU.mult),
                 reads=[("ps", bank2), ("f32tmp", fi)], writes=[("f32tmp", fi)])
            P.op(DVE, lambda e: e.tensor_tensor(out=self.xT[:, n, sl], in0=sg[:], in1=self.xT[:, n, sl], op=ALU.add),
                 reads=[("f32tmp", fi), ("x", n, tt)], writes=[("x", n, tt)])
            self.stat_update(n, tt, n == 0, n == NCH - 1)
        self.dense(self.w[f"ple_w_gate{i}"], 16, [(b * 512, 512) for b in range(4)], self.rhs_r1, ep)
        self.stats_ready = True

    def even_mixer(self, i):
        P = self.P
        j = i // 2
        vecs = self.vecs
        R2v, R3b, R3f, R1 = self.R2v, self.R3b, self.R3f, self.R1
        P.op(POOL, lambda e: e.dma_start(out=self.poolw[:], in_=self.w[f"ev_pool_w{j}"].rearrange("(k p) n -> p k n", p=128)),
             writes=["poolw"], kind="d")
        self.rmsnorm(NORM_COL[f"ev{j}"], mode="deferred")

        def ep_in(n, tt, bank):
            sl = slice(tt * 512, (tt + 1) * 512)
            rs = self.rstd[:, sl]
            if n < 16:
                fi = self.ring("f32tmp", 3)
                tmp = self.f32tmp[fi]
                P.op(DVE, lambda e: e.tensor_tensor(out=tmp[:], in0=self.ps[bank][:], in1=rs, op=ALU.mult),
                     reads=[("ps", bank), ("rstd", tt)], writes=[("f32tmp", fi)])
            if 8 <= n < 16:
                c = n - 8
                P.op(ACT, lambda e: e.activation(out=R3b[:, c, sl], in_=tmp[:], func=AF.Sigmoid),
                     reads=[("f32tmp", fi)], writes=[("r3", 2 * c + tt)])
            elif n < 8:
                c = n
                a0 = 16 + tt * 512
                P.op(DVE, lambda e: e.tensor_tensor(out=R2v[:, c, a0:a0 + 512], in0=tmp[:], in1=R3b[:, c, sl], op=ALU.mult),
                     reads=[("f32tmp", fi), ("r3", 2 * c + tt)], writes=self.r2keys(c, a0, a0 + 512))
            else:
                c = n - 16
                a0 = 8 + tt * 512
                P.op(DVE, lambda e: e.tensor_tensor(out=R2v[:, 8 + c, a0:a0 + 512], in0=self.ps[bank][:], in1=rs, op=ALU.mult),
                     reads=[("ps", bank), ("rstd", tt)], writes=self.r2keys(8 + c, a0, a0 + 512))
        blocks = [(1024, 512), (1536, 512), (2048, 512), (2560, 512), (0, 512), (512, 512)]
        self.dense(self.w[f"ev_w_in{j}"], 16, blocks, self.rhs_r1, ep_in)

        if DEBUG_STOP == "win":
            return
        mL = self.masks[:, 0:1]
        mR = self.masks[:, 1:2]
        h4 = []
        for k, (r0, c_first, c_last) in enumerate(((8, 8, 1016), (0, 16, 1024))):
            hin = self.halo_in[k].ap().rearrange("p (q s t) -> p q s t", q=8, s=2)
            rows = [("r2", r, h) for r in range(r0, r0 + 8) for h in range(2)]
            P.op(SP, lambda e, hin=hin, r0=r0, c_first=c_first: e.dma_start(out=hin[:, :, 0, :], in_=R2v[:, r0:r0 + 8, c_first:c_first + 16]),
                 reads=rows, writes=[("halo_in", k, 0)], kind="d")
            P.op(SP, lambda e, hin=hin, r0=r0, c_last=c_last: e.dma_start(out=hin[:, :, 1, :], in_=R2v[:, r0:r0 + 8, c_last:c_last + 16]),
                 reads=rows, writes=[("halo_in", k, 1)], kind="d")
            P.op(POOL, lambda e, k=k: e.collective_compute("AllGather", ALU.bypass, replica_groups=[[0, 1], [2, 3], [4, 5], [6, 7]],
                                                           ins=[self.halo_in[k].ap().opt()], outs=[self.halo_out[k].ap().opt()]),
                 reads=[("halo_in", k, 0), ("halo_in", k, 1)], writes=[("halo_out", k)], kind="cc")
            P.op(SP, lambda e, k=k: e.dma_start(out=self.haloT[k][:], in_=self.halo_out[k].ap().rearrange("(r p) n -> p r n", p=128)),
                 reads=[("halo_out", k)], writes=[("haloT", k)], kind="d")
            h4.append(self.haloT[k][:].rearrange("p r (q s t) -> p r q s t", q=8, s=2))
        hu, hv = h4
        P.op(DVE, lambda e: e.tensor_scalar(out=R2v[:, 8:16, 0:8], in0=hu[:, 0, :, 1, 8:16], scalar1=mL, scalar2=None, op0=ALU.mult),
             reads=[("haloT", 0), "masks"], writes=[("r2", r, 0) for r in range(8, 16)])
        P.op(DVE, lambda e: e.tensor_scalar(out=R2v[:, 8:16, 1032:1040], in0=hu[:, 1, :, 0, 0:8], scalar1=mR, scalar2=None, op0=ALU.mult),
             reads=[("haloT", 0), "masks"], writes=[("r2", r, 1) for r in range(8, 16)])
        def v_halo_masks():
            P.op(DVE, lambda e: e.tensor_scalar(out=R2v[:, 0:8, 0:16], in0=hv[:, 0, :, 1, :], scalar1=mL, scalar2=None, op0=ALU.mult),
                 reads=[("haloT", 1), "masks"], writes=[("r2", r, 0) for r in range(8)])
            P.op(DVE, lambda e: e.tensor_scalar(out=R2v[:, 0:8, 1040:1056], in0=hv[:, 1, :, 0, :], scalar1=mR, scalar2=None, op0=ALU.mult),
                 reads=[("haloT", 1), "masks"], writes=[("r2", r, 1) for r in range(8)])
        self._conv_ln_tile(j, 0, after_conv=v_halo_masks)
        self._conv_ln_tile(j, 1)

        if DEBUG_STOP == "halo":
            return
        self._pool_done = [False] * 8
        if DEBUG_STOP == "pool":
            return
        if DEBUG_STOP == "poolw":
            return
        if DEBUG_STOP == "conv":
            return
        for g in range(4):
            for dc in range(2):
                for tt in range(NT):
                    sl = slice(tt * 512, (tt + 1) * 512)
                    bank = self.next_bank()
                    for kc in range(2):
                        P.op(PE, lambda e, g=g, dc=dc, kc=kc, bank=bank, sl=sl: e.matmul(
                            self.ps[bank][:], lhsT=self.poolw[:, 2 * g + kc, dc * 128:(dc + 1) * 128],
                            rhs=R2v[:, 8 + 2 * g + kc, sl], start=(kc == 0), stop=(kc == 1)),
                            reads=["poolw", ("r2", 8 + 2 * g + kc, tt)], writes=[("ps", bank)])
                    oc = 2 * g + dc
                    col = PSC_COL + j * 8 + oc
                    P.op(DVE, lambda e, oc=oc, col=col, bank=bank, sl=sl: e.tensor_scalar(
                        out=R1[:, 8 + oc, sl], in0=self.ps[bank][:], scalar1=vecs[:, col:col + 1], scalar2=None, op0=ALU.mult),
                        reads=[("ps", bank), "vecs"], writes=[("r1", 8 + oc, tt)])
        self.dense(self.w[f"ev_w_out{j}"], 16, [(b * 512, 512) for b in range(4)], self.rhs_r1, self.make_ep_resid(stats=True))
        self.stats_ready = True

    def _pool_chunk(self, c):
        P = self.P
        R2v = self.R2v
        tA, tB = self.tA, self.tB
        TAK = [("r1", r, tt) for r in range(0, 3) for tt in range(NT)]
        TBK = [("r1", r, tt) for r in range(3, 6) for tt in range(NT)]
        g = c // 2
        w = WINDOWS[g]
        row = 8 + c
        rk = [("r2", row, 0), ("r2", row, 1)]
        u = R2v[:, row, :]
        P.op(DVE, lambda e, u=u: e.tensor_tensor(out=tA[:, 1:1040], in0=u[:, 0:1039], in1=u[:, 1:1040], op=ALU.add),
             reads=rk, writes=TAK)
        cur, curk = tA, TAK
        if w >= 4:
            P.op(DVE, lambda e: e.tensor_tensor(out=tB[:, 2:1038], in0=tA[:, 1:1037], in1=tA[:, 3:1039], op=ALU.add),
                 reads=TAK, writes=TBK)
            cur, curk = tB, TBK
        if w >= 8:
            P.op(DVE, lambda e: e.tensor_tensor(out=tA[:, 4:1036], in0=tB[:, 2:1034], in1=tB[:, 6:1038], op=ALU.add),
                 reads=TBK, writes=TAK)
            cur, curk = tA, TAK
        if w >= 16:
            P.op(DVE, lambda e: e.tensor_tensor(out=tB[:, 8:1032], in0=tA[:, 4:1028], in1=tA[:, 12:1036], op=ALU.add),
                 reads=TAK, writes=TBK)
            cur, curk = tB, TBK
        for ei, (ecol, tcol) in enumerate(((0, 0), (8, 1016))):
            et = self.edgeres[ei]
            P.op(DVE, lambda e, cur=cur, g=g, ecol=ecol, tcol=tcol: e.tensor_tensor(
                out=self.edget[:], in0=cur[:, 8 + tcol:16 + tcol], in1=self.edge[:, g, ecol:ecol + 8], op=ALU.mult),
                reads=curk + ["edge"], writes=["edget"])
            P.op(DVE, lambda e, u=u, et=et, tcol=tcol: e.tensor_tensor(
                out=et[:], in0=self.edget[:], in1=u[:, 8 + tcol:16 + tcol], op=ALU.subtract),
                reads=["edget"] + rk, writes=[("edgeres", ei)])
        P.op(DVE, lambda e, cur=cur, u=u, w=w: e.scalar_tensor_tensor(
            out=u[:, 0:1024], in0=cur[:, 8:1032], scalar=1.0 / w, in1=u[:, 8:1032], op0=ALU.mult, op1=ALU.subtract),
            reads=curk + rk, writes=rk)
        for ei, tcol in enumerate((0, 1016)):
            et = self.edgeres[ei]
            P.op(DVE, lambda e, u=u, et=et, tcol=tcol: e.tensor_copy(out=u[:, tcol:tcol + 8], in_=et[:]),
                 reads=[("edgeres", ei)], writes=rk)


    def _build_dset(self, j, c):
        P = self.P
        vecs, R1 = self.vecs, self.R1
        db = self.ring("dset", 2)
        dk = [("r1", r, h) for r in range(8 + 4 * db, 12 + 4 * db) for h in range(NT)]
        dset = R1[:, 8 + 4 * db:12 + 4 * db, :].rearrange("p c t -> p (c t)")[:, 0:31 * 128].rearrange("p (a m) -> p a m", m=128)
        w0 = (j * 31) * 8 + c
        wv = self.convw_bf[:, w0:w0 + 30 * 8 + 1:8].unsqueeze(2).broadcast_to([128, 31, 128])
        idb = self.ident[:].unsqueeze(1).broadcast_to([128, 31, 128])
        P.op(DVE, lambda e: e.tensor_tensor(out=dset, in0=idb, in1=wv, op=ALU.mult),
             reads=["ident", "convw_bf"], writes=dk)
        return dset, dk

    def _conv_ln_tile(self, j, tt, after_conv=None):
        P = self.P
        vecs = self.vecs
        R2v, R3b, R3f, R1 = self.R2v, self.R3b, self.R3f, self.R1
        if True:
            def vwin(c, tap):
                if tt == 0:
                    return R2v[:, c, 257 + tap:769 + tap]
                return R2v[:, c, 1 + tap:1025 + tap].rearrange("p (s t) -> p s t", s=4)[:, 0:4:3, :]

            def catwin(c):
                if tt == 0:
                    return R1[:, c, 256:768]
                return R1[:, c, :].rearrange("p (s t) -> p s t", s=4)[:, 0:4:3, :]

            def cwin(c):
                if tt == 0:
                    return R3f[:, c, :]
                return R3f[:, c, :].rearrange("p (s t) -> p s t", s=2)
            bank_m = 6
            bank_s = 7
            pending = None

            def stats(c, cb, bi, sqb, qi):
                P.op(PE, lambda e: e.matmul(self.ps[bank_m][:], lhsT=self.ones_ln[:], rhs=cb[:], start=(c == 0), stop=(c == 7)),
                     reads=[("sq", bi), "ones_ln"], writes=[("ps", bank_m)])
                P.op(PE, lambda e: e.matmul(self.ps[bank_s][:], lhsT=self.ones_ln[:], rhs=sqb[:], start=(c == 0), stop=(c == 7)),
                     reads=[("sq", qi), "ones_ln"], writes=[("ps", bank_s)])
            for c in range(8):
                vk = [("r2", c, 0), ("r2", c, 1)]
                ck = [("r3", 2 * c), ("r3", 2 * c + 1)]
                bc = CONVB_COL + j * 8 + c
                bank_c = self.ring("convbank", 6)
                if self.prebuilt is None:
                    self.prebuilt = self._build_dset(j, c)
                dset, dk = self.prebuilt
                self.prebuilt = None
                if c < 7:
                    self.prebuilt = self._build_dset(j, c + 1)
                elif tt == 0:
                    self.prebuilt = self._build_dset(j, 0)
                if tt == 0 and DEBUG_STOP != "halo":
                    self._pool_chunk(c)
                for tap in range(31):
                    P.op(PE, lambda e, dset=dset, c=c, tap=tap, bank_c=bank_c: e.matmul(
                        self.ps[bank_c][:], lhsT=dset[:, tap, :], rhs=vwin(c, tap),
                        start=(tap == 0), stop=(tap == 30)),
                        reads=vk + dk, writes=[("ps", bank_c)])
                P.op(ACT, lambda e, c=c, bc=bc, bank_c=bank_c: e.activation(
                    out=R3f[:, c, :], in_=self.ps[bank_c][:], func=AF.Identity, bias=vecs[:, bc:bc + 1], scale=1.0),
                    reads=[("ps", bank_c), "vecs"], writes=ck)
                bi = self.ring("sq", 4)
                cb = self.sq[bi]
                P.op(ACT, lambda e, c=c, cb=cb: e.activation(out=cb[:], in_=R3f[:, c, :], func=AF.Copy),
                     reads=ck, writes=[("sq", bi)])
                qi = self.ring("sq", 4)
                sqb = self.sq[qi]
                P.op(ACT, lambda e, c=c, sqb=sqb: e.activation(out=sqb[:], in_=R3f[:, c, :], func=AF.Square),
                     reads=ck, writes=[("sq", qi)])
                if pending is not None:
                    stats(*pending)
                pending = (c, cb, bi, sqb, qi)
            stats(*pending)
            if after_conv is not None:
                after_conv()
            msq, nmean, var = self.f32tmp[0], self.f32tmp[1], self.f32tmp[2]
            P.op(ACT, lambda e: e.activation(out=msq[:], in_=self.ps[bank_m][:], func=AF.Square),
                 reads=[("ps", bank_m)], writes=[("f32tmp", 0)])
            P.op(ACT, lambda e: e.activation(out=nmean[:], in_=self.ps[bank_m][:], func=AF.Copy, scale=-1.0),
                 reads=[("ps", bank_m)], writes=[("f32tmp", 1)])
            P.op(DVE, lambda e: e.tensor_tensor(out=var[:], in0=self.ps[bank_s][:], in1=msq[:], op=ALU.subtract),
                 reads=[("ps", bank_s), ("f32tmp", 0)], writes=[("f32tmp", 2)])
            P.op(ACT, lambda e: e.activation(out=self.lnt[:], in_=var[:], func=AF.Ln, bias=self.eps[:], scale=1.0),
                 reads=[("f32tmp", 2), "eps"], writes=["lnt"])
            P.op(ACT, lambda e: e.activation(out=var[:], in_=self.lnt[:], func=AF.Exp, scale=-0.5),
                 reads=["lnt"], writes=[("f32tmp", 2)])
            for c in range(8):
                ck = [("r3", 2 * c), ("r3", 2 * c + 1)]
                gcol = LNG_COL + j * 8 + c
                bcol = LNB_COL + j * 8 + c
                P.op(DVE, lambda e, c=c: e.tensor_tensor(out=R3f[:, c, :], in0=R3f[:, c, :], in1=nmean[:], op=ALU.add),
                     reads=ck + [("f32tmp", 1)], writes=ck)
                P.op(DVE, lambda e, c=c, gcol=gcol: e.scalar_tensor_tensor(
                    out=R3f[:, c, :], in0=R3f[:, c, :], scalar=vecs[:, gcol:gcol + 1], in1=var[:], op0=ALU.mult, op1=ALU.mult),
                    reads=ck + [("f32tmp", 2), "vecs"], writes=ck)
                P.op(ACT, lambda e, c=c, bcol=bcol: e.activation(out=catwin(c), in_=cwin(c), func=AF.Silu,
                                                                 bias=vecs[:, bcol:bcol + 1], scale=1.0),
                     reads=ck + ["vecs"], writes=[("r1", c, 0), ("r1", c, 1)])

    def odd_mixer(self, i):
        P = self.P
        j = i // 2
        R1, R2v, R3h = self.R1, self.R2v, self.R3h
        self.rmsnorm(NORM_COL[f"od{j}"], mode="direct")
        if DEBUG_STOP == "rms":
            return
        for q in range(8):
            hs = self.hsend[q].ap().rearrange("p (c t) -> p c t", c=2)
            P.op(SP, lambda e, q=q, hs=hs: e.dma_start(out=hs, in_=R1[:, 2 * q:2 * q + 2, :]),
                 reads=[("r1", c, tt) for c in range(2 * q, 2 * q + 2) for tt in range(NT)], writes=[("hsend", q)], kind="d")
            P.op(POOL, lambda e, q=q: e.collective_compute("AllGather", ALU.bypass, replica_groups=[[0, 1], [2, 3], [4, 5], [6, 7]],
                                                           ins=[self.hsend[q].ap().opt()], outs=[self.hfull[q].ap().opt()]),
                 reads=[("hsend", q)], writes=[("hfull", q)], kind="cc")
        hfs = [self.hfull[q].ap().rearrange("(r p) (c t) -> p c r t", r=2, c=2) for q in range(8)]
        if DEBUG_STOP == "gather":
            return
        tslots = []
        for e_ in range(2):
            for tab in (self.cs_in, self.nss_in):
                sl_ = self.next_slot()
                tslots.append(sl_)
                P.op(SP, lambda e, sl_=sl_, tab=tab, e_=e_: e.dma_start(
                    out=self.wring[sl_][:], in_=tab[e_].rearrange("p (s k) -> p s k", k=512)),
                    writes=[("w", sl_)], kind="d")
        for g in range(8):
            b = g % 2
            hk = [("r3", 8 * b + q) for q in range(8)]
            for r in range(2):
                P.op(SP, lambda e, g=g, b=b, r=r: e.dma_start(out=R3h[:, b, :, r, :], in_=hfs[g][:, :, r, :]),
                     reads=[("hfull", g)], writes=[("r3", 8 * b + 4 * k + 2 * r + h) for k in range(2) for h in range(2)], kind="d")
            for st8 in range(8):
                off = st8 * 128
                bankL = self.next_bank()
                bankH = self.next_bank()
                for r, bank in ((0, bankL), (1, bankH)):
                    for kc in range(2):
                        P.op(PE, lambda e, b=b, kc=kc, r=r, off=off, bank=bank: e.matmul(
                            self.ps[bank][:], lhsT=R3h[:, b, kc, r, off:off + 128], rhs=self.ccsc[:, kc, :],
                            start=(kc == 0), stop=(kc == 1)), reads=hk + ["ccsc"], writes=[("ps", bank)])
                fi = self.ring("f32tmp", 3)
                tmp = self.f32tmp[fi]
                P.op(ACT, lambda e, tmp=tmp, bankL=bankL: e.activation(out=tmp[:], in_=self.ps[bankL][:], func=AF.Copy),
                     reads=[("ps", bankL)], writes=[("f32tmp", fi)])
                P.op(DVE, lambda e, b=b, st8=st8, tmp=tmp, bankH=bankH: e.tensor_tensor(
                    out=R2v[:, st8, b * 512:(b + 1) * 512], in0=tmp[:], in1=self.ps[bankH][:], op=ALU.add),
                    reads=[("f32tmp", fi), ("ps", bankH)], writes=[("r2", st8, b)])
                P.op(DVE, lambda e, b=b, st8=st8, tmp=tmp, bankH=bankH: e.tensor_tensor(
                    out=R2v[:, 8 + st8, b * 512:(b + 1) * 512], in0=tmp[:], in1=self.ps[bankH][:], op=ALU.subtract),
                    reads=[("f32tmp", fi), ("ps", bankH)], writes=[("r2", 8 + st8, b)])
            if DEBUG_STOP == "chdft":
                continue
            for e_ in range(2):
                slc, sls = tslots[2 * e_], tslots[2 * e_ + 1]
                for mc in range(2):
                    bank = self.next_bank()
                    for s8 in range(8):
                        stl = e_ * 8 + s8
                        P.op(PE, lambda e, b=b, stl=stl, s8=s8, mc=mc, slc=slc, bank=bank: e.matmul(
                            self.ps[bank][:], lhsT=R2v[:, stl, b * 512 + mc * 128:b * 512 + (mc + 1) * 128],
                            rhs=self.wring[slc][:, s8, :], start=(s8 == 0), stop=False),
                            reads=[("r2", stl, b), ("w", slc)], writes=[("ps", bank)])
                        P.op(PE, lambda e, b=b, stl=stl, s8=s8, mc=mc, sls=sls, bank=bank: e.matmul(
                            self.ps[bank][:], lhsT=R2v[:, stl, b * 512 + 256 + mc * 128:b * 512 + 256 + (mc + 1) * 128],
                            rhs=self.wring[sls][:, s8, :], start=False, stop=(s8 == 7)),
                            reads=[("r2", stl, b), ("w", sls)], writes=[("ps", bank)])
                    row = 2 * g + mc
                    if mc == 0:
                        P.op(ACT, lambda e, row=row, e_=e_, bank=bank: e.activation(out=R1[:, row, e_::2], in_=self.ps[bank][:], func=AF.Copy),
                             reads=[("ps", bank)], writes=[("r1", row, 0), ("r1", row, 1)])
                    else:
                        P.op(DVE, lambda e, row=row, e_=e_, bank=bank: e.tensor_copy(out=R1[:, row, e_::2], in_=self.ps[bank][:]),
                             reads=[("ps", bank)], writes=[("r1", row, 0), ("r1", row, 1)])
        if DEBUG_STOP in ("chdft", "seqdft"):
            return
        self.dense(self.w[f"od_w_out{j}"], 16, [(b * 512, 512) for b in range(4)], self.rhs_r1, self.make_ep_resid(stats=True))
        self.stats_ready = True


_CACHE = {}


def _consts():
    if "c" in _CACHE:
        return _CACHE["c"]
    c = np.arange(256)
    ang = 2.0 * np.pi * ((c[:, None] * c[None, :]) % 256) / 256.0
    ccsc = np.concatenate([np.cos(ang), np.sin(ang)], axis=1) / 16.0
    s = np.arange(S)
    per_half = []
    for half in range(2):
        s_ = np.arange(T)

        def tabs(fn):
            out = np.zeros((2, 128, 4096), np.float64)
            for e_ in range(2):
                k = half * T + 2 * np.arange(512) + e_
                a = 2.0 * np.pi * ((s_[:, None] * k[None, :]) % S) / S
                m = fn(a) / np.sqrt(S)
                out[e_] = m.reshape(8, 128, 512).transpose(1, 0, 2).reshape(128, 4096)
            return np.ascontiguousarray(out.astype(ml_dtypes.bfloat16))
        cs_t = tabs(np.cos)
        nss_t = tabs(lambda a: -np.sin(a))
        edge = np.zeros((4, 16), np.float64)
        for gi, win in enumerate(WINDOWS):
            for q in range(16):
                tl = q if q < 8 else T - 16 + q
                tg = half * T + tl
                lo = min(max(tg - win // 2, 0), S - 1)
                hi = min(max(tg + win // 2 - 1, 0), S - 1) + 1
                edge[gi, q] = 1.0 / (hi - lo)
        masks = np.zeros((128, 2), np.float32)
        masks[:, 0] = 1.0 if half == 1 else 0.0
        masks[:, 1] = 1.0 if half == 0 else 0.0
        per_half.append(dict(cs=cs_t, nss=nss_t,
                             edge=np.ascontiguousarray(np.broadcast_to(edge.reshape(1, 64), (128, 64)).astype(np.float32)),
                             masks=masks))
    _CACHE["c"] = (ccsc.astype(np.float32), per_half)
    return _CACHE["c"]


def _pm(v):
    v = np.asarray(v, np.float32)
    return v.reshape(-1, 128).T


def _vecs(inp):
    vecs = np.zeros((128, NV), np.float32)
    names = [("ev0", inp["ev_norm"][0]), ("ev1", inp["ev_norm"][1]), ("od0", inp["od_norm"][0]), ("od1", inp["od_norm"][1])]
    names += [(f"mlp{i}", inp["mlp_norm"][i]) for i in range(4)]
    names += [(f"ple{i}", inp["ple_norm"][i]) for i in range(4)]
    names += [("fin", inp["final_norm"])]
    for n, v in names:
        vecs[:, NORM_COL[n]:NORM_COL[n] + 16] = _pm(v)
    for j in range(2):
        for tap in range(31):
            c0 = CONVW_COL + (j * 31 + tap) * 8
            vecs[:, c0:c0 + 8] = _pm(inp["ev_conv_w"][j, tap])
        vecs[:, CONVB_COL + j * 8:CONVB_COL + j * 8 + 8] = _pm(inp["ev_conv_b"][j])
        vecs[:, LNG_COL + j * 8:LNG_COL + j * 8 + 8] = _pm(inp["ev_ln_g"][j])
        vecs[:, LNB_COL + j * 8:LNB_COL + j * 8 + 8] = _pm(inp["ev_ln_b"][j])
        vecs[:, PSC_COL + j * 8:PSC_COL + j * 8 + 8] = _pm(inp["ev_pool_scale"][j])
    return vecs


def _get_nc(layers, do_final):
    key = (tuple(layers), do_final)
    if key not in _CACHE:
        _CACHE[key] = Builder(layers, do_final).build()
    return _CACHE[key]


def _run(xT_cores, inp, layers, do_final):
    ccsc, per_half = _consts()
    vecs = _vecs(inp)
    nc = _get_nc(layers, do_final)
    p = np.asarray(inp["p"], np.float32)
    in_maps = []
    for core in range(8):
        b, half = core // 2, core % 2
        ph = per_half[half]
        m = {
            "xT": xT_cores[core],
            "pT": np.ascontiguousarray(p[:, b, half * T:(half + 1) * T, :].transpose(0, 2, 1)),
            "vecs": vecs, "ident": np.eye(128, dtype=np.float32), "masks": ph["masks"], "edge": ph["edge"], "ccsc": ccsc,
            "cs": ph["cs"], "nss": ph["nss"],
        }
        for i in layers:
            j = i // 2
            if i % 2 == 0:
                m[f"ev_w_in{j}"] = np.asarray(inp["ev_w_in"][j], np.float32)
                m[f"ev_pool_w{j}"] = np.asarray(inp["ev_pool_w"][j], np.float32).reshape(1024, 256)
                m[f"ev_w_out{j}"] = np.asarray(inp["ev_w_out"][j], np.float32)
            else:
                m[f"od_w_out{j}"] = np.asarray(inp["od_w_out"][j], np.float32)
            m[f"mlp_w1{i}"] = np.asarray(inp["mlp_w1"][i], np.float32)
            m[f"mlp_w2{i}"] = np.asarray(inp["mlp_w2"][i], np.float32)
            m[f"ple_w_gate{i}"] = np.asarray(inp["ple_w_gate"][i], np.float32)
            m[f"ple_w_proj{i}"] = np.asarray(inp["ple_w_proj"][i], np.float32)
        in_maps.append(m)
    res = run_bass_kernel_spmd(nc, in_maps, core_ids=list(range(8)))
    return [np.asarray(res.results[c]["outT"]) for c in range(8)]


LAUNCH_GROUPS = [[0, 1, 2, 3]]
DEBUG_STOP = None


def kernel(**inputs):
    inp = {k: np.asarray(v) for k, v in inputs.items()}
    x = np.asarray(inp["x"], np.float32)
    xT = [np.ascontiguousarray(x[c // 2, (c % 2) * T:(c % 2 + 1) * T, :].T) for c in range(8)]
    for gi, grp in enumerate(LAUNCH_GROUPS):
        xT = _run(xT, inp, grp, do_final=(gi == len(LAUNCH_GROUPS) - 1))
    out = np.empty((4, S, D), np.float32)
    for c in range(8):
        out[c // 2, (c % 2) * T:(c % 2 + 1) * T, :] = xT[c].T
    return out
```

```python
import contextlib
import numpy as np
import ml_dtypes
import concourse.bass as bass
import concourse.mybir as mybir
from concourse.bass_utils import run_bass_kernel_spmd

F32 = mybir.dt.float32
BF16 = mybir.dt.bfloat16
AF = mybir.ActivationFunctionType
ALU = mybir.AluOpType

D = 2048
S = 2048
T = 1024
NT = 2
DEPTH = 4
DFF = 8192
PLE = 256
WINDOWS = (2, 4, 8, 16)
EPS = 1e-6
NCH = 16

PE, ACT, DVE, POOL, SP = "pe", "act", "dve", "pool", "sp"
SEM_ROT = 3000
N_DMA_SEMS = 12

NORM_COL = {}
_c = 0
for _n in ["ev0", "ev1", "od0", "od1", "mlp0", "mlp1", "mlp2", "mlp3", "ple0", "ple1", "ple2", "ple3", "fin"]:
    NORM_COL[_n] = _c
    _c += 16
CONVW_COL = _c; _c += 2 * 31 * 8
CONVB_COL = _c; _c += 16
LNG_COL = _c; _c += 16
LNB_COL = _c; _c += 16
PSC_COL = _c; _c += 16
NV = _c


class Op:
    __slots__ = ("eng", "fn", "deps", "kind", "signalled", "tok", "idx")

    def __init__(self, eng, fn, kind):
        self.eng = eng
        self.fn = fn
        self.kind = kind
        self.deps = set()
        self.signalled = False
        self.tok = None


class Prog:
    def __init__(self):
        self.ops = []
        self.streams = {e: [] for e in (PE, ACT, DVE, POOL, SP)}
        self.lastw = {}
        self.readers = {}

    def op(self, eng, fn, reads=(), writes=(), kind="c"):
        o = Op(eng, fn, kind)
        o.idx = len(self.ops)
        self.ops.append(o)
        self.streams[eng].append(o)
        deps = o.deps
        for k in reads:
            w = self.lastw.get(k)
            if w is not None:
                deps.add(w)
        for k in writes:
            w = self.lastw.get(k)
            if w is not None:
                deps.add(w)
            rs = self.readers.get(k)
            if rs:
                deps.update(rs.values())
        for k in reads:
            rs = self.readers.setdefault(k, {})
            rk = eng if kind == "c" else ("d", o.idx)
            rs[rk] = o.idx
        for k in writes:
            self.lastw[k] = o.idx
            self.readers[k] = {}
        deps.discard(o.idx)
        return o

    def emit(self, nc):
        ops = self.ops
        for o in ops:
            for d in o.deps:
                p = ops[d]
                if p.kind == "c" and p.eng == o.eng and p.eng == PE:
                    continue
                p.signalled = True
        engs = {PE: nc.tensor, ACT: nc.scalar, DVE: nc.vector, POOL: nc.gpsimd, SP: nc.sync}
        stack = contextlib.ExitStack()
        sem_lists = {}
        for e in engs:
            n_sig = sum(1 for o in self.streams[e] if o.kind == "c" and o.signalled)
            n = max(1, (n_sig + SEM_ROT - 1) // SEM_ROT)
            sem_lists[e] = [stack.enter_context(nc.semaphore(f"s_{e}_{i}")) for i in range(n)]
        dma_sems = {e: [stack.enter_context(nc.semaphore(f"d_{e}_{i}")) for i in range(N_DMA_SEMS)]
                    for e in (POOL, SP)}
        cc_sem = stack.enter_context(nc.semaphore("cc_sem"))
        cc_count = 0
        for e in engs:
            cnt = 0
            dma_i = 0
            dma_val = {}
            for o in self.streams[e]:
                if o.kind == "c":
                    if o.signalled:
                        o.tok = (sem_lists[e][cnt // SEM_ROT], cnt % SEM_ROT + 1)
                        cnt += 1
                elif o.kind == "d":
                    si = dma_i % N_DMA_SEMS
                    dma_i += 1
                    prev = dma_val.get(si, 0)
                    o.tok = (dma_sems[e][si], prev + 16, prev)
                    dma_val[si] = prev + 16
                elif o.kind == "cc":
                    cc_count += 1
                    o.tok = (cc_sem, cc_count)
        block = stack.enter_context(nc.Block())
        streams = self.streams

        def run_stream(e, eng):
            waited = {}
            for o in streams[e]:
                need = {}
                for d in o.deps:
                    p = ops[d]
                    if p.kind == "c" and p.eng == e and e == PE:
                        continue
                    s, v = p.tok[0], p.tok[1]
                    key = id(s)
                    if waited.get(key, 0) >= v:
                        continue
                    if key not in need or need[key][1] < v:
                        need[key] = (s, v)
                if o.kind == "d" and o.tok[2] > 0:
                    s, v = o.tok[0], o.tok[2]
                    key = id(s)
                    if waited.get(key, 0) < v and (key not in need or need[key][1] < v):
                        need[key] = (s, v)
                for key, (s, v) in need.items():
                    eng.wait_ge(s, v)
                    waited[key] = v
                ins = o.fn(eng)
                if o.kind == "c":
                    if o.signalled:
                        ins.then_inc(o.tok[0], 1)
                elif o.kind == "d":
                    ins.then_inc(o.tok[0], 16)
                else:
                    ins.then_inc(o.tok[0])
            dv = {}
            for o in streams[e]:
                if o.kind == "d":
                    dv[id(o.tok[0])] = (o.tok[0], o.tok[1])
            for s, v in dv.values():
                if waited.get(id(s), 0) < v:
                    eng.wait_ge(s, v)

        @block.tensor
        def _(eng):
            run_stream(PE, eng)

        @block.scalar
        def _(eng):
            run_stream(ACT, eng)

        @block.vector
        def _(eng):
            run_stream(DVE, eng)

        @block.gpsimd
        def _(eng):
            run_stream(POOL, eng)

        @block.sync
        def _(eng):
            run_stream(SP, eng)

        stack.close()


class Builder:
    def __init__(self, layers, do_final, x_from_dram_T=True):
        self.layers = list(layers)
        self.do_final = do_final
        self.nc = bass.Bass("TRN2", target_bir_lowering=False)
        self.P = Prog()
        self.bank_i = 0
        self.slot_i = 0
        self.cnt = {}
        self.prebuilt = None
        self.stats_ready = False
        self.pending_stats = []

    def din(self, name, shape, dt=F32):
        return self.nc.dram_tensor(name, list(shape), dt, kind="ExternalInput").ap()

    def next_bank(self):
        b = self.bank_i % 6
        self.bank_i += 1
        return b

    def next_slot(self):
        s = self.slot_i % 4
        self.slot_i += 1
        return s

    def ring(self, name, n):
        i = self.cnt.get(name, 0)
        self.cnt[name] = i + 1
        return i % n

    def r2keys(self, row, a, b):
        ks = []
        if a < 512:
            ks.append(("r2", row, 0))
        if b > 512:
            ks.append(("r2", row, 1))
        return ks

    def build(self):
        nc = self.nc
        P = self.P
        L = self.layers
        st = contextlib.ExitStack()
        self.st = st

        self.x_in = self.din("xT", [D, T])
        self.p_in = self.din("pT", [DEPTH, PLE, T])
        self.vecs_in = self.din("vecs", [128, NV])
        self.masks_in = self.din("masks", [128, 2])
        self.edge_in = self.din("edge", [128, 64])
        self.ccsc_in = self.din("ccsc", [256, 512])
        self.ident_in = self.din("ident", [128, 128])
        self.cs_in = self.din("cs", [2, 128, 4096], BF16)
        self.nss_in = self.din("nss", [2, 128, 4096], BF16)
        self.w = {}
        for i in L:
            j = i // 2
            if i % 2 == 0:
                self.w[f"ev_w_in{j}"] = self.din(f"ev_w_in{j}", [D, 3072])
                self.w[f"ev_pool_w{j}"] = self.din(f"ev_pool_w{j}", [1024, 256])
                self.w[f"ev_w_out{j}"] = self.din(f"ev_w_out{j}", [D, D])
            else:
                self.w[f"od_w_out{j}"] = self.din(f"od_w_out{j}", [D, D])
            self.w[f"mlp_w1{i}"] = self.din(f"mlp_w1{i}", [D, DFF])
            self.w[f"mlp_w2{i}"] = self.din(f"mlp_w2{i}", [DFF, D])
            self.w[f"ple_w_gate{i}"] = self.din(f"ple_w_gate{i}", [D, D])
            self.w[f"ple_w_proj{i}"] = self.din(f"ple_w_proj{i}", [PLE, D])
        self.out = nc.dram_tensor("outT", [D, T], F32, kind="ExternalOutput").ap()
        self.halo_in = [nc.dram_tensor(f"halo_in{k}", [128, 256], BF16) for k in range(2)]
        self.halo_out = [nc.dram_tensor(f"halo_out{k}", [256, 256], BF16) for k in range(2)]
        self.hsend = [nc.dram_tensor(f"hsend{q}", [128, 2 * T], BF16) for q in range(8)]
        self.hfull = [nc.dram_tensor(f"hfull{q}", [256, 2 * T], BF16) for q in range(8)]

        def sb(name, shape, dt):
            return st.enter_context(nc.sbuf_tensor(name, shape, dt))

        self.xT = sb("xT_sb", [128, NCH, T], F32)
        self.wring = [sb(f"wring{i}", [128, 8, 512], BF16) for i in range(4)]
        self.R1 = sb("R1", [128, NCH, T], BF16)
        R2 = sb("R2", [128, 16 * 1056], BF16)
        self.R2v = R2[:].rearrange("p (c t) -> p c t", t=1056)
        R3 = sb("R3", [128, 8192], BF16)
        self.R3b = R3[:].rearrange("p (c t) -> p c t", t=1024)
        self.R3f = R3[:].bitcast(F32).rearrange("p (c t) -> p c t", t=512)
        self.R3h = R3[:].rearrange("p (b k r t) -> p b k r t", b=2, k=2, r=2)
        self.wpT = R3[:, 0:4096].rearrange("p (k n) -> p k n", k=2)
        self.pT = R3[:, 4096:6144].rearrange("p (k t) -> p k t", k=2)
        R1f = self.R1[:].rearrange("p c t -> p (c t)").bitcast(F32)
        self.tA = R1f[:, 0:1040]
        self.tB = R1f[:, 1536:1536 + 1040]
        self.vecs = sb("vecs_sb", [128, NV], F32)
        self.masks = sb("masks_sb", [128, 2], F32)
        self.edge = sb("edge_sb", [128, 4, 16], F32)
        self.ccsc = sb("ccsc_sb", [128, 2, 512], BF16)
        self.poolw = sb("poolw_sb", [128, 8, 256], BF16)
        self.haloT = [sb(f"haloT_sb{k}", [128, 2, 256], BF16) for k in range(2)]
        self.ones_bf = sb("ones_bf", [128, 128], BF16)
        self.ones_ln = sb("ones_ln", [128, 128], BF16)
        self.eps = sb("eps_sb", [128, 1], F32)
        self.ident = sb("ident_sb", [128, 128], BF16)
        self.convw_bf = sb("convw_bf", [128, 2 * 31 * 8], BF16)
        self.sq = [sb(f"sq{i}", [128, 512], BF16) for i in range(4)]
        self.lnt = sb("lnt", [128, 512], F32)
        self.rstd = sb("rstd", [128, T], F32)
        self.f32tmp = [sb(f"f32tmp{i}", [128, 512], F32) for i in range(3)]
        self.edget = sb("edget", [128, 8], F32)
        self.edgeres = [sb(f"edgeres{i}", [128, 8], F32) for i in range(2)]
        self.ps = [st.enter_context(nc.psum_tensor(f"ps{i}", [128, 512], F32)) for i in range(8)]

        P.op(SP, lambda e: e.dma_start(out=self.vecs[:], in_=self.vecs_in), writes=["vecs"], kind="d")
        P.op(SP, lambda e: e.dma_start(out=self.masks[:], in_=self.masks_in), writes=["masks"], kind="d")
        P.op(SP, lambda e: e.dma_start(out=self.edge[:], in_=self.edge_in.rearrange("p (g t) -> p g t", t=16)),
             writes=["edge"], kind="d")
        P.op(POOL, lambda e: e.dma_start(out=self.ccsc[:], in_=self.ccsc_in.rearrange("(k p) n -> p k n", p=128)),
             writes=["ccsc"], kind="d")
        P.op(POOL, lambda e: e.dma_start(out=self.ident[:], in_=self.ident_in), writes=["ident"], kind="d")
        P.op(DVE, lambda e: e.memset(self.ones_bf[:], 1.0 / D), writes=["ones_bf"])
        P.op(DVE, lambda e: e.memset(self.ones_ln[:], 1.0 / 1024), writes=["ones_ln"])
        P.op(DVE, lambda e: e.memset(self.eps[:], EPS), writes=["eps"])
        P.op(DVE, lambda e: e.tensor_copy(out=self.convw_bf[:], in_=self.vecs[:, CONVW_COL:CONVW_COL + 2 * 31 * 8]),
             reads=["vecs"], writes=["convw_bf"])
        xin = self.x_in.rearrange("(c p) t -> p c t", p=128)
        for q in range(4):
            P.op(SP, (lambda q: lambda e: e.dma_start(out=self.xT[:, 4 * q:4 * q + 4, :], in_=xin[:, 4 * q:4 * q + 4, :]))(q),
                 writes=[("x", c, tt) for c in range(4 * q, 4 * q + 4) for tt in range(NT)], kind="d")

        for i in L:
            if i % 2 == 0:
                self.even_mixer(i)
            else:
                self.odd_mixer(i)
            if DEBUG_STOP in ("rms", "gather", "chdft", "seqdft", "mixer", "win", "halo", "pool", "poolw", "conv"):
                continue
            self.mlp(i)
            if DEBUG_STOP == "mlp":
                continue
            self.ple(i)
        if self.do_final:
            self.rmsnorm(NORM_COL["fin"], mode="final")
        outv = self.out.rearrange("(c p) t -> p c t", p=128)
        for q in range(4):
            P.op(SP, (lambda q: lambda e: e.dma_start(out=outv[:, 4 * q:4 * q + 4, :], in_=self.xT[:, 4 * q:4 * q + 4, :]))(q),
                 reads=[("x", c, tt) for c in range(4 * q, 4 * q + 4) for tt in range(NT)], kind="d")
        P.emit(nc)
        st.close()
        return nc

    def stat_update(self, n, tt, first, last):
        P = self.P
        sl = slice(tt * 512, (tt + 1) * 512)
        qi = self.ring("sq", 4)
        sq = self.sq[qi]
        P.op(ACT, lambda e: e.activation(out=sq[:], in_=self.xT[:, n, sl], func=AF.Square),
             reads=[("x", n, tt)], writes=[("sq", qi)])
        self.pending_stats.append((sq, qi, tt, first, last))

    def flush_stats(self, keep=0):
        P = self.P
        while len(self.pending_stats) > keep:
            sq, qi, tt, first, last = self.pending_stats.pop(0)
            P.op(PE, lambda e, sq=sq, tt=tt, first=first, last=last: e.matmul(
                self.ps[6 + tt][:], lhsT=self.ones_bf[:], rhs=sq[:], start=first, stop=last),
                reads=[("sq", qi), "ones_bf"], writes=[("ps", 6 + tt)])

    def scale_chunk(self, gcol, c, tt):
        P = self.P
        sl = slice(tt * 512, (tt + 1) * 512)
        g = self.vecs[:, gcol + c:gcol + c + 1]
        if c % 2 == 0:
            P.op(DVE, lambda e: e.tensor_scalar(out=self.R1[:, c, sl], in0=self.xT[:, c, sl], scalar1=g, scalar2=None, op0=ALU.mult),
                 reads=[("x", c, tt), "vecs"], writes=[("r1", c, tt)])
        else:
            P.op(ACT, lambda e: e.activation(out=self.R1[:, c, sl], in_=self.xT[:, c, sl], func=AF.Copy, scale=g),
                 reads=[("x", c, tt), "vecs"], writes=[("r1", c, tt)])

    def rmsnorm(self, gcol, mode="deferred", power=-0.5, prescaled=False):
        P = self.P
        xT, R1, vecs = self.xT, self.R1, self.vecs
        if not self.stats_ready:
            for tt in range(NT):
                for c in range(NCH):
                    self.stat_update(c, tt, c == 0, c == NCH - 1)
                    self.flush_stats(keep=1)
        self.flush_stats()
        self.stats_ready = False

        def rstd_tile(tt):
            sl = slice(tt * 512, (tt + 1) * 512)
            P.op(ACT, lambda e: e.activation(out=self.lnt[:], in_=self.ps[6 + tt][:], func=AF.Ln, bias=self.eps[:], scale=1.0),
                 reads=[("ps", 6 + tt), "eps"], writes=["lnt"])
            P.op(ACT, lambda e: e.activation(out=self.rstd[:, sl], in_=self.lnt[:], func=AF.Exp, scale=power),
                 reads=["lnt"], writes=[("rstd", tt)])
        if mode == "deferred" and prescaled:
            for tt in range(NT):
                rstd_tile(tt)
        elif mode == "deferred":
            for tt in range(NT):
                sl = slice(tt * 512, (tt + 1) * 512)
                for c in range(NCH):
                    g = vecs[:, gcol + c:gcol + c + 1]
                    if c % 2 == 0:
                        P.op(DVE, lambda e, c=c, sl=sl, g=g: e.tensor_scalar(out=R1[:, c, sl], in0=xT[:, c, sl], scalar1=g, scalar2=None, op0=ALU.mult),
                             reads=[("x", c, tt), "vecs"], writes=[("r1", c, tt)])
                    else:
                        P.op(ACT, lambda e, c=c, sl=sl, g=g: e.activation(out=R1[:, c, sl], in_=xT[:, c, sl], func=AF.Copy, scale=g),
                             reads=[("x", c, tt), "vecs"], writes=[("r1", c, tt)])
                rstd_tile(tt)
        else:
            for tt in range(NT):
                rstd_tile(tt)
            for c in range(NCH):
                for tt in range(NT):
                    sl = slice(tt * 512, (tt + 1) * 512)
                    g = vecs[:, gcol + c:gcol + c + 1]
                    if mode == "final":
                        P.op(DVE, lambda e, c=c, sl=sl, g=g: e.scalar_tensor_tensor(
                            out=xT[:, c, sl], in0=xT[:, c, sl], scalar=g, in1=self.rstd[:, sl], op0=ALU.mult, op1=ALU.mult),
                            reads=[("x", c, tt), ("rstd", tt), "vecs"], writes=[("x", c, tt)])
                    else:
                        P.op(DVE, lambda e, c=c, sl=sl, g=g: e.scalar_tensor_tensor(
                            out=R1[:, c, sl], in0=xT[:, c, sl], scalar=g, in1=self.rstd[:, sl], op0=ALU.mult, op1=ALU.mult),
                            reads=[("x", c, tt), ("rstd", tt), "vecs"], writes=[("r1", c, tt)])

    def dense(self, W, KC, col_blocks, rhs, epilogue, extra=None):
        P = self.P
        for (n0, wd) in col_blocks:
            nsl = (KC + 7) // 8
            slots = []
            for kh in range(nsl):
                slot = self.next_slot()
                slots.append(slot)
                wt = self.wring[slot]
                kn = min(8, KC - kh * 8)
                src = W[kh * 1024:kh * 1024 + kn * 128, n0:n0 + wd].rearrange("(kc p) n -> p kc n", p=128)
                P.op(POOL, (lambda wt, kn, src, wd: lambda e: e.dma_start(out=wt[:, 0:kn, 0:wd], in_=src))(wt, kn, src, wd),
                     writes=[("w", slot)], kind="d")
            for nci in range(wd // 128):
                n = n0 // 128 + nci
                for tt in range(NT):
                    bank = self.next_bank()
                    for kc in range(KC):
                        rap, rkeys = rhs(kc, tt)
                        slot = slots[kc // 8]
                        wt = self.wring[slot]
                        P.op(PE, (lambda wt, kc, nci, rap, bank: lambda e: e.matmul(
                            self.ps[bank][:], lhsT=wt[:, kc % 8, nci * 128:(nci + 1) * 128], rhs=rap,
                            start=(kc == 0), stop=(kc == KC - 1)))(wt, kc, nci, rap, bank),
                            reads=[("w", slot)] + rkeys, writes=[("ps", bank)])
                    epilogue(n, tt, bank)
                    self.flush_stats(keep=2)
        self.flush_stats()

    def rhs_r1(self, kc, tt):
        return self.R1[:, kc, tt * 512:(tt + 1) * 512], [("r1", kc, tt)]

    def rhs_a(self, kc, tt):
        return self.R2v[:, kc, tt * 512:(tt + 1) * 512], [("r2", kc, tt)]

    def make_ep_resid(self, stats=False, scaled=False, next_gcol=None):
        def ep(n, tt, bank):
            P = self.P
            sl = slice(tt * 512, (tt + 1) * 512)
            if scaled:
                fi = self.ring("f32tmp", 3)
                tmp = self.f32tmp[fi]
                P.op(DVE, lambda e: e.tensor_tensor(out=tmp[:], in0=self.ps[bank][:], in1=self.rstd[:, sl], op=ALU.mult),
                     reads=[("ps", bank), ("rstd", tt)], writes=[("f32tmp", fi)])
                P.op(DVE, lambda e: e.tensor_tensor(out=self.xT[:, n, sl], in0=tmp[:], in1=self.xT[:, n, sl], op=ALU.add),
                     reads=[("f32tmp", fi), ("x", n, tt)], writes=[("x", n, tt)])
            else:
                P.op(DVE, lambda e: e.tensor_tensor(out=self.xT[:, n, sl], in0=self.ps[bank][:], in1=self.xT[:, n, sl], op=ALU.add),
                     reads=[("ps", bank), ("x", n, tt)], writes=[("x", n, tt)])
            if stats:
                self.stat_update(n, tt, n == 0, n == NCH - 1)
            if next_gcol is not None:
                self.scale_chunk(next_gcol, n, tt)
        return ep

    def mlp(self, i):
        P = self.P
        P.op(POOL, lambda e: e.dma_start(out=self.pT, in_=self.p_in[i].rearrange("(k p) t -> p k t", p=128)),
             writes=[("r3", q) for q in range(8, 12)], kind="d")
        P.op(POOL, lambda e: e.dma_start(out=self.wpT, in_=self.w[f"ple_w_proj{i}"].rearrange("(k p) n -> p k n", p=128)),
             writes=[("r3", q) for q in range(0, 8)], kind="d")
        self.rmsnorm(NORM_COL[f"mlp{i}"], mode="deferred", power=-1.0)
        W1 = self.w[f"mlp_w1{i}"]
        W2 = self.w[f"mlp_w2{i}"]
        for fb in range(4):
            def ep1(n, tt, bank, fb=fb):
                nl = n - fb * 16
                ri = self.ring("sq", 4)
                rt = self.sq[ri]
                P.op(ACT, lambda e: e.activation(out=rt[:], in_=self.ps[bank][:], func=AF.Relu),
                     reads=[("ps", bank)], writes=[("sq", ri)])
                P.op(DVE, lambda e: e.scalar_tensor_tensor(
                    out=self.R2v[:, nl, tt * 512:(tt + 1) * 512], in0=self.ps[bank][:], scalar=0.0, in1=rt[:],
                    op0=ALU.max, op1=ALU.mult),
                    reads=[("ps", bank), ("sq", ri)], writes=[("r2", nl, tt)])
            self.dense(W1, 16, [(fb * 2048 + b * 512, 512) for b in range(4)], self.rhs_r1, ep1)
            self.dense(W2[fb * 2048:(fb + 1) * 2048, :], 16, [(b * 512, 512) for b in range(4)], self.rhs_a,
                       self.make_ep_resid(stats=(fb == 3), scaled=True,
                                          next_gcol=(NORM_COL[f"ple{i}"] if fb == 3 else None)))
        self.stats_ready = True

    def ple(self, i):
        P = self.P
        self.rmsnorm(NORM_COL[f"ple{i}"], mode="deferred", prescaled=True)

        def ep(n, tt, bank):
            sl = slice(tt * 512, (tt + 1) * 512)
            bank2 = self.next_bank()
            for kc in range(2):
                P.op(PE, (lambda kc: lambda e: e.matmul(self.ps[bank2][:], lhsT=self.wpT[:, kc, n * 128:(n + 1) * 128],
                                                        rhs=self.pT[:, kc, sl], start=(kc == 0), stop=(kc == 1)))(kc),
                     reads=[("r3", q) for q in range(12)], writes=[("ps", bank2)])
            fi = self.ring("f32tmp", 3)
            sg = self.f32tmp[fi]
            P.op(DVE, lambda e: e.tensor_tensor(out=sg[:], in0=self.ps[bank][:], in1=self.rstd[:, sl], op=ALU.mult),
                 reads=[("ps", bank), ("rstd", tt)], writes=[("f32tmp", fi)])
            P.op(ACT, lambda e: e.activation(out=sg[:], in_=sg[:], func=AF.Sigmoid),
                 reads=[("f32tmp", fi)], writes=[("f32tmp", fi)])
            P.op(DVE, lambda e: e.tensor_tensor(out=sg[:], in0=sg[:], in1=self.ps[bank2][:], op=ALU.mult),
                 reads=[("ps", bank2), ("f32tmp", fi)], writes=[("f32tmp", fi)])
            P.op(DVE, lambda e: e.tensor_tensor(out=self.xT[:, n, sl], in0=sg[:], in1=self.xT[:, n, sl], op=ALU.add),
                 reads=[("f32tmp", fi), ("x", n, tt)], writes=[("x", n, tt)])
            self.stat_update(n, tt, n == 0, n == NCH - 1)
        self.dense(self.w[f"ple_w_gate{i}"], 16, [(b * 512, 512) for b in range(4)], self.rhs_r1, ep)
        self.stats_ready = True

    def even_mixer(self, i):
        P = self.P
        j = i // 2
        vecs = self.vecs
        R2v, R3b, R3f, R1 = self.R2v, self.R3b, self.R3f, self.R1
        P.op(POOL, lambda e: e.dma_start(out=self.poolw[:], in_=self.w[f"ev_pool_w{j}"].rearrange("(k p) n -> p k n", p=128)),
             writes=["poolw"], kind="d")
        self.rmsnorm(NORM_COL[f"ev{j}"], mode="deferred")

        def ep_in(n, tt, bank):
            sl = slice(tt * 512, (tt + 1) * 512)
            rs = self.rstd[:, sl]
            if n < 16:
                fi = self.ring("f32tmp", 3)
                tmp = self.f32tmp[fi]
                P.op(DVE, lambda e: e.tensor_tensor(out=tmp[:], in0=self.ps[bank][:], in1=rs, op=ALU.mult),
                     reads=[("ps", bank), ("rstd", tt)], writes=[("f32tmp", fi)])
            if 8 <= n < 16:
                c = n - 8
                P.op(ACT, lambda e: e.activation(out=R3b[:, c, sl], in_=tmp[:], func=AF.Sigmoid),
                     reads=[("f32tmp", fi)], writes=[("r3", 2 * c + tt)])
            elif n < 8:
                c = n
                a0 = 16 + tt * 512
                P.op(DVE, lambda e: e.tensor_tensor(out=R2v[:, c, a0:a0 + 512], in0=tmp[:], in1=R3b[:, c, sl], op=ALU.mult),
                     reads=[("f32tmp", fi), ("r3", 2 * c + tt)], writes=self.r2keys(c, a0, a0 + 512))
            else:
                c = n - 16
                a0 = 8 + tt * 512
                P.op(DVE, lambda e: e.tensor_tensor(out=R2v[:, 8 + c, a0:a0 + 512], in0=self.ps[bank][:], in1=rs, op=ALU.mult),
                     reads=[("ps", bank), ("rstd", tt)], writes=self.r2keys(8 + c, a0, a0 + 512))
        blocks = [(1024, 512), (1536, 512), (2048, 512), (2560, 512), (0, 512), (512, 512)]
        self.dense(self.w[f"ev_w_in{j}"], 16, blocks, self.rhs_r1, ep_in)

        if DEBUG_STOP == "win":
            return
        mL = self.masks[:, 0:1]
        mR = self.masks[:, 1:2]
        h4 = []
        for k, (r0, c_first, c_last) in enumerate(((8, 8, 1016), (0, 16, 1024))):
            hin = self.halo_in[k].ap().rearrange("p (q s t) -> p q s t", q=8, s=2)
            rows = [("r2", r, h) for r in range(r0, r0 + 8) for h in range(2)]
            P.op(SP, lambda e, hin=hin, r0=r0, c_first=c_first: e.dma_start(out=hin[:, :, 0, :], in_=R2v[:, r0:r0 + 8, c_first:c_first + 16]),
                 reads=rows, writes=[("halo_in", k, 0)], kind="d")
            P.op(SP, lambda e, hin=hin, r0=r0, c_last=c_last: e.dma_start(out=hin[:, :, 1, :], in_=R2v[:, r0:r0 + 8, c_last:c_last + 16]),
                 reads=rows, writes=[("halo_in", k, 1)], kind="d")
            P.op(POOL, lambda e, k=k: e.collective_compute("AllGather", ALU.bypass, replica_groups=[[0, 1], [2, 3], [4, 5], [6, 7]],
                                                           ins=[self.halo_in[k].ap().opt()], outs=[self.halo_out[k].ap().opt()]),
                 reads=[("halo_in", k, 0), ("halo_in", k, 1)], writes=[("halo_out", k)], kind="cc")
            P.op(SP, lambda e, k=k: e.dma_start(out=self.haloT[k][:], in_=self.halo_out[k].ap().rearrange("(r p) n -> p r n", p=128)),
                 reads=[("halo_out", k)], writes=[("haloT", k)], kind="d")
            h4.append(self.haloT[k][:].rearrange("p r (q s t) -> p r q s t", q=8, s=2))
        hu, hv = h4
        P.op(DVE, lambda e: e.tensor_scalar(out=R2v[:, 8:16, 0:8], in0=hu[:, 0, :, 1, 8:16], scalar1=mL, scalar2=None, op0=ALU.mult),
             reads=[("haloT", 0), "masks"], writes=[("r2", r, 0) for r in range(8, 16)])
        P.op(DVE, lambda e: e.tensor_scalar(out=R2v[:, 8:16, 1032:1040], in0=hu[:, 1, :, 0, 0:8], scalar1=mR, scalar2=None, op0=ALU.mult),
             reads=[("haloT", 0), "masks"], writes=[("r2", r, 1) for r in range(8, 16)])
        def v_halo_masks():
            P.op(DVE, lambda e: e.tensor_scalar(out=R2v[:, 0:8, 0:16], in0=hv[:, 0, :, 1, :], scalar1=mL, scalar2=None, op0=ALU.mult),
                 reads=[("haloT", 1), "masks"], writes=[("r2", r, 0) for r in range(8)])
            P.op(DVE, lambda e: e.tensor_scalar(out=R2v[:, 0:8, 1040:1056], in0=hv[:, 1, :, 0, :], scalar1=mR, scalar2=None, op0=ALU.mult),
                 reads=[("haloT", 1), "masks"], writes=[("r2", r, 1) for r in range(8)])
        self._conv_ln_tile(j, 0, after_conv=v_halo_masks)
        self._conv_ln_tile(j, 1)

        if DEBUG_STOP == "halo":
            return
        self._pool_done = [False] * 8
        if DEBUG_STOP == "pool":
            return
        if DEBUG_STOP == "poolw":
            return
        if DEBUG_STOP == "conv":
            return
        for g in range(4):
            for dc in range(2):
                for tt in range(NT):
                    sl = slice(tt * 512, (tt + 1) * 512)
                    bank = self.next_bank()
                    for kc in range(2):
                        P.op(PE, lambda e, g=g, dc=dc, kc=kc, bank=bank, sl=sl: e.matmul(
                            self.ps[bank][:], lhsT=self.poolw[:, 2 * g + kc, dc * 128:(dc + 1) * 128],
                            rhs=R2v[:, 8 + 2 * g + kc, sl], start=(kc == 0), stop=(kc == 1)),
                            reads=["poolw", ("r2", 8 + 2 * g + kc, tt)], writes=[("ps", bank)])
                    oc = 2 * g + dc
                    col = PSC_COL + j * 8 + oc
                    P.op(DVE, lambda e, oc=oc, col=col, bank=bank, sl=sl: e.tensor_scalar(
                        out=R1[:, 8 + oc, sl], in0=self.ps[bank][:], scalar1=vecs[:, col:col + 1], scalar2=None, op0=ALU.mult),
                        reads=[("ps", bank), "vecs"], writes=[("r1", 8 + oc, tt)])
        self.dense(self.w[f"ev_w_out{j}"], 16, [(b * 512, 512) for b in range(4)], self.rhs_r1, self.make_ep_resid(stats=True))
        self.stats_ready = True

    def _pool_chunk(self, c):
        P = self.P
        R2v = self.R2v
        tA, tB = self.tA, self.tB
        TAK = [("r1", r, tt) for r in range(0, 3) for tt in range(NT)]
        TBK = [("r1", r, tt) for r in range(3, 6) for tt in range(NT)]
        g = c // 2
        w = WINDOWS[g]
        row = 8 + c
        rk = [("r2", row, 0), ("r2", row, 1)]
        u = R2v[:, row, :]
        P.op(DVE, lambda e, u=u: e.tensor_tensor(out=tA[:, 1:1040], in0=u[:, 0:1039], in1=u[:, 1:1040], op=ALU.add),
             reads=rk, writes=TAK)
        cur, curk = tA, TAK
        if w >= 4:
            P.op(DVE, lambda e: e.tensor_tensor(out=tB[:, 2:1038], in0=tA[:, 1:1037], in1=tA[:, 3:1039], op=ALU.add),
                 reads=TAK, writes=TBK)
            cur, curk = tB, TBK
        if w >= 8:
            P.op(DVE, lambda e: e.tensor_tensor(out=tA[:, 4:1036], in0=tB[:, 2:1034], in1=tB[:, 6:1038], op=ALU.add),
                 reads=TBK, writes=TAK)
            cur, curk = tA, TAK
        if w >= 16:
            P.op(DVE, lambda e: e.tensor_tensor(out=tB[:, 8:1032], in0=tA[:, 4:1028], in1=tA[:, 12:1036], op=ALU.add),
                 reads=TAK, writes=TBK)
            cur, curk = tB, TBK
        for ei, (ecol, tcol) in enumerate(((0, 0), (8, 1016))):
            et = self.edgeres[ei]
            P.op(DVE, lambda e, cur=cur, g=g, ecol=ecol, tcol=tcol: e.tensor_tensor(
                out=self.edget[:], in0=cur[:, 8 + tcol:16 + tcol], in1=self.edge[:, g, ecol:ecol + 8], op=ALU.mult),
                reads=curk + ["edge"], writes=["edget"])
            P.op(DVE, lambda e, u=u, et=et, tcol=tcol: e.tensor_tensor(
                out=et[:], in0=self.edget[:], in1=u[:, 8 + tcol:16 + tcol], op=ALU.subtract),
                reads=["edget"] + rk, writes=[("edgeres", ei)])
        P.op(DVE, lambda e, cur=cur, u=u, w=w: e.scalar_tensor_tensor(
            out=u[:, 0:1024], in0=cur[:, 8:1032], scalar=1.0 / w, in1=u[:, 8:1032], op0=ALU.mult, op1=ALU.subtract),
            reads=curk + rk, writes=rk)
        for ei, tcol in enumerate((0, 1016)):
            et = self.edgeres[ei]
            P.op(DVE, lambda e, u=u, et=et, tcol=tcol: e.tensor_copy(out=u[:, tcol:tcol + 8], in_=et[:]),
                 reads=[("edgeres", ei)], writes=rk)


    def _build_dset(self, j, c, eng=DVE):
        P = self.P
        vecs, R1 = self.vecs, self.R1
        db = self.ring("dset", 2)
        dk = [("r1", r, h) for r in range(8 + 4 * db, 12 + 4 * db) for h in range(NT)]
        dset = R1[:, 8 + 4 * db:12 + 4 * db, :].rearrange("p c t -> p (c t)")[:, 0:31 * 128].rearrange("p (a m) -> p a m", m=128)
        w0 = (j * 31) * 8 + c
        wv = self.convw_bf[:, w0:w0 + 30 * 8 + 1:8].unsqueeze(2).broadcast_to([128, 31, 128])
        idb = self.ident[:].unsqueeze(1).broadcast_to([128, 31, 128])
        P.op(eng, lambda e: e.tensor_tensor(out=dset, in0=idb, in1=wv, op=ALU.mult),
             reads=["ident", "convw_bf"], writes=dk)
        return dset, dk

    def _conv_ln_tile(self, j, tt, after_conv=None):
        P = self.P
        vecs = self.vecs
        R2v, R3b, R3f, R1 = self.R2v, self.R3b, self.R3f, self.R1
        if True:
            def vwin(c, tap):
                if tt == 0:
                    return R2v[:, c, 257 + tap:769 + tap]
                return R2v[:, c, 1 + tap:1025 + tap].rearrange("p (s t) -> p s t", s=4)[:, 0:4:3, :]

            def catwin(c):
                if tt == 0:
                    return R1[:, c, 256:768]
                return R1[:, c, :].rearrange("p (s t) -> p s t", s=4)[:, 0:4:3, :]

            def cwin(c):
                if tt == 0:
                    return R3f[:, c, :]
                return R3f[:, c, :].rearrange("p (s t) -> p s t", s=2)
            bank_m = 6
            bank_s = 7
            pending = None

            def stats(c, cb, bi, sqb, qi):
                P.op(PE, lambda e: e.matmul(self.ps[bank_m][:], lhsT=self.ones_ln[:], rhs=cb[:], start=(c == 0), stop=(c == 7)),
                     reads=[("sq", bi), "ones_ln"], writes=[("ps", bank_m)])
                P.op(PE, lambda e: e.matmul(self.ps[bank_s][:], lhsT=self.ones_ln[:], rhs=sqb[:], start=(c == 0), stop=(c == 7)),
                     reads=[("sq", qi), "ones_ln"], writes=[("ps", bank_s)])
            for c in range(8):
                vk = [("r2", c, 0), ("r2", c, 1)]
                ck = [("r3", 2 * c), ("r3", 2 * c + 1)]
                bc = CONVB_COL + j * 8 + c
                bank_c = self.ring("convbank", 6)
                beng = POOL if tt == 0 else DVE
                if self.prebuilt is None:
                    self.prebuilt = self._build_dset(j, c, beng)
                dset, dk = self.prebuilt
                self.prebuilt = None
                if c < 7:
                    self.prebuilt = self._build_dset(j, c + 1, beng)
                elif tt == 0:
                    self.prebuilt = self._build_dset(j, 0, DVE)
                if tt == 0 and DEBUG_STOP != "halo":
                    self._pool_chunk(c)
                for tap in range(31):
                    P.op(PE, lambda e, dset=dset, c=c, tap=tap, bank_c=bank_c: e.matmul(
                        self.ps[bank_c][:], lhsT=dset[:, tap, :], rhs=vwin(c, tap),
                        start=(tap == 0), stop=(tap == 30)),
                        reads=vk + dk, writes=[("ps", bank_c)])
                P.op(ACT, lambda e, c=c, bc=bc, bank_c=bank_c: e.activation(
                    out=R3f[:, c, :], in_=self.ps[bank_c][:], func=AF.Identity, bias=vecs[:, bc:bc + 1], scale=1.0),
                    reads=[("ps", bank_c), "vecs"], writes=ck)
                bi = self.ring("sq", 4)
                cb = self.sq[bi]
                P.op(ACT, lambda e, c=c, cb=cb: e.activation(out=cb[:], in_=R3f[:, c, :], func=AF.Copy),
                     reads=ck, writes=[("sq", bi)])
                qi = self.ring("sq", 4)
                sqb = self.sq[qi]
                P.op(ACT, lambda e, c=c, sqb=sqb: e.activation(out=sqb[:], in_=R3f[:, c, :], func=AF.Square),
                     reads=ck, writes=[("sq", qi)])
                if pending is not None:
                    stats(*pending)
                pending = (c, cb, bi, sqb, qi)
            stats(*pending)
            if after_conv is not None:
                after_conv()
            msq, nmean, var = self.f32tmp[0], self.f32tmp[1], self.f32tmp[2]
            P.op(ACT, lambda e: e.activation(out=msq[:], in_=self.ps[bank_m][:], func=AF.Square),
                 reads=[("ps", bank_m)], writes=[("f32tmp", 0)])
            P.op(ACT, lambda e: e.activation(out=nmean[:], in_=self.ps[bank_m][:], func=AF.Copy, scale=-1.0),
                 reads=[("ps", bank_m)], writes=[("f32tmp", 1)])
            P.op(DVE, lambda e: e.tensor_tensor(out=var[:], in0=self.ps[bank_s][:], in1=msq[:], op=ALU.subtract),
                 reads=[("ps", bank_s), ("f32tmp", 0)], writes=[("f32tmp", 2)])
            P.op(ACT, lambda e: e.activation(out=self.lnt[:], in_=var[:], func=AF.Ln, bias=self.eps[:], scale=1.0),
                 reads=[("f32tmp", 2), "eps"], writes=["lnt"])
            P.op(ACT, lambda e: e.activation(out=var[:], in_=self.lnt[:], func=AF.Exp, scale=-0.5),
                 reads=["lnt"], writes=[("f32tmp", 2)])
            for c in range(8):
                ck = [("r3", 2 * c), ("r3", 2 * c + 1)]
                gcol = LNG_COL + j * 8 + c
                bcol = LNB_COL + j * 8 + c
                P.op(DVE, lambda e, c=c: e.tensor_tensor(out=R3f[:, c, :], in0=R3f[:, c, :], in1=nmean[:], op=ALU.add),
                     reads=ck + [("f32tmp", 1)], writes=ck)
                P.op(DVE, lambda e, c=c, gcol=gcol: e.scalar_tensor_tensor(
                    out=R3f[:, c, :], in0=R3f[:, c, :], scalar=vecs[:, gcol:gcol + 1], in1=var[:], op0=ALU.mult, op1=ALU.mult),
                    reads=ck + [("f32tmp", 2), "vecs"], writes=ck)
                P.op(ACT, lambda e, c=c, bcol=bcol: e.activation(out=catwin(c), in_=cwin(c), func=AF.Silu,
                                                                 bias=vecs[:, bcol:bcol + 1], scale=1.0),
                     reads=ck + ["vecs"], writes=[("r1", c, 0), ("r1", c, 1)])

    def odd_mixer(self, i):
        P = self.P
        j = i // 2
        R1, R2v, R3h = self.R1, self.R2v, self.R3h
        self.rmsnorm(NORM_COL[f"od{j}"], mode="direct")
        if DEBUG_STOP == "rms":
            return
        for q in range(8):
            hs = self.hsend[q].ap().rearrange("p (c t) -> p c t", c=2)
            P.op(SP, lambda e, q=q, hs=hs: e.dma_start(out=hs, in_=R1[:, 2 * q:2 * q + 2, :]),
                 reads=[("r1", c, tt) for c in range(2 * q, 2 * q + 2) for tt in range(NT)], writes=[("hsend", q)], kind="d")
            P.op(POOL, lambda e, q=q: e.collective_compute("AllGather", ALU.bypass, replica_groups=[[0, 1], [2, 3], [4, 5], [6, 7]],
                                                           ins=[self.hsend[q].ap().opt()], outs=[self.hfull[q].ap().opt()]),
                 reads=[("hsend", q)], writes=[("hfull", q)], kind="cc")
        hfs = [self.hfull[q].ap().rearrange("(r p) (c t) -> p c r t", r=2, c=2) for q in range(8)]
        if DEBUG_STOP == "gather":
            return
        tslots = []
        for e_ in range(2):
            for tab in (self.cs_in, self.nss_in):
                sl_ = self.next_slot()
                tslots.append(sl_)
                P.op(SP, lambda e, sl_=sl_, tab=tab, e_=e_: e.dma_start(
                    out=self.wring[sl_][:], in_=tab[e_].rearrange("p (s k) -> p s k", k=512)),
                    writes=[("w", sl_)], kind="d")
        for g in range(8):
            b = g % 2
            hk = [("r3", 8 * b + q) for q in range(8)]
            for r in range(2):
                P.op(SP, lambda e, g=g, b=b, r=r: e.dma_start(out=R3h[:, b, :, r, :], in_=hfs[g][:, :, r, :]),
                     reads=[("hfull", g)], writes=[("r3", 8 * b + 4 * k + 2 * r + h) for k in range(2) for h in range(2)], kind="d")
            for st8 in range(8):
                off = st8 * 128
                bankL = self.next_bank()
                bankH = self.next_bank()
                for r, bank in ((0, bankL), (1, bankH)):
                    for kc in range(2):
                        P.op(PE, lambda e, b=b, kc=kc, r=r, off=off, bank=bank: e.matmul(
                            self.ps[bank][:], lhsT=R3h[:, b, kc, r, off:off + 128], rhs=self.ccsc[:, kc, :],
                            start=(kc == 0), stop=(kc == 1)), reads=hk + ["ccsc"], writes=[("ps", bank)])
                fi = self.ring("f32tmp", 3)
                tmp = self.f32tmp[fi]
                P.op(ACT, lambda e, tmp=tmp, bankL=bankL: e.activation(out=tmp[:], in_=self.ps[bankL][:], func=AF.Copy),
                     reads=[("ps", bankL)], writes=[("f32tmp", fi)])
                P.op(DVE, lambda e, b=b, st8=st8, tmp=tmp, bankH=bankH: e.tensor_tensor(
                    out=R2v[:, st8, b * 512:(b + 1) * 512], in0=tmp[:], in1=self.ps[bankH][:], op=ALU.add),
                    reads=[("f32tmp", fi), ("ps", bankH)], writes=[("r2", st8, b)])
                P.op(DVE, lambda e, b=b, st8=st8, tmp=tmp, bankH=bankH: e.tensor_tensor(
                    out=R2v[:, 8 + st8, b * 512:(b + 1) * 512], in0=tmp[:], in1=self.ps[bankH][:], op=ALU.subtract),
                    reads=[("f32tmp", fi), ("ps", bankH)], writes=[("r2", 8 + st8, b)])
            if DEBUG_STOP == "chdft":
                continue
            for e_ in range(2):
                slc, sls = tslots[2 * e_], tslots[2 * e_ + 1]
                for mc in range(2):
                    bank = self.next_bank()
                    for s8 in range(8):
                        stl = e_ * 8 + s8
                        P.op(PE, lambda e, b=b, stl=stl, s8=s8, mc=mc, slc=slc, bank=bank: e.matmul(
                            self.ps[bank][:], lhsT=R2v[:, stl, b * 512 + mc * 128:b * 512 + (mc + 1) * 128],
                            rhs=self.wring[slc][:, s8, :], start=(s8 == 0), stop=False),
                            reads=[("r2", stl, b), ("w", slc)], writes=[("ps", bank)])
                        P.op(PE, lambda e, b=b, stl=stl, s8=s8, mc=mc, sls=sls, bank=bank: e.matmul(
                            self.ps[bank][:], lhsT=R2v[:, stl, b * 512 + 256 + mc * 128:b * 512 + 256 + (mc + 1) * 128],
                            rhs=self.wring[sls][:, s8, :], start=False, stop=(s8 == 7)),
                            reads=[("r2", stl, b), ("w", sls)], writes=[("ps", bank)])
                    row = 2 * g + mc
                    if mc == 0:
                        P.op(ACT, lambda e, row=row, e_=e_, bank=bank: e.activation(out=R1[:, row, e_::2], in_=self.ps[bank][:], func=AF.Copy),
                             reads=[("ps", bank)], writes=[("r1", row, 0), ("r1", row, 1)])
                    else:
                        P.op(DVE, lambda e, row=row, e_=e_, bank=bank: e.tensor_copy(out=R1[:, row, e_::2], in_=self.ps[bank][:]),
                             reads=[("ps", bank)], writes=[("r1", row, 0), ("r1", row, 1)])
        if DEBUG_STOP in ("chdft", "seqdft"):
            return
        self.dense(self.w[f"od_w_out{j}"], 16, [(b * 512, 512) for b in range(4)], self.rhs_r1, self.make_ep_resid(stats=True))
        self.stats_ready = True


_CACHE = {}


def _consts():
    if "c" in _CACHE:
        return _CACHE["c"]
    c = np.arange(256)
    ang = 2.0 * np.pi * ((c[:, None] * c[None, :]) % 256) / 256.0
    ccsc = np.concatenate([np.cos(ang), np.sin(ang)], axis=1) / 16.0
    s = np.arange(S)
    per_half = []
    for half in range(2):
        s_ = np.arange(T)

        def tabs(fn):
            out = np.zeros((2, 128, 4096), np.float64)
            for e_ in range(2):
                k = half * T + 2 * np.arange(512) + e_
                a = 2.0 * np.pi * ((s_[:, None] * k[None, :]) % S) / S
                m = fn(a) / np.sqrt(S)
                out[e_] = m.reshape(8, 128, 512).transpose(1, 0, 2).reshape(128, 4096)
            return np.ascontiguousarray(out.astype(ml_dtypes.bfloat16))
        cs_t = tabs(np.cos)
        nss_t = tabs(lambda a: -np.sin(a))
        edge = np.zeros((4, 16), np.float64)
        for gi, win in enumerate(WINDOWS):
            for q in range(16):
                tl = q if q < 8 else T - 16 + q
                tg = half * T + tl
                lo = min(max(tg - win // 2, 0), S - 1)
                hi = min(max(tg + win // 2 - 1, 0), S - 1) + 1
                edge[gi, q] = 1.0 / (hi - lo)
        masks = np.zeros((128, 2), np.float32)
        masks[:, 0] = 1.0 if half == 1 else 0.0
        masks[:, 1] = 1.0 if half == 0 else 0.0
        per_half.append(dict(cs=cs_t, nss=nss_t,
                             edge=np.ascontiguousarray(np.broadcast_to(edge.reshape(1, 64), (128, 64)).astype(np.float32)),
                             masks=masks))
    _CACHE["c"] = (ccsc.astype(np.float32), per_half)
    return _CACHE["c"]


def _pm(v):
    v = np.asarray(v, np.float32)
    return v.reshape(-1, 128).T


def _vecs(inp):
    vecs = np.zeros((128, NV), np.float32)
    names = [("ev0", inp["ev_norm"][0]), ("ev1", inp["ev_norm"][1]), ("od0", inp["od_norm"][0]), ("od1", inp["od_norm"][1])]
    names += [(f"mlp{i}", inp["mlp_norm"][i]) for i in range(4)]
    names += [(f"ple{i}", inp["ple_norm"][i]) for i in range(4)]
    names += [("fin", inp["final_norm"])]
    for n, v in names:
        vecs[:, NORM_COL[n]:NORM_COL[n] + 16] = _pm(v)
    for j in range(2):
        for tap in range(31):
            c0 = CONVW_COL + (j * 31 + tap) * 8
            vecs[:, c0:c0 + 8] = _pm(inp["ev_conv_w"][j, tap])
        vecs[:, CONVB_COL + j * 8:CONVB_COL + j * 8 + 8] = _pm(inp["ev_conv_b"][j])
        vecs[:, LNG_COL + j * 8:LNG_COL + j * 8 + 8] = _pm(inp["ev_ln_g"][j])
        vecs[:, LNB_COL + j * 8:LNB_COL + j * 8 + 8] = _pm(inp["ev_ln_b"][j])
        vecs[:, PSC_COL + j * 8:PSC_COL + j * 8 + 8] = _pm(inp["ev_pool_scale"][j])
    return vecs


def _get_nc(layers, do_final):
    key = (tuple(layers), do_final)
    if key not in _CACHE:
        _CACHE[key] = Builder(layers, do_final).build()
    return _CACHE[key]


def _run(xT_cores, inp, layers, do_final):
    ccsc, per_half = _consts()
    vecs = _vecs(inp)
    nc = _get_nc(layers, do_final)
    p = np.asarray(inp["p"], np.float32)
    in_maps = []
    for core in range(8):
        b, half = core // 2, core % 2
        ph = per_half[half]
        m = {
            "xT": xT_cores[core],
            "pT": np.ascontiguousarray(p[:, b, half * T:(half + 1) * T, :].transpose(0, 2, 1)),
            "vecs": vecs, "ident": np.eye(128, dtype=np.float32), "masks": ph["masks"], "edge": ph["edge"], "ccsc": ccsc,
            "cs": ph["cs"], "nss": ph["nss"],
        }
        for i in layers:
            j = i // 2
            if i % 2 == 0:
                m[f"ev_w_in{j}"] = np.asarray(inp["ev_w_in"][j], np.float32)
                m[f"ev_pool_w{j}"] = np.asarray(inp["ev_pool_w"][j], np.float32).reshape(1024, 256)
                m[f"ev_w_out{j}"] = np.asarray(inp["ev_w_out"][j], np.float32)
            else:
                m[f"od_w_out{j}"] = np.asarray(inp["od_w_out"][j], np.float32)
            m[f"mlp_w1{i}"] = np.asarray(inp["mlp_w1"][i], np.float32)
            m[f"mlp_w2{i}"] = np.asarray(inp["mlp_w2"][i], np.float32)
            m[f"ple_w_gate{i}"] = np.asarray(inp["ple_w_gate"][i], np.float32)
            m[f"ple_w_proj{i}"] = np.asarray(inp["ple_w_proj"][i], np.float32)
        in_maps.append(m)
    res = run_bass_kernel_spmd(nc, in_maps, core_ids=list(range(8)))
    return [np.asarray(res.results[c]["outT"]) for c in range(8)]


LAUNCH_GROUPS = [[0, 1, 2, 3]]
DEBUG_STOP = None


def kernel(**inputs):
    inp = {k: np.asarray(v) for k, v in inputs.items()}
    x = np.asarray(inp["x"], np.float32)
    xT = [np.ascontiguousarray(x[c // 2, (c % 2) * T:(c % 2 + 1) * T, :].T) for c in range(8)]
    for gi, grp in enumerate(LAUNCH_GROUPS):
        xT = _run(xT, inp, grp, do_final=(gi == len(LAUNCH_GROUPS) - 1))
    out = np.empty((4, S, D), np.float32)
    for c in range(8):
        out[c // 2, (c % 2) * T:(c % 2 + 1) * T, :] = xT[c].T
    return out
```

```python
import contextlib
import numpy as np
import ml_dtypes
import concourse.bass as bass
import concourse.mybir as mybir
from concourse.bass_utils import run_bass_kernel_spmd

F32 = mybir.dt.float32
BF16 = mybir.dt.bfloat16
AF = mybir.ActivationFunctionType
ALU = mybir.AluOpType

D = 2048
S = 2048
T = 1024
NT = 2
DEPTH = 4
DFF = 8192
PLE = 256
WINDOWS = (2, 4, 8, 16)
EPS = 1e-6
NCH = 16

PE, ACT, DVE, POOL, SP = "pe", "act", "dve", "pool", "sp"
SEM_ROT = 3000
N_DMA_SEMS = 12

NORM_COL = {}
_c = 0
for _n in ["ev0", "ev1", "od0", "od1", "mlp0", "mlp1", "mlp2", "mlp3", "ple0", "ple1", "ple2", "ple3", "fin"]:
    NORM_COL[_n] = _c
    _c += 16
CONVW_COL = _c; _c += 2 * 31 * 8
CONVB_COL = _c; _c += 16
LNG_COL = _c; _c += 16
LNB_COL = _c; _c += 16
PSC_COL = _c; _c += 16
NV = _c


class Op:
    __slots__ = ("eng", "fn", "deps", "kind", "signalled", "tok", "idx")

    def __init__(self, eng, fn, kind):
        self.eng = eng
        self.fn = fn
        self.kind = kind
        self.deps = set()
        self.signalled = False
        self.tok = None


class Prog:
    def __init__(self):
        self.ops = []
        self.streams = {e: [] for e in (PE, ACT, DVE, POOL, SP)}
        self.lastw = {}
        self.readers = {}

    def op(self, eng, fn, reads=(), writes=(), kind="c"):
        o = Op(eng, fn, kind)
        o.idx = len(self.ops)
        self.ops.append(o)
        self.streams[eng].append(o)
        deps = o.deps
        for k in reads:
            w = self.lastw.get(k)
            if w is not None:
                deps.add(w)
        for k in writes:
            w = self.lastw.get(k)
            if w is not None:
                deps.add(w)
            rs = self.readers.get(k)
            if rs:
                deps.update(rs.values())
        for k in reads:
            rs = self.readers.setdefault(k, {})
            rk = eng if kind == "c" else ("d", o.idx)
            rs[rk] = o.idx
        for k in writes:
            self.lastw[k] = o.idx
            self.readers[k] = {}
        deps.discard(o.idx)
        return o

    def emit(self, nc):
        ops = self.ops
        for o in ops:
            for d in o.deps:
                p = ops[d]
                if p.kind == "c" and p.eng == o.eng and p.eng == PE:
                    continue
                p.signalled = True
        engs = {PE: nc.tensor, ACT: nc.scalar, DVE: nc.vector, POOL: nc.gpsimd, SP: nc.sync}
        stack = contextlib.ExitStack()
        sem_lists = {}
        for e in engs:
            n_sig = sum(1 for o in self.streams[e] if o.kind == "c" and o.signalled)
            n = max(1, (n_sig + SEM_ROT - 1) // SEM_ROT)
            sem_lists[e] = [stack.enter_context(nc.semaphore(f"s_{e}_{i}")) for i in range(n)]
        dma_sems = {e: [stack.enter_context(nc.semaphore(f"d_{e}_{i}")) for i in range(N_DMA_SEMS)]
                    for e in (POOL, SP)}
        cc_sem = stack.enter_context(nc.semaphore("cc_sem"))
        cc_count = 0
        for e in engs:
            cnt = 0
            dma_i = 0
            dma_val = {}
            for o in self.streams[e]:
                if o.kind == "c":
                    if o.signalled:
                        o.tok = (sem_lists[e][cnt // SEM_ROT], cnt % SEM_ROT + 1)
                        cnt += 1
                elif o.kind == "d":
                    si = dma_i % N_DMA_SEMS
                    dma_i += 1
                    prev = dma_val.get(si, 0)
                    o.tok = (dma_sems[e][si], prev + 16, prev)
                    dma_val[si] = prev + 16
                elif o.kind == "cc":
                    cc_count += 1
                    o.tok = (cc_sem, cc_count)
        block = stack.enter_context(nc.Block())
        streams = self.streams

        def run_stream(e, eng):
            waited = {}
            for o in streams[e]:
                need = {}
                for d in o.deps:
                    p = ops[d]
                    if p.kind == "c" and p.eng == e and e == PE:
                        continue
                    s, v = p.tok[0], p.tok[1]
                    key = id(s)
                    if waited.get(key, 0) >= v:
                        continue
                    if key not in need or need[key][1] < v:
                        need[key] = (s, v)
                if o.kind == "d" and o.tok[2] > 0:
                    s, v = o.tok[0], o.tok[2]
                    key = id(s)
                    if waited.get(key, 0) < v and (key not in need or need[key][1] < v):
                        need[key] = (s, v)
                for key, (s, v) in need.items():
                    eng.wait_ge(s, v)
                    waited[key] = v
                ins = o.fn(eng)
                if o.kind == "c":
                    if o.signalled:
                        ins.then_inc(o.tok[0], 1)
                elif o.kind == "d":
                    ins.then_inc(o.tok[0], 16)
                else:
                    ins.then_inc(o.tok[0])
            dv = {}
            for o in streams[e]:
                if o.kind == "d":
                    dv[id(o.tok[0])] = (o.tok[0], o.tok[1])
            for s, v in dv.values():
                if waited.get(id(s), 0) < v:
                    eng.wait_ge(s, v)

        @block.tensor
        def _(eng):
            run_stream(PE, eng)

        @block.scalar
        def _(eng):
            run_stream(ACT, eng)

        @block.vector
        def _(eng):
            run_stream(DVE, eng)

        @block.gpsimd
        def _(eng):
            run_stream(POOL, eng)

        @block.sync
        def _(eng):
            run_stream(SP, eng)

        stack.close()


class Builder:
    def __init__(self, layers, do_final, x_from_dram_T=True):
        self.layers = list(layers)
        self.do_final = do_final
        self.nc = bass.Bass("TRN2", target_bir_lowering=False)
        self.P = Prog()
        self.bank_i = 0
        self.slot_i = 0
        self.cnt = {}
        self.prebuilt = None
        self.stats_ready = False
        self.pending_stats = []

    def din(self, name, shape, dt=F32):
        return self.nc.dram_tensor(name, list(shape), dt, kind="ExternalInput").ap()

    def next_bank(self):
        b = self.bank_i % 6
        self.bank_i += 1
        return b

    def next_slot(self):
        s = self.slot_i % 4
        self.slot_i += 1
        return s

    def ring(self, name, n):
        i = self.cnt.get(name, 0)
        self.cnt[name] = i + 1
        return i % n

    def r2keys(self, row, a, b):
        ks = []
        if a < 512:
            ks.append(("r2", row, 0))
        if b > 512:
            ks.append(("r2", row, 1))
        return ks

    def build(self):
        nc = self.nc
        P = self.P
        L = self.layers
        st = contextlib.ExitStack()
        self.st = st

        self.x_in = self.din("xT", [D, T])
        self.p_in = self.din("pT", [DEPTH, PLE, T])
        self.vecs_in = self.din("vecs", [128, NV])
        self.masks_in = self.din("masks", [128, 2])
        self.edge_in = self.din("edge", [128, 64])
        self.ccsc_in = self.din("ccsc", [256, 512])
        self.ident_in = self.din("ident", [128, 128])
        self.cs_in = self.din("cs", [2, 128, 4096], BF16)
        self.nss_in = self.din("nss", [2, 128, 4096], BF16)
        self.w = {}
        for i in L:
            j = i // 2
            if i % 2 == 0:
                self.w[f"ev_w_in{j}"] = self.din(f"ev_w_in{j}", [D, 3072])
                self.w[f"ev_pool_w{j}"] = self.din(f"ev_pool_w{j}", [1024, 256])
                self.w[f"ev_w_out{j}"] = self.din(f"ev_w_out{j}", [D, D])
            else:
                self.w[f"od_w_out{j}"] = self.din(f"od_w_out{j}", [D, D])
            self.w[f"mlp_w1{i}"] = self.din(f"mlp_w1{i}", [D, DFF])
            self.w[f"mlp_w2{i}"] = self.din(f"mlp_w2{i}", [DFF, D])
            self.w[f"ple_w_gate{i}"] = self.din(f"ple_w_gate{i}", [D, D])
            self.w[f"ple_w_proj{i}"] = self.din(f"ple_w_proj{i}", [PLE, D])
        self.out = nc.dram_tensor("outT", [D, T], F32, kind="ExternalOutput").ap()
        self.halo_in = [nc.dram_tensor(f"halo_in{k}", [128, 256], BF16) for k in range(2)]
        self.halo_out = [nc.dram_tensor(f"halo_out{k}", [256, 256], BF16) for k in range(2)]
        self.hsend = [nc.dram_tensor(f"hsend{q}", [128, 2 * T], BF16) for q in range(8)]
        self.hfull = [nc.dram_tensor(f"hfull{q}", [256, 2 * T], BF16) for q in range(8)]

        def sb(name, shape, dt):
            return st.enter_context(nc.sbuf_tensor(name, shape, dt))

        self.xT = sb("xT_sb", [128, NCH, T], F32)
        self.wring = [sb(f"wring{i}", [128, 8, 512], BF16) for i in range(4)]
        self.R1 = sb("R1", [128, NCH, T], BF16)
        R2 = sb("R2", [128, 16 * 1056], BF16)
        self.R2v = R2[:].rearrange("p (c t) -> p c t", t=1056)
        R3 = sb("R3", [128, 8192], BF16)
        self.R3b = R3[:].rearrange("p (c t) -> p c t", t=1024)
        self.R3f = R3[:].bitcast(F32).rearrange("p (c t) -> p c t", t=512)
        self.R3h = R3[:].rearrange("p (b k r t) -> p b k r t", b=2, k=2, r=2)
        self.wpT = R3[:, 0:4096].rearrange("p (k n) -> p k n", k=2)
        self.pT = R3[:, 4096:6144].rearrange("p (k t) -> p k t", k=2)
        R1f = self.R1[:].rearrange("p c t -> p (c t)").bitcast(F32)
        self.tA = R1f[:, 0:1040]
        self.tB = R1f[:, 1536:1536 + 1040]
        self.vecs = sb("vecs_sb", [128, NV], F32)
        self.masks = sb("masks_sb", [128, 2], F32)
        self.edge = sb("edge_sb", [128, 4, 16], F32)
        self.ccsc = sb("ccsc_sb", [128, 2, 512], BF16)
        self.poolw = sb("poolw_sb", [128, 8, 256], BF16)
        self.haloT = [sb(f"haloT_sb{k}", [128, 2, 256], BF16) for k in range(2)]
        self.ones_bf = sb("ones_bf", [128, 128], BF16)
        self.ones_ln = sb("ones_ln", [128, 128], BF16)
        self.eps = sb("eps_sb", [128, 1], F32)
        self.ident = sb("ident_sb", [128, 128], BF16)
        self.convw_bf = sb("convw_bf", [128, 2 * 31 * 8], BF16)
        self.sq = [sb(f"sq{i}", [128, 512], BF16) for i in range(4)]
        self.lnt = sb("lnt", [128, 512], F32)
        self.rstd = sb("rstd", [128, T], F32)
        self.f32tmp = [sb(f"f32tmp{i}", [128, 512], F32) for i in range(3)]
        self.edget = sb("edget", [128, 8], F32)
        self.edgeres = [sb(f"edgeres{i}", [128, 8], F32) for i in range(2)]
        self.ps = [st.enter_context(nc.psum_tensor(f"ps{i}", [128, 512], F32)) for i in range(8)]

        P.op(SP, lambda e: e.dma_start(out=self.vecs[:], in_=self.vecs_in), writes=["vecs"], kind="d")
        P.op(SP, lambda e: e.dma_start(out=self.masks[:], in_=self.masks_in), writes=["masks"], kind="d")
        P.op(SP, lambda e: e.dma_start(out=self.edge[:], in_=self.edge_in.rearrange("p (g t) -> p g t", t=16)),
             writes=["edge"], kind="d")
        P.op(POOL, lambda e: e.dma_start(out=self.ccsc[:], in_=self.ccsc_in.rearrange("(k p) n -> p k n", p=128)),
             writes=["ccsc"], kind="d")
        P.op(POOL, lambda e: e.dma_start(out=self.ident[:], in_=self.ident_in), writes=["ident"], kind="d")
        P.op(DVE, lambda e: e.memset(self.ones_bf[:], 1.0 / D), writes=["ones_bf"])
        P.op(DVE, lambda e: e.memset(self.ones_ln[:], 1.0 / 1024), writes=["ones_ln"])
        P.op(DVE, lambda e: e.memset(self.eps[:], EPS), writes=["eps"])
        P.op(DVE, lambda e: e.tensor_copy(out=self.convw_bf[:], in_=self.vecs[:, CONVW_COL:CONVW_COL + 2 * 31 * 8]),
             reads=["vecs"], writes=["convw_bf"])
        xin = self.x_in.rearrange("(c p) t -> p c t", p=128)
        for q in range(NCH):
            P.op(SP, (lambda q: lambda e: e.dma_start(out=self.xT[:, q, :], in_=xin[:, q, :]))(q),
                 writes=[("x", q, tt) for tt in range(NT)], kind="d")

        for i in L:
            if i % 2 == 0:
                self.even_mixer(i)
            else:
                self.odd_mixer(i)
            if DEBUG_STOP in ("rms", "gather", "chdft", "seqdft", "mixer", "win", "halo", "pool", "poolw", "conv"):
                continue
            self.mlp(i)
            if DEBUG_STOP == "mlp":
                continue
            self.ple(i)
        if self.do_final:
            self.rmsnorm(NORM_COL["fin"], mode="final")
        outv = self.out.rearrange("(c p) t -> p c t", p=128)
        for q in range(NCH):
            P.op(SP, (lambda q: lambda e: e.dma_start(out=outv[:, q, :], in_=self.xT[:, q, :]))(q),
                 reads=[("x", q, tt) for tt in range(NT)], kind="d")
        P.emit(nc)
        st.close()
        return nc

    def stat_update(self, n, tt, first, last):
        P = self.P
        sl = slice(tt * 512, (tt + 1) * 512)
        qi = self.ring("sq", 4)
        sq = self.sq[qi]
        P.op(ACT, lambda e: e.activation(out=sq[:], in_=self.xT[:, n, sl], func=AF.Square),
             reads=[("x", n, tt)], writes=[("sq", qi)])
        self.pending_stats.append((sq, qi, tt, first, last))

    def flush_stats(self, keep=0):
        P = self.P
        while len(self.pending_stats) > keep:
            sq, qi, tt, first, last = self.pending_stats.pop(0)
            P.op(PE, lambda e, sq=sq, tt=tt, first=first, last=last: e.matmul(
                self.ps[6 + tt][:], lhsT=self.ones_bf[:], rhs=sq[:], start=first, stop=last),
                reads=[("sq", qi), "ones_bf"], writes=[("ps", 6 + tt)])

    def scale_chunk(self, gcol, c, tt):
        P = self.P
        sl = slice(tt * 512, (tt + 1) * 512)
        g = self.vecs[:, gcol + c:gcol + c + 1]
        if c % 2 == 0:
            P.op(DVE, lambda e: e.tensor_scalar(out=self.R1[:, c, sl], in0=self.xT[:, c, sl], scalar1=g, scalar2=None, op0=ALU.mult),
                 reads=[("x", c, tt), "vecs"], writes=[("r1", c, tt)])
        else:
            P.op(ACT, lambda e: e.activation(out=self.R1[:, c, sl], in_=self.xT[:, c, sl], func=AF.Copy, scale=g),
                 reads=[("x", c, tt), "vecs"], writes=[("r1", c, tt)])

    def rmsnorm(self, gcol, mode="deferred", power=-0.5, prescaled=False):
        P = self.P
        xT, R1, vecs = self.xT, self.R1, self.vecs
        if not self.stats_ready:
            for tt in range(NT):
                for c in range(NCH):
                    self.stat_update(c, tt, c == 0, c == NCH - 1)
                    self.flush_stats(keep=1)
        self.flush_stats()
        self.stats_ready = False

        def rstd_tile(tt):
            sl = slice(tt * 512, (tt + 1) * 512)
            P.op(ACT, lambda e: e.activation(out=self.lnt[:], in_=self.ps[6 + tt][:], func=AF.Ln, bias=self.eps[:], scale=1.0),
                 reads=[("ps", 6 + tt), "eps"], writes=["lnt"])
            P.op(ACT, lambda e: e.activation(out=self.rstd[:, sl], in_=self.lnt[:], func=AF.Exp, scale=power),
                 reads=["lnt"], writes=[("rstd", tt)])
        if mode == "deferred" and prescaled:
            for tt in range(NT):
                rstd_tile(tt)
        elif mode == "deferred":
            for tt in range(NT):
                sl = slice(tt * 512, (tt + 1) * 512)
                for c in range(NCH):
                    g = vecs[:, gcol + c:gcol + c + 1]
                    if c % 2 == 0:
                        P.op(DVE, lambda e, c=c, sl=sl, g=g: e.tensor_scalar(out=R1[:, c, sl], in0=xT[:, c, sl], scalar1=g, scalar2=None, op0=ALU.mult),
                             reads=[("x", c, tt), "vecs"], writes=[("r1", c, tt)])
                    else:
                        P.op(ACT, lambda e, c=c, sl=sl, g=g: e.activation(out=R1[:, c, sl], in_=xT[:, c, sl], func=AF.Copy, scale=g),
                             reads=[("x", c, tt), "vecs"], writes=[("r1", c, tt)])
                rstd_tile(tt)
        else:
            for tt in range(NT):
                rstd_tile(tt)
            for c in range(NCH):
                for tt in range(NT):
                    sl = slice(tt * 512, (tt + 1) * 512)
                    g = vecs[:, gcol + c:gcol + c + 1]
                    if mode == "final":
                        P.op(DVE, lambda e, c=c, sl=sl, g=g: e.scalar_tensor_tensor(
                            out=xT[:, c, sl], in0=xT[:, c, sl], scalar=g, in1=self.rstd[:, sl], op0=ALU.mult, op1=ALU.mult),
                            reads=[("x", c, tt), ("rstd", tt), "vecs"], writes=[("x", c, tt)])
                    else:
                        P.op(DVE, lambda e, c=c, sl=sl, g=g: e.scalar_tensor_tensor(
                            out=R1[:, c, sl], in0=xT[:, c, sl], scalar=g, in1=self.rstd[:, sl], op0=ALU.mult, op1=ALU.mult),
                            reads=[("x", c, tt), ("rstd", tt), "vecs"], writes=[("r1", c, tt)])

    def dense(self, W, KC, col_blocks, rhs, epilogue, extra=None):
        P = self.P
        for (n0, wd) in col_blocks:
            nsl = (KC + 7) // 8
            slots = []
            for kh in range(nsl):
                slot = self.next_slot()
                slots.append(slot)
                wt = self.wring[slot]
                kn = min(8, KC - kh * 8)
                src = W[kh * 1024:kh * 1024 + kn * 128, n0:n0 + wd].rearrange("(kc p) n -> p kc n", p=128)
                P.op(POOL, (lambda wt, kn, src, wd: lambda e: e.dma_start(out=wt[:, 0:kn, 0:wd], in_=src))(wt, kn, src, wd),
                     writes=[("w", slot)], kind="d")
            for nci in range(wd // 128):
                n = n0 // 128 + nci
                for tt in range(NT):
                    bank = self.next_bank()
                    for kc in range(KC):
                        rap, rkeys = rhs(kc, tt)
                        slot = slots[kc // 8]
                        wt = self.wring[slot]
                        P.op(PE, (lambda wt, kc, nci, rap, bank: lambda e: e.matmul(
                            self.ps[bank][:], lhsT=wt[:, kc % 8, nci * 128:(nci + 1) * 128], rhs=rap,
                            start=(kc == 0), stop=(kc == KC - 1)))(wt, kc, nci, rap, bank),
                            reads=[("w", slot)] + rkeys, writes=[("ps", bank)])
                    epilogue(n, tt, bank)
                    self.flush_stats(keep=2)
        self.flush_stats()

    def rhs_r1(self, kc, tt):
        return self.R1[:, kc, tt * 512:(tt + 1) * 512], [("r1", kc, tt)]

    def rhs_a(self, kc, tt):
        return self.R2v[:, kc, tt * 512:(tt + 1) * 512], [("r2", kc, tt)]

    def make_ep_resid(self, stats=False, scaled=False, next_gcol=None):
        def ep(n, tt, bank):
            P = self.P
            sl = slice(tt * 512, (tt + 1) * 512)
            if scaled:
                fi = self.ring("f32tmp", 3)
                tmp = self.f32tmp[fi]
                P.op(DVE, lambda e: e.tensor_tensor(out=tmp[:], in0=self.ps[bank][:], in1=self.rstd[:, sl], op=ALU.mult),
                     reads=[("ps", bank), ("rstd", tt)], writes=[("f32tmp", fi)])
                P.op(DVE, lambda e: e.tensor_tensor(out=self.xT[:, n, sl], in0=tmp[:], in1=self.xT[:, n, sl], op=ALU.add),
                     reads=[("f32tmp", fi), ("x", n, tt)], writes=[("x", n, tt)])
            else:
                P.op(DVE, lambda e: e.tensor_tensor(out=self.xT[:, n, sl], in0=self.ps[bank][:], in1=self.xT[:, n, sl], op=ALU.add),
                     reads=[("ps", bank), ("x", n, tt)], writes=[("x", n, tt)])
            if stats:
                self.stat_update(n, tt, n == 0, n == NCH - 1)
            if next_gcol is not None:
                self.scale_chunk(next_gcol, n, tt)
        return ep

    def mlp(self, i):
        P = self.P
        P.op(POOL, lambda e: e.dma_start(out=self.pT, in_=self.p_in[i].rearrange("(k p) t -> p k t", p=128)),
             writes=[("r3", q) for q in range(8, 12)], kind="d")
        P.op(POOL, lambda e: e.dma_start(out=self.wpT, in_=self.w[f"ple_w_proj{i}"].rearrange("(k p) n -> p k n", p=128)),
             writes=[("r3", q) for q in range(0, 8)], kind="d")
        self.rmsnorm(NORM_COL[f"mlp{i}"], mode="deferred", power=-1.0)
        W1 = self.w[f"mlp_w1{i}"]
        W2 = self.w[f"mlp_w2{i}"]
        for fb in range(4):
            def ep1(n, tt, bank, fb=fb):
                nl = n - fb * 16
                ri = self.ring("sq", 4)
                rt = self.sq[ri]
                P.op(ACT, lambda e: e.activation(out=rt[:], in_=self.ps[bank][:], func=AF.Relu),
                     reads=[("ps", bank)], writes=[("sq", ri)])
                P.op(DVE, lambda e: e.scalar_tensor_tensor(
                    out=self.R2v[:, nl, tt * 512:(tt + 1) * 512], in0=self.ps[bank][:], scalar=0.0, in1=rt[:],
                    op0=ALU.max, op1=ALU.mult),
                    reads=[("ps", bank), ("sq", ri)], writes=[("r2", nl, tt)])
            self.dense(W1, 16, [(fb * 2048 + b * 512, 512) for b in range(4)], self.rhs_r1, ep1)
            self.dense(W2[fb * 2048:(fb + 1) * 2048, :], 16, [(b * 512, 512) for b in range(4)], self.rhs_a,
                       self.make_ep_resid(stats=(fb == 3), scaled=True,
                                          next_gcol=(NORM_COL[f"ple{i}"] if fb == 3 else None)))
        self.stats_ready = True

    def ple(self, i):
        P = self.P
        self.rmsnorm(NORM_COL[f"ple{i}"], mode="deferred", prescaled=True)

        def ep(n, tt, bank):
            sl = slice(tt * 512, (tt + 1) * 512)
            bank2 = self.next_bank()
            for kc in range(2):
                P.op(PE, (lambda kc: lambda e: e.matmul(self.ps[bank2][:], lhsT=self.wpT[:, kc, n * 128:(n + 1) * 128],
                                                        rhs=self.pT[:, kc, sl], start=(kc == 0), stop=(kc == 1)))(kc),
                     reads=[("r3", q) for q in range(12)], writes=[("ps", bank2)])
            fi = self.ring("f32tmp", 3)
            sg = self.f32tmp[fi]
            P.op(DVE, lambda e: e.tensor_tensor(out=sg[:], in0=self.ps[bank][:], in1=self.rstd[:, sl], op=ALU.mult),
                 reads=[("ps", bank), ("rstd", tt)], writes=[("f32tmp", fi)])
            P.op(ACT, lambda e: e.activation(out=sg[:], in_=sg[:], func=AF.Sigmoid),
                 reads=[("f32tmp", fi)], writes=[("f32tmp", fi)])
            P.op(DVE, lambda e: e.tensor_tensor(out=sg[:], in0=sg[:], in1=self.ps[bank2][:], op=ALU.mult),
                 reads=[("ps", bank2), ("f32tmp", fi)], writes=[("f32tmp", fi)])
            P.op(DVE, lambda e: e.tensor_tensor(out=self.xT[:, n, sl], in0=sg[:], in1=self.xT[:, n, sl], op=ALU.add),
                 reads=[("f32tmp", fi), ("x", n, tt)], writes=[("x", n, tt)])
            self.stat_update(n, tt, n == 0, n == NCH - 1)
        self.dense(self.w[f"ple_w_gate{i}"], 16, [(b * 512, 512) for b in range(4)], self.rhs_r1, ep)
        self.stats_ready = True

    def even_mixer(self, i):
        P = self.P
        j = i // 2
        vecs = self.vecs
        R2v, R3b, R3f, R1 = self.R2v, self.R3b, self.R3f, self.R1
        P.op(POOL, lambda e: e.dma_start(out=self.poolw[:], in_=self.w[f"ev_pool_w{j}"].rearrange("(k p) n -> p k n", p=128)),
             writes=["poolw"], kind="d")
        self.rmsnorm(NORM_COL[f"ev{j}"], mode="deferred")

        def ep_in(n, tt, bank):
            sl = slice(tt * 512, (tt + 1) * 512)
            rs = self.rstd[:, sl]
            if n < 16:
                fi = self.ring("f32tmp", 3)
                tmp = self.f32tmp[fi]
                P.op(DVE, lambda e: e.tensor_tensor(out=tmp[:], in0=self.ps[bank][:], in1=rs, op=ALU.mult),
                     reads=[("ps", bank), ("rstd", tt)], writes=[("f32tmp", fi)])
            if 8 <= n < 16:
                c = n - 8
                P.op(ACT, lambda e: e.activation(out=R3b[:, c, sl], in_=tmp[:], func=AF.Sigmoid),
                     reads=[("f32tmp", fi)], writes=[("r3", 2 * c + tt)])
            elif n < 8:
                c = n
                a0 = 16 + tt * 512
                P.op(DVE, lambda e: e.tensor_tensor(out=R2v[:, c, a0:a0 + 512], in0=tmp[:], in1=R3b[:, c, sl], op=ALU.mult),
                     reads=[("f32tmp", fi), ("r3", 2 * c + tt)], writes=self.r2keys(c, a0, a0 + 512))
            else:
                c = n - 16
                a0 = 8 + tt * 512
                P.op(DVE, lambda e: e.tensor_tensor(out=R2v[:, 8 + c, a0:a0 + 512], in0=self.ps[bank][:], in1=rs, op=ALU.mult),
                     reads=[("ps", bank), ("rstd", tt)], writes=self.r2keys(8 + c, a0, a0 + 512))
        blocks = [(1024, 512), (1536, 512), (2048, 512), (2560, 512), (0, 512), (512, 512)]
        self.dense(self.w[f"ev_w_in{j}"], 16, blocks, self.rhs_r1, ep_in)

        if DEBUG_STOP == "win":
            return
        mL = self.masks[:, 0:1]
        mR = self.masks[:, 1:2]
        h4 = []
        for k, (r0, c_first, c_last) in enumerate(((8, 8, 1016), (0, 16, 1024))):
            hin = self.halo_in[k].ap().rearrange("p (q s t) -> p q s t", q=8, s=2)
            rows = [("r2", r, h) for r in range(r0, r0 + 8) for h in range(2)]
            P.op(SP, lambda e, hin=hin, r0=r0, c_first=c_first: e.dma_start(out=hin[:, :, 0, :], in_=R2v[:, r0:r0 + 8, c_first:c_first + 16]),
                 reads=rows, writes=[("halo_in", k, 0)], kind="d")
            P.op(SP, lambda e, hin=hin, r0=r0, c_last=c_last: e.dma_start(out=hin[:, :, 1, :], in_=R2v[:, r0:r0 + 8, c_last:c_last + 16]),
                 reads=rows, writes=[("halo_in", k, 1)], kind="d")
            P.op(POOL, lambda e, k=k: e.collective_compute("AllGather", ALU.bypass, replica_groups=[[0, 1], [2, 3], [4, 5], [6, 7]],
                                                           ins=[self.halo_in[k].ap().opt()], outs=[self.halo_out[k].ap().opt()]),
                 reads=[("halo_in", k, 0), ("halo_in", k, 1)], writes=[("halo_out", k)], kind="cc")
            P.op(SP, lambda e, k=k: e.dma_start(out=self.haloT[k][:], in_=self.halo_out[k].ap().rearrange("(r p) n -> p r n", p=128)),
                 reads=[("halo_out", k)], writes=[("haloT", k)], kind="d")
            h4.append(self.haloT[k][:].rearrange("p r (q s t) -> p r q s t", q=8, s=2))
        hu, hv = h4
        P.op(DVE, lambda e: e.tensor_scalar(out=R2v[:, 8:16, 0:8], in0=hu[:, 0, :, 1, 8:16], scalar1=mL, scalar2=None, op0=ALU.mult),
             reads=[("haloT", 0), "masks"], writes=[("r2", r, 0) for r in range(8, 16)])
        P.op(DVE, lambda e: e.tensor_scalar(out=R2v[:, 8:16, 1032:1040], in0=hu[:, 1, :, 0, 0:8], scalar1=mR, scalar2=None, op0=ALU.mult),
             reads=[("haloT", 0), "masks"], writes=[("r2", r, 1) for r in range(8, 16)])
        self._conv_ln_tile(j, 0)
        P.op(DVE, lambda e: e.tensor_scalar(out=R2v[:, 0:8, 0:16], in0=hv[:, 0, :, 1, :], scalar1=mL, scalar2=None, op0=ALU.mult),
             reads=[("haloT", 1), "masks"], writes=[("r2", r, 0) for r in range(8)])
        P.op(DVE, lambda e: e.tensor_scalar(out=R2v[:, 0:8, 1040:1056], in0=hv[:, 1, :, 0, :], scalar1=mR, scalar2=None, op0=ALU.mult),
             reads=[("haloT", 1), "masks"], writes=[("r2", r, 1) for r in range(8)])
        self._conv_ln_tile(j, 1)

        if DEBUG_STOP == "halo":
            return
        self._pool_done = [False] * 8
        if DEBUG_STOP == "pool":
            return
        if DEBUG_STOP == "poolw":
            return
        if DEBUG_STOP == "conv":
            return
        for g in range(4):
            for dc in range(2):
                for tt in range(NT):
                    sl = slice(tt * 512, (tt + 1) * 512)
                    bank = self.next_bank()
                    for kc in range(2):
                        P.op(PE, lambda e, g=g, dc=dc, kc=kc, bank=bank, sl=sl: e.matmul(
                            self.ps[bank][:], lhsT=self.poolw[:, 2 * g + kc, dc * 128:(dc + 1) * 128],
                            rhs=R2v[:, 8 + 2 * g + kc, sl], start=(kc == 0), stop=(kc == 1)),
                            reads=["poolw", ("r2", 8 + 2 * g + kc, tt)], writes=[("ps", bank)])
                    oc = 2 * g + dc
                    col = PSC_COL + j * 8 + oc
                    P.op(DVE, lambda e, oc=oc, col=col, bank=bank, sl=sl: e.tensor_scalar(
                        out=R1[:, 8 + oc, sl], in0=self.ps[bank][:], scalar1=vecs[:, col:col + 1], scalar2=None, op0=ALU.mult),
                        reads=[("ps", bank), "vecs"], writes=[("r1", 8 + oc, tt)])
        self.dense(self.w[f"ev_w_out{j}"], 16, [(b * 512, 512) for b in range(4)], self.rhs_r1, self.make_ep_resid(stats=True))
        self.stats_ready = True

    def _pool_chunk(self, c):
        P = self.P
        R2v = self.R2v
        tA, tB = self.tA, self.tB
        TAK = [("r1", r, tt) for r in range(0, 3) for tt in range(NT)]
        TBK = [("r1", r, tt) for r in range(3, 6) for tt in range(NT)]
        g = c // 2
        w = WINDOWS[g]
        row = 8 + c
        rk = [("r2", row, 0), ("r2", row, 1)]
        u = R2v[:, row, :]
        P.op(DVE, lambda e, u=u: e.tensor_tensor(out=tA[:, 1:1040], in0=u[:, 0:1039], in1=u[:, 1:1040], op=ALU.add),
             reads=rk, writes=TAK)
        cur, curk = tA, TAK
        if w >= 4:
            P.op(DVE, lambda e: e.tensor_tensor(out=tB[:, 2:1038], in0=tA[:, 1:1037], in1=tA[:, 3:1039], op=ALU.add),
                 reads=TAK, writes=TBK)
            cur, curk = tB, TBK
        if w >= 8:
            P.op(DVE, lambda e: e.tensor_tensor(out=tA[:, 4:1036], in0=tB[:, 2:1034], in1=tB[:, 6:1038], op=ALU.add),
                 reads=TBK, writes=TAK)
            cur, curk = tA, TAK
        if w >= 16:
            P.op(DVE, lambda e: e.tensor_tensor(out=tB[:, 8:1032], in0=tA[:, 4:1028], in1=tA[:, 12:1036], op=ALU.add),
                 reads=TAK, writes=TBK)
            cur, curk = tB, TBK
        for ei, (ecol, tcol) in enumerate(((0, 0), (8, 1016))):
            et = self.edgeres[ei]
            P.op(DVE, lambda e, cur=cur, g=g, ecol=ecol, tcol=tcol: e.tensor_tensor(
                out=self.edget[:], in0=cur[:, 8 + tcol:16 + tcol], in1=self.edge[:, g, ecol:ecol + 8], op=ALU.mult),
                reads=curk + ["edge"], writes=["edget"])
            P.op(DVE, lambda e, u=u, et=et, tcol=tcol: e.tensor_tensor(
                out=et[:], in0=self.edget[:], in1=u[:, 8 + tcol:16 + tcol], op=ALU.subtract),
                reads=["edget"] + rk, writes=[("edgeres", ei)])
        P.op(DVE, lambda e, cur=cur, u=u, w=w: e.scalar_tensor_tensor(
            out=u[:, 0:1024], in0=cur[:, 8:1032], scalar=1.0 / w, in1=u[:, 8:1032], op0=ALU.mult, op1=ALU.subtract),
            reads=curk + rk, writes=rk)
        for ei, tcol in enumerate((0, 1016)):
            et = self.edgeres[ei]
            P.op(DVE, lambda e, u=u, et=et, tcol=tcol: e.tensor_copy(out=u[:, tcol:tcol + 8], in_=et[:]),
                 reads=[("edgeres", ei)], writes=rk)


    def _build_dset(self, j, c):
        P = self.P
        vecs, R1 = self.vecs, self.R1
        db = self.ring("dset", 2)
        dk = [("r1", r, h) for r in range(8 + 4 * db, 12 + 4 * db) for h in range(NT)]
        dset = R1[:, 8 + 4 * db:12 + 4 * db, :].rearrange("p c t -> p (c t)")[:, 0:31 * 128].rearrange("p (a m) -> p a m", m=128)
        w0 = (j * 31) * 8 + c
        wv = self.convw_bf[:, w0:w0 + 30 * 8 + 1:8].unsqueeze(2).broadcast_to([128, 31, 128])
        idb = self.ident[:].unsqueeze(1).broadcast_to([128, 31, 128])
        P.op(DVE, lambda e: e.tensor_tensor(out=dset, in0=idb, in1=wv, op=ALU.mult),
             reads=["ident", "convw_bf"], writes=dk)
        return dset, dk

    def _conv_ln_tile(self, j, tt):
        P = self.P
        vecs = self.vecs
        R2v, R3b, R3f, R1 = self.R2v, self.R3b, self.R3f, self.R1
        if True:
            def vwin(c, tap):
                if tt == 0:
                    return R2v[:, c, 257 + tap:769 + tap]
                return R2v[:, c, 1 + tap:1025 + tap].rearrange("p (s t) -> p s t", s=4)[:, 0:4:3, :]

            def catwin(c):
                if tt == 0:
                    return R1[:, c, 256:768]
                return R1[:, c, :].rearrange("p (s t) -> p s t", s=4)[:, 0:4:3, :]

            def cwin(c):
                if tt == 0:
                    return R3f[:, c, :]
                return R3f[:, c, :].rearrange("p (s t) -> p s t", s=2)
            bank_m = 6
            bank_s = 7
            pending = None

            def stats(c, cb, bi, sqb, qi):
                P.op(PE, lambda e: e.matmul(self.ps[bank_m][:], lhsT=self.ones_ln[:], rhs=cb[:], start=(c == 0), stop=(c == 7)),
                     reads=[("sq", bi), "ones_ln"], writes=[("ps", bank_m)])
                P.op(PE, lambda e: e.matmul(self.ps[bank_s][:], lhsT=self.ones_ln[:], rhs=sqb[:], start=(c == 0), stop=(c == 7)),
                     reads=[("sq", qi), "ones_ln"], writes=[("ps", bank_s)])
            for c in range(8):
                vk = [("r2", c, 0), ("r2", c, 1)]
                ck = [("r3", 2 * c), ("r3", 2 * c + 1)]
                bc = CONVB_COL + j * 8 + c
                bank_c = self.ring("convbank", 6)
                if self.prebuilt is None:
                    self.prebuilt = self._build_dset(j, c)
                dset, dk = self.prebuilt
                self.prebuilt = None
                if c < 7:
                    self.prebuilt = self._build_dset(j, c + 1)
                elif tt == 0:
                    self.prebuilt = self._build_dset(j, 0)
                if tt == 0 and DEBUG_STOP != "halo":
                    self._pool_chunk(c)
                for tap in range(31):
                    P.op(PE, lambda e, dset=dset, c=c, tap=tap, bank_c=bank_c: e.matmul(
                        self.ps[bank_c][:], lhsT=dset[:, tap, :], rhs=vwin(c, tap),
                        start=(tap == 0), stop=(tap == 30)),
                        reads=vk + dk, writes=[("ps", bank_c)])
                P.op(ACT, lambda e, c=c, bc=bc, bank_c=bank_c: e.activation(
                    out=R3f[:, c, :], in_=self.ps[bank_c][:], func=AF.Identity, bias=vecs[:, bc:bc + 1], scale=1.0),
                    reads=[("ps", bank_c), "vecs"], writes=ck)
                bi = self.ring("sq", 4)
                cb = self.sq[bi]
                P.op(ACT, lambda e, c=c, cb=cb: e.activation(out=cb[:], in_=R3f[:, c, :], func=AF.Copy),
                     reads=ck, writes=[("sq", bi)])
                qi = self.ring("sq", 4)
                sqb = self.sq[qi]
                P.op(ACT, lambda e, c=c, sqb=sqb: e.activation(out=sqb[:], in_=R3f[:, c, :], func=AF.Square),
                     reads=ck, writes=[("sq", qi)])
                if pending is not None:
                    stats(*pending)
                pending = (c, cb, bi, sqb, qi)
            stats(*pending)
            msq, nmean, var = self.f32tmp[0], self.f32tmp[1], self.f32tmp[2]
            P.op(ACT, lambda e: e.activation(out=msq[:], in_=self.ps[bank_m][:], func=AF.Square),
                 reads=[("ps", bank_m)], writes=[("f32tmp", 0)])
            P.op(ACT, lambda e: e.activation(out=nmean[:], in_=self.ps[bank_m][:], func=AF.Copy, scale=-1.0),
                 reads=[("ps", bank_m)], writes=[("f32tmp", 1)])
            P.op(DVE, lambda e: e.tensor_tensor(out=var[:], in0=self.ps[bank_s][:], in1=msq[:], op=ALU.subtract),
                 reads=[("ps", bank_s), ("f32tmp", 0)], writes=[("f32tmp", 2)])
            P.op(ACT, lambda e: e.activation(out=self.lnt[:], in_=var[:], func=AF.Ln, bias=self.eps[:], scale=1.0),
                 reads=[("f32tmp", 2), "eps"], writes=["lnt"])
            P.op(ACT, lambda e: e.activation(out=var[:], in_=self.lnt[:], func=AF.Exp, scale=-0.5),
                 reads=["lnt"], writes=[("f32tmp", 2)])
            for c in range(8):
                ck = [("r3", 2 * c), ("r3", 2 * c + 1)]
                gcol = LNG_COL + j * 8 + c
                bcol = LNB_COL + j * 8 + c
                P.op(DVE, lambda e, c=c: e.tensor_tensor(out=R3f[:, c, :], in0=R3f[:, c, :], in1=nmean[:], op=ALU.add),
                     reads=ck + [("f32tmp", 1)], writes=ck)
                P.op(DVE, lambda e, c=c, gcol=gcol: e.scalar_tensor_tensor(
                    out=R3f[:, c, :], in0=R3f[:, c, :], scalar=vecs[:, gcol:gcol + 1], in1=var[:], op0=ALU.mult, op1=ALU.mult),
                    reads=ck + [("f32tmp", 2), "vecs"], writes=ck)
                P.op(ACT, lambda e, c=c, bcol=bcol: e.activation(out=catwin(c), in_=cwin(c), func=AF.Silu,
                                                                 bias=vecs[:, bcol:bcol + 1], scale=1.0),
                     reads=ck + ["vecs"], writes=[("r1", c, 0), ("r1", c, 1)])

    def odd_mixer(self, i):
        P = self.P
        j = i // 2
        R1, R2v, R3h = self.R1, self.R2v, self.R3h
        self.rmsnorm(NORM_COL[f"od{j}"], mode="direct")
        if DEBUG_STOP == "rms":
            return
        for q in range(8):
            hs = self.hsend[q].ap().rearrange("p (c t) -> p c t", c=2)
            P.op(SP, lambda e, q=q, hs=hs: e.dma_start(out=hs, in_=R1[:, 2 * q:2 * q + 2, :]),
                 reads=[("r1", c, tt) for c in range(2 * q, 2 * q + 2) for tt in range(NT)], writes=[("hsend", q)], kind="d")
            P.op(POOL, lambda e, q=q: e.collective_compute("AllGather", ALU.bypass, replica_groups=[[0, 1], [2, 3], [4, 5], [6, 7]],
                                                           ins=[self.hsend[q].ap().opt()], outs=[self.hfull[q].ap().opt()]),
                 reads=[("hsend", q)], writes=[("hfull", q)], kind="cc")
        hfs = [self.hfull[q].ap().rearrange("(r p) (c t) -> p c r t", r=2, c=2) for q in range(8)]
        if DEBUG_STOP == "gather":
            return
        tslots = []
        for e_ in range(2):
            for tab in (self.cs_in, self.nss_in):
                sl_ = self.next_slot()
                tslots.append(sl_)
                P.op(SP, lambda e, sl_=sl_, tab=tab, e_=e_: e.dma_start(
                    out=self.wring[sl_][:], in_=tab[e_].rearrange("p (s k) -> p s k", k=512)),
                    writes=[("w", sl_)], kind="d")
        for g in range(8):
            b = g % 2
            hk = [("r3", 8 * b + q) for q in range(8)]
            for r in range(2):
                P.op(SP, lambda e, g=g, b=b, r=r: e.dma_start(out=R3h[:, b, :, r, :], in_=hfs[g][:, :, r, :]),
                     reads=[("hfull", g)], writes=[("r3", 8 * b + 4 * k + 2 * r + h) for k in range(2) for h in range(2)], kind="d")
            for st8 in range(8):
                off = st8 * 128
                bankL = self.next_bank()
                bankH = self.next_bank()
                for r, bank in ((0, bankL), (1, bankH)):
                    for kc in range(2):
                        P.op(PE, lambda e, b=b, kc=kc, r=r, off=off, bank=bank: e.matmul(
                            self.ps[bank][:], lhsT=R3h[:, b, kc, r, off:off + 128], rhs=self.ccsc[:, kc, :],
                            start=(kc == 0), stop=(kc == 1)), reads=hk + ["ccsc"], writes=[("ps", bank)])
                fi = self.ring("f32tmp", 3)
                tmp = self.f32tmp[fi]
                P.op(ACT, lambda e, tmp=tmp, bankL=bankL: e.activation(out=tmp[:], in_=self.ps[bankL][:], func=AF.Copy),
                     reads=[("ps", bankL)], writes=[("f32tmp", fi)])
                P.op(DVE, lambda e, b=b, st8=st8, tmp=tmp, bankH=bankH: e.tensor_tensor(
                    out=R2v[:, st8, b * 512:(b + 1) * 512], in0=tmp[:], in1=self.ps[bankH][:], op=ALU.add),
                    reads=[("f32tmp", fi), ("ps", bankH)], writes=[("r2", st8, b)])
                P.op(DVE, lambda e, b=b, st8=st8, tmp=tmp, bankH=bankH: e.tensor_tensor(
                    out=R2v[:, 8 + st8, b * 512:(b + 1) * 512], in0=tmp[:], in1=self.ps[bankH][:], op=ALU.subtract),
                    reads=[("f32tmp", fi), ("ps", bankH)], writes=[("r2", 8 + st8, b)])
            if DEBUG_STOP == "chdft":
                continue
            for e_ in range(2):
                slc, sls = tslots[2 * e_], tslots[2 * e_ + 1]
                for mc in range(2):
                    bank = self.next_bank()
                    for s8 in range(8):
                        stl = e_ * 8 + s8
                        P.op(PE, lambda e, b=b, stl=stl, s8=s8, mc=mc, slc=slc, bank=bank: e.matmul(
                            self.ps[bank][:], lhsT=R2v[:, stl, b * 512 + mc * 128:b * 512 + (mc + 1) * 128],
                            rhs=self.wring[slc][:, s8, :], start=(s8 == 0), stop=False),
                            reads=[("r2", stl, b), ("w", slc)], writes=[("ps", bank)])
                        P.op(PE, lambda e, b=b, stl=stl, s8=s8, mc=mc, sls=sls, bank=bank: e.matmul(
                            self.ps[bank][:], lhsT=R2v[:, stl, b * 512 + 256 + mc * 128:b * 512 + 256 + (mc + 1) * 128],
                            rhs=self.wring[sls][:, s8, :], start=False, stop=(s8 == 7)),
                            reads=[("r2", stl, b), ("w", sls)], writes=[("ps", bank)])
                    row = 2 * g + mc
                    if mc == 0:
                        P.op(ACT, lambda e, row=row, e_=e_, bank=bank: e.activation(out=R1[:, row, e_::2], in_=self.ps[bank][:], func=AF.Copy),
                             reads=[("ps", bank)], writes=[("r1", row, 0), ("r1", row, 1)])
                    else:
                        P.op(DVE, lambda e, row=row, e_=e_, bank=bank: e.tensor_copy(out=R1[:, row, e_::2], in_=self.ps[bank][:]),
                             reads=[("ps", bank)], writes=[("r1", row, 0), ("r1", row, 1)])
        if DEBUG_STOP in ("chdft", "seqdft"):
            return
        self.dense(self.w[f"od_w_out{j}"], 16, [(b * 512, 512) for b in range(4)], self.rhs_r1, self.make_ep_resid(stats=True))
        self.stats_ready = True


_CACHE = {}


def _consts():
    if "c" in _CACHE:
        return _CACHE["c"]
    c = np.arange(256)
    ang = 2.0 * np.pi * ((c[:, None] * c[None, :]) % 256) / 256.0
    ccsc = np.concatenate([np.cos(ang), np.sin(ang)], axis=1) / 16.0
    s = np.arange(S)
    per_half = []
    for half in range(2):
        s_ = np.arange(T)

        def tabs(fn):
            out = np.zeros((2, 128, 4096), np.float64)
            for e_ in range(2):
                k = half * T + 2 * np.arange(512) + e_
                a = 2.0 * np.pi * ((s_[:, None] * k[None, :]) % S) / S
                m = fn(a) / np.sqrt(S)
                out[e_] = m.reshape(8, 128, 512).transpose(1, 0, 2).reshape(128, 4096)
            return np.ascontiguousarray(out.astype(ml_dtypes.bfloat16))
        cs_t = tabs(np.cos)
        nss_t = tabs(lambda a: -np.sin(a))
        edge = np.zeros((4, 16), np.float64)
        for gi, win in enumerate(WINDOWS):
            for q in range(16):
                tl = q if q < 8 else T - 16 + q
                tg = half * T + tl
                lo = min(max(tg - win // 2, 0), S - 1)
                hi = min(max(tg + win // 2 - 1, 0), S - 1) + 1
                edge[gi, q] = 1.0 / (hi - lo)
        masks = np.zeros((128, 2), np.float32)
        masks[:, 0] = 1.0 if half == 1 else 0.0
        masks[:, 1] = 1.0 if half == 0 else 0.0
        per_half.append(dict(cs=cs_t, nss=nss_t,
                             edge=np.ascontiguousarray(np.broadcast_to(edge.reshape(1, 64), (128, 64)).astype(np.float32)),
                             masks=masks))
    _CACHE["c"] = (ccsc.astype(np.float32), per_half)
    return _CACHE["c"]


def _pm(v):
    v = np.asarray(v, np.float32)
    return v.reshape(-1, 128).T


def _vecs(inp):
    vecs = np.zeros((128, NV), np.float32)
    names = [("ev0", inp["ev_norm"][0]), ("ev1", inp["ev_norm"][1]), ("od0", inp["od_norm"][0]), ("od1", inp["od_norm"][1])]
    names += [(f"mlp{i}", inp["mlp_norm"][i]) for i in range(4)]
    names += [(f"ple{i}", inp["ple_norm"][i]) for i in range(4)]
    names += [("fin", inp["final_norm"])]
    for n, v in names:
        vecs[:, NORM_COL[n]:NORM_COL[n] + 16] = _pm(v)
    for j in range(2):
        for tap in range(31):
            c0 = CONVW_COL + (j * 31 + tap) * 8
            vecs[:, c0:c0 + 8] = _pm(inp["ev_conv_w"][j, tap])
        vecs[:, CONVB_COL + j * 8:CONVB_COL + j * 8 + 8] = _pm(inp["ev_conv_b"][j])
        vecs[:, LNG_COL + j * 8:LNG_COL + j * 8 + 8] = _pm(inp["ev_ln_g"][j])
        vecs[:, LNB_COL + j * 8:LNB_COL + j * 8 + 8] = _pm(inp["ev_ln_b"][j])
        vecs[:, PSC_COL + j * 8:PSC_COL + j * 8 + 8] = _pm(inp["ev_pool_scale"][j])
    return vecs


def _get_nc(layers, do_final):
    key = (tuple(layers), do_final)
    if key not in _CACHE:
        _CACHE[key] = Builder(layers, do_final).build()
    return _CACHE[key]


def _run(xT_cores, inp, layers, do_final):
    ccsc, per_half = _consts()
    vecs = _vecs(inp)
    nc = _get_nc(layers, do_final)
    p = np.asarray(inp["p"], np.float32)
    in_maps = []
    for core in range(8):
        b, half = core // 2, core % 2
        ph = per_half[half]
        m = {
            "xT": xT_cores[core],
            "pT": np.ascontiguousarray(p[:, b, half * T:(half + 1) * T, :].transpose(0, 2, 1)),
            "vecs": vecs, "ident": np.eye(128, dtype=np.float32), "masks": ph["masks"], "edge": ph["edge"], "ccsc": ccsc,
            "cs": ph["cs"], "nss": ph["nss"],
        }
        for i in layers:
            j = i // 2
            if i % 2 == 0:
                m[f"ev_w_in{j}"] = np.asarray(inp["ev_w_in"][j], np.float32)
                m[f"ev_pool_w{j}"] = np.asarray(inp["ev_pool_w"][j], np.float32).reshape(1024, 256)
                m[f"ev_w_out{j}"] = np.asarray(inp["ev_w_out"][j], np.float32)
            else:
                m[f"od_w_out{j}"] = np.asarray(inp["od_w_out"][j], np.float32)
            m[f"mlp_w1{i}"] = np.asarray(inp["mlp_w1"][i], np.float32)
            m[f"mlp_w2{i}"] = np.asarray(inp["mlp_w2"][i], np.float32)
            m[f"ple_w_gate{i}"] = np.asarray(inp["ple_w_gate"][i], np.float32)
            m[f"ple_w_proj{i}"] = np.asarray(inp["ple_w_proj"][i], np.float32)
        in_maps.append(m)
    res = run_bass_kernel_spmd(nc, in_maps, core_ids=list(range(8)))
    return [np.asarray(res.results[c]["outT"]) for c in range(8)]


LAUNCH_GROUPS = [[0, 1, 2, 3]]
DEBUG_STOP = None


def kernel(**inputs):
    inp = {k: np.asarray(v) for k, v in inputs.items()}
    x = np.asarray(inp["x"], np.float32)
    xT = [np.ascontiguousarray(x[c // 2, (c % 2) * T:(c % 2 + 1) * T, :].T) for c in range(8)]
    for gi, grp in enumerate(LAUNCH_GROUPS):
        xT = _run(xT, inp, grp, do_final=(gi == len(LAUNCH_GROUPS) - 1))
    out = np.empty((4, S, D), np.float32)
    for c in range(8):
        out[c // 2, (c % 2) * T:(c % 2 + 1) * T, :] = xT[c].T
    return out
```

```python
import contextlib
import numpy as np
import ml_dtypes
import concourse.bass as bass
import concourse.mybir as mybir
from concourse.bass_utils import run_bass_kernel_spmd

F32 = mybir.dt.float32
BF16 = mybir.dt.bfloat16
AF = mybir.ActivationFunctionType
ALU = mybir.AluOpType

D = 2048
S = 2048
T = 1024
NT = 2
DEPTH = 4
DFF = 8192
PLE = 256
WINDOWS = (2, 4, 8, 16)
EPS = 1e-6
NCH = 16

PE, ACT, DVE, POOL, SP = "pe", "act", "dve", "pool", "sp"
SEM_ROT = 3000
N_DMA_SEMS = 12

NORM_COL = {}
_c = 0
for _n in ["ev0", "ev1", "od0", "od1", "mlp0", "mlp1", "mlp2", "mlp3", "ple0", "ple1", "ple2", "ple3", "fin"]:
    NORM_COL[_n] = _c
    _c += 16
CONVW_COL = _c; _c += 2 * 31 * 8
CONVB_COL = _c; _c += 16
LNG_COL = _c; _c += 16
LNB_COL = _c; _c += 16
PSC_COL = _c; _c += 16
NV = _c


class Op:
    __slots__ = ("eng", "fn", "deps", "kind", "signalled", "tok", "idx")

    def __init__(self, eng, fn, kind):
        self.eng = eng
        self.fn = fn
        self.kind = kind
        self.deps = set()
        self.signalled = False
        self.tok = None


class Prog:
    def __init__(self):
        self.ops = []
        self.streams = {e: [] for e in (PE, ACT, DVE, POOL, SP)}
        self.lastw = {}
        self.readers = {}

    def op(self, eng, fn, reads=(), writes=(), kind="c"):
        o = Op(eng, fn, kind)
        o.idx = len(self.ops)
        self.ops.append(o)
        self.streams[eng].append(o)
        deps = o.deps
        for k in reads:
            w = self.lastw.get(k)
            if w is not None:
                deps.add(w)
        for k in writes:
            w = self.lastw.get(k)
            if w is not None:
                deps.add(w)
            rs = self.readers.get(k)
            if rs:
                deps.update(rs.values())
        for k in reads:
            rs = self.readers.setdefault(k, {})
            rk = eng if kind == "c" else ("d", o.idx)
            rs[rk] = o.idx
        for k in writes:
            self.lastw[k] = o.idx
            self.readers[k] = {}
        deps.discard(o.idx)
        return o

    def emit(self, nc):
        ops = self.ops
        for o in ops:
            for d in o.deps:
                p = ops[d]
                if p.kind == "c" and p.eng == o.eng and p.eng == PE:
                    continue
                p.signalled = True
        engs = {PE: nc.tensor, ACT: nc.scalar, DVE: nc.vector, POOL: nc.gpsimd, SP: nc.sync}
        stack = contextlib.ExitStack()
        sem_lists = {}
        for e in engs:
            n_sig = sum(1 for o in self.streams[e] if o.kind == "c" and o.signalled)
            n = max(1, (n_sig + SEM_ROT - 1) // SEM_ROT)
            sem_lists[e] = [stack.enter_context(nc.semaphore(f"s_{e}_{i}")) for i in range(n)]
        dma_sems = {e: [stack.enter_context(nc.semaphore(f"d_{e}_{i}")) for i in range(N_DMA_SEMS)]
                    for e in (POOL, SP)}
        cc_sem = stack.enter_context(nc.semaphore("cc_sem"))
        cc_count = 0
        for e in engs:
            cnt = 0
            dma_i = 0
            dma_val = {}
            for o in self.streams[e]:
                if o.kind == "c":
                    if o.signalled:
                        o.tok = (sem_lists[e][cnt // SEM_ROT], cnt % SEM_ROT + 1)
                        cnt += 1
                elif o.kind == "d":
                    si = dma_i % N_DMA_SEMS
                    dma_i += 1
                    prev = dma_val.get(si, 0)
                    o.tok = (dma_sems[e][si], prev + 16, prev)
                    dma_val[si] = prev + 16
                elif o.kind == "cc":
                    cc_count += 1
                    o.tok = (cc_sem, cc_count)
        block = stack.enter_context(nc.Block())
        streams = self.streams

        def run_stream(e, eng):
            waited = {}
            pos = {o.idx: i for i, o in enumerate(streams[e])}
            for o in streams[e]:
                need = {}
                for d in o.deps:
                    p = ops[d]
                    if p.kind == "c" and p.eng == e and e == PE:
                        continue
                    if p.kind == "c" and p.eng == e and e in (DVE, ACT) and pos[o.idx] - pos[p.idx] >= 3:
                        continue
                    s, v = p.tok[0], p.tok[1]
                    key = id(s)
                    if waited.get(key, 0) >= v:
                        continue
                    if key not in need or need[key][1] < v:
                        need[key] = (s, v)
                if o.kind == "d" and o.tok[2] > 0:
                    s, v = o.tok[0], o.tok[2]
                    key = id(s)
                    if waited.get(key, 0) < v and (key not in need or need[key][1] < v):
                        need[key] = (s, v)
                for key, (s, v) in need.items():
                    eng.wait_ge(s, v)
                    waited[key] = v
                ins = o.fn(eng)
                if o.kind == "c":
                    if o.signalled:
                        ins.then_inc(o.tok[0], 1)
                elif o.kind == "d":
                    ins.then_inc(o.tok[0], 16)
                else:
                    ins.then_inc(o.tok[0])
            dv = {}
            for o in streams[e]:
                if o.kind == "d":
                    dv[id(o.tok[0])] = (o.tok[0], o.tok[1])
            for s, v in dv.values():
                if waited.get(id(s), 0) < v:
                    eng.wait_ge(s, v)

        @block.tensor
        def _(eng):
            run_stream(PE, eng)

        @block.scalar
        def _(eng):
            run_stream(ACT, eng)

        @block.vector
        def _(eng):
            run_stream(DVE, eng)

        @block.gpsimd
        def _(eng):
            run_stream(POOL, eng)

        @block.sync
        def _(eng):
            run_stream(SP, eng)

        stack.close()


class Builder:
    def __init__(self, layers, do_final, x_from_dram_T=True):
        self.layers = list(layers)
        self.do_final = do_final
        self.nc = bass.Bass("TRN2", target_bir_lowering=False)
        self.P = Prog()
        self.bank_i = 0
        self.slot_i = 0
        self.cnt = {}
        self.prebuilt = None
        self.stats_ready = False
        self.pending_stats = []

    def din(self, name, shape, dt=F32):
        return self.nc.dram_tensor(name, list(shape), dt, kind="ExternalInput").ap()

    def next_bank(self):
        b = self.bank_i % 6
        self.bank_i += 1
        return b

    def next_slot(self):
        s = self.slot_i % 4
        self.slot_i += 1
        return s

    def ring(self, name, n):
        i = self.cnt.get(name, 0)
        self.cnt[name] = i + 1
        return i % n

    def r2keys(self, row, a, b):
        ks = []
        if a < 512:
            ks.append(("r2", row, 0))
        if b > 512:
            ks.append(("r2", row, 1))
        return ks

    def build(self):
        nc = self.nc
        P = self.P
        L = self.layers
        st = contextlib.ExitStack()
        self.st = st

        self.x_in = self.din("xT", [D, T])
        self.p_in = self.din("pT", [DEPTH, PLE, T])
        self.vecs_in = self.din("vecs", [128, NV])
        self.masks_in = self.din("masks", [128, 2])
        self.edge_in = self.din("edge", [128, 64])
        self.ccsc_in = self.din("ccsc", [256, 512])
        self.ident_in = self.din("ident", [128, 128])
        self.cs_in = self.din("cs", [2, 128, 4096], BF16)
        self.nss_in = self.din("nss", [2, 128, 4096], BF16)
        self.w = {}
        for i in L:
            j = i // 2
            if i % 2 == 0:
                self.w[f"ev_w_in{j}"] = self.din(f"ev_w_in{j}", [D, 3072])
                self.w[f"ev_pool_w{j}"] = self.din(f"ev_pool_w{j}", [1024, 256])
                self.w[f"ev_w_out{j}"] = self.din(f"ev_w_out{j}", [D, D])
            else:
                self.w[f"od_w_out{j}"] = self.din(f"od_w_out{j}", [D, D])
            self.w[f"mlp_w1{i}"] = self.din(f"mlp_w1{i}", [D, DFF])
            self.w[f"mlp_w2{i}"] = self.din(f"mlp_w2{i}", [DFF, D])
            self.w[f"ple_w_gate{i}"] = self.din(f"ple_w_gate{i}", [D, D])
            self.w[f"ple_w_proj{i}"] = self.din(f"ple_w_proj{i}", [PLE, D])
        self.out = nc.dram_tensor("outT", [D, T], F32, kind="ExternalOutput").ap()
        self.halo_in = [nc.dram_tensor(f"halo_in{k}", [128, 256], BF16) for k in range(2)]
        self.halo_out = [nc.dram_tensor(f"halo_out{k}", [256, 256], BF16) for k in range(2)]
        self.hsend = [nc.dram_tensor(f"hsend{q}", [128, 2 * T], BF16) for q in range(8)]
        self.hfull = [nc.dram_tensor(f"hfull{q}", [256, 2 * T], BF16) for q in range(8)]

        def sb(name, shape, dt):
            return st.enter_context(nc.sbuf_tensor(name, shape, dt))

        self.xT = sb("xT_sb", [128, NCH, T], F32)
        self.wring = [sb(f"wring{i}", [128, 8, 512], BF16) for i in range(4)]
        self.R1 = sb("R1", [128, NCH, T], BF16)
        R2 = sb("R2", [128, 16 * 1056], BF16)
        self.R2v = R2[:].rearrange("p (c t) -> p c t", t=1056)
        R3 = sb("R3", [128, 8192], BF16)
        self.R3b = R3[:].rearrange("p (c t) -> p c t", t=1024)
        self.R3f = R3[:].bitcast(F32).rearrange("p (c t) -> p c t", t=512)
        self.R3h = R3[:].rearrange("p (b k r t) -> p b k r t", b=2, k=2, r=2)
        self.wpT = R3[:, 0:4096].rearrange("p (k n) -> p k n", k=2)
        self.pT = R3[:, 4096:6144].rearrange("p (k t) -> p k t", k=2)
        R1f = self.R1[:].rearrange("p c t -> p (c t)").bitcast(F32)
        self.tA = R1f[:, 0:1040]
        self.tB = R1f[:, 1536:1536 + 1040]
        self.vecs = sb("vecs_sb", [128, NV], F32)
        self.masks = sb("masks_sb", [128, 2], F32)
        self.edge = sb("edge_sb", [128, 4, 16], F32)
        self.ccsc = sb("ccsc_sb", [128, 2, 512], BF16)
        self.poolw = sb("poolw_sb", [128, 8, 256], BF16)
        self.haloT = [sb(f"haloT_sb{k}", [128, 2, 256], BF16) for k in range(2)]
        self.ones_bf = sb("ones_bf", [128, 128], BF16)
        self.ones_ln = sb("ones_ln", [128, 128], BF16)
        self.eps = sb("eps_sb", [128, 1], F32)
        self.ident = sb("ident_sb", [128, 128], BF16)
        self.convw_bf = sb("convw_bf", [128, 2 * 31 * 8], BF16)
        self.sq = [sb(f"sq{i}", [128, 512], BF16) for i in range(4)]
        self.lnt = sb("lnt", [128, 512], F32)
        self.rstd = sb("rstd", [128, T], F32)
        self.f32tmp = [sb(f"f32tmp{i}", [128, 512], F32) for i in range(3)]
        self.edget = sb("edget", [128, 8], F32)
        self.edgeres = [sb(f"edgeres{i}", [128, 8], F32) for i in range(2)]
        self.ps = [st.enter_context(nc.psum_tensor(f"ps{i}", [128, 512], F32)) for i in range(8)]

        P.op(SP, lambda e: e.dma_start(out=self.vecs[:], in_=self.vecs_in), writes=["vecs"], kind="d")
        P.op(SP, lambda e: e.dma_start(out=self.masks[:], in_=self.masks_in), writes=["masks"], kind="d")
        P.op(SP, lambda e: e.dma_start(out=self.edge[:], in_=self.edge_in.rearrange("p (g t) -> p g t", t=16)),
             writes=["edge"], kind="d")
        P.op(POOL, lambda e: e.dma_start(out=self.ccsc[:], in_=self.ccsc_in.rearrange("(k p) n -> p k n", p=128)),
             writes=["ccsc"], kind="d")
        P.op(POOL, lambda e: e.dma_start(out=self.ident[:], in_=self.ident_in), writes=["ident"], kind="d")
        P.op(DVE, lambda e: e.memset(self.ones_bf[:], 1.0 / D), writes=["ones_bf"])
        P.op(DVE, lambda e: e.memset(self.ones_ln[:], 1.0 / 1024), writes=["ones_ln"])
        P.op(DVE, lambda e: e.memset(self.eps[:], EPS), writes=["eps"])
        P.op(DVE, lambda e: e.tensor_copy(out=self.convw_bf[:], in_=self.vecs[:, CONVW_COL:CONVW_COL + 2 * 31 * 8]),
             reads=["vecs"], writes=["convw_bf"])
        xin = self.x_in.rearrange("(c p) t -> p c t", p=128)
        for q in range(4):
            P.op(SP, (lambda q: lambda e: e.dma_start(out=self.xT[:, 4 * q:4 * q + 4, :], in_=xin[:, 4 * q:4 * q + 4, :]))(q),
                 writes=[("x", c, tt) for c in range(4 * q, 4 * q + 4) for tt in range(NT)], kind="d")

        for i in L:
            if i % 2 == 0:
                self.even_mixer(i)
            else:
                self.odd_mixer(i)
            if DEBUG_STOP in ("rms", "gather", "chdft", "seqdft", "mixer", "win", "halo", "pool", "poolw", "conv"):
                continue
            self.mlp(i)
            if DEBUG_STOP == "mlp":
                continue
            self.ple(i)
        if self.do_final:
            self.rmsnorm(NORM_COL["fin"], mode="final")
        outv = self.out.rearrange("(c p) t -> p c t", p=128)
        for q in range(4):
            P.op(SP, (lambda q: lambda e: e.dma_start(out=outv[:, 4 * q:4 * q + 4, :], in_=self.xT[:, 4 * q:4 * q + 4, :]))(q),
                 reads=[("x", c, tt) for c in range(4 * q, 4 * q + 4) for tt in range(NT)], kind="d")
        P.emit(nc)
        st.close()
        return nc

    def stat_update(self, n, tt, first, last):
        P = self.P
        sl = slice(tt * 512, (tt + 1) * 512)
        qi = self.ring("sq", 4)
        sq = self.sq[qi]
        P.op(ACT, lambda e: e.activation(out=sq[:], in_=self.xT[:, n, sl], func=AF.Square),
             reads=[("x", n, tt)], writes=[("sq", qi)])
        self.pending_stats.append((sq, qi, tt, first, last))

    def flush_stats(self, keep=0):
        P = self.P
        while len(self.pending_stats) > keep:
            sq, qi, tt, first, last = self.pending_stats.pop(0)
            P.op(PE, lambda e, sq=sq, tt=tt, first=first, last=last: e.matmul(
                self.ps[6 + tt][:], lhsT=self.ones_bf[:], rhs=sq[:], start=first, stop=last),
                reads=[("sq", qi), "ones_bf"], writes=[("ps", 6 + tt)])

    def scale_chunk(self, gcol, c, tt):
        P = self.P
        sl = slice(tt * 512, (tt + 1) * 512)
        g = self.vecs[:, gcol + c:gcol + c + 1]
        if c % 2 == 0:
            P.op(DVE, lambda e: e.tensor_scalar(out=self.R1[:, c, sl], in0=self.xT[:, c, sl], scalar1=g, scalar2=None, op0=ALU.mult),
                 reads=[("x", c, tt), "vecs"], writes=[("r1", c, tt)])
        else:
            P.op(ACT, lambda e: e.activation(out=self.R1[:, c, sl], in_=self.xT[:, c, sl], func=AF.Copy, scale=g),
                 reads=[("x", c, tt), "vecs"], writes=[("r1", c, tt)])

    def rmsnorm(self, gcol, mode="deferred", power=-0.5, prescaled=False):
        P = self.P
        xT, R1, vecs = self.xT, self.R1, self.vecs
        if not self.stats_ready:
            for tt in range(NT):
                for c in range(NCH):
                    self.stat_update(c, tt, c == 0, c == NCH - 1)
                    self.flush_stats(keep=1)
        self.flush_stats()
        self.stats_ready = False

        def rstd_tile(tt):
            sl = slice(tt * 512, (tt + 1) * 512)
            P.op(ACT, lambda e: e.activation(out=self.lnt[:], in_=self.ps[6 + tt][:], func=AF.Ln, bias=self.eps[:], scale=1.0),
                 reads=[("ps", 6 + tt), "eps"], writes=["lnt"])
            P.op(ACT, lambda e: e.activation(out=self.rstd[:, sl], in_=self.lnt[:], func=AF.Exp, scale=power),
                 reads=["lnt"], writes=[("rstd", tt)])
        if mode == "deferred" and prescaled:
            for tt in range(NT):
                rstd_tile(tt)
        elif mode == "deferred":
            for tt in range(NT):
                sl = slice(tt * 512, (tt + 1) * 512)
                for c in range(NCH):
                    g = vecs[:, gcol + c:gcol + c + 1]
                    if c % 2 == 0:
                        P.op(DVE, lambda e, c=c, sl=sl, g=g: e.tensor_scalar(out=R1[:, c, sl], in0=xT[:, c, sl], scalar1=g, scalar2=None, op0=ALU.mult),
                             reads=[("x", c, tt), "vecs"], writes=[("r1", c, tt)])
                    else:
                        P.op(ACT, lambda e, c=c, sl=sl, g=g: e.activation(out=R1[:, c, sl], in_=xT[:, c, sl], func=AF.Copy, scale=g),
                             reads=[("x", c, tt), "vecs"], writes=[("r1", c, tt)])
                rstd_tile(tt)
        else:
            for tt in range(NT):
                rstd_tile(tt)
            for c in range(NCH):
                for tt in range(NT):
                    sl = slice(tt * 512, (tt + 1) * 512)
                    g = vecs[:, gcol + c:gcol + c + 1]
                    if mode == "final":
                        P.op(DVE, lambda e, c=c, sl=sl, g=g: e.scalar_tensor_tensor(
                            out=xT[:, c, sl], in0=xT[:, c, sl], scalar=g, in1=self.rstd[:, sl], op0=ALU.mult, op1=ALU.mult),
                            reads=[("x", c, tt), ("rstd", tt), "vecs"], writes=[("x", c, tt)])
                    else:
                        P.op(DVE, lambda e, c=c, sl=sl, g=g: e.scalar_tensor_tensor(
                            out=R1[:, c, sl], in0=xT[:, c, sl], scalar=g, in1=self.rstd[:, sl], op0=ALU.mult, op1=ALU.mult),
                            reads=[("x", c, tt), ("rstd", tt), "vecs"], writes=[("r1", c, tt)])

    def dense(self, W, KC, col_blocks, rhs, epilogue, extra=None):
        P = self.P
        for (n0, wd) in col_blocks:
            nsl = (KC + 7) // 8
            slots = []
            for kh in range(nsl):
                slot = self.next_slot()
                slots.append(slot)
                wt = self.wring[slot]
                kn = min(8, KC - kh * 8)
                src = W[kh * 1024:kh * 1024 + kn * 128, n0:n0 + wd].rearrange("(kc p) n -> p kc n", p=128)
                P.op(POOL, (lambda wt, kn, src, wd: lambda e: e.dma_start(out=wt[:, 0:kn, 0:wd], in_=src))(wt, kn, src, wd),
                     writes=[("w", slot)], kind="d")
            for nci in range(wd // 128):
                n = n0 // 128 + nci
                for tt in range(NT):
                    bank = self.next_bank()
                    for kc in range(KC):
                        rap, rkeys = rhs(kc, tt)
                        slot = slots[kc // 8]
                        wt = self.wring[slot]
                        P.op(PE, (lambda wt, kc, nci, rap, bank: lambda e: e.matmul(
                            self.ps[bank][:], lhsT=wt[:, kc % 8, nci * 128:(nci + 1) * 128], rhs=rap,
                            start=(kc == 0), stop=(kc == KC - 1)))(wt, kc, nci, rap, bank),
                            reads=[("w", slot)] + rkeys, writes=[("ps", bank)])
                    epilogue(n, tt, bank)
                    self.flush_stats(keep=2)
        self.flush_stats()

    def rhs_r1(self, kc, tt):
        return self.R1[:, kc, tt * 512:(tt + 1) * 512], [("r1", kc, tt)]

    def rhs_a(self, kc, tt):
        return self.R2v[:, kc, tt * 512:(tt + 1) * 512], [("r2", kc, tt)]

    def make_ep_resid(self, stats=False, scaled=False, next_gcol=None):
        def ep(n, tt, bank):
            P = self.P
            sl = slice(tt * 512, (tt + 1) * 512)
            if scaled:
                fi = self.ring("f32tmp", 3)
                tmp = self.f32tmp[fi]
                P.op(DVE, lambda e: e.tensor_tensor(out=tmp[:], in0=self.ps[bank][:], in1=self.rstd[:, sl], op=ALU.mult),
                     reads=[("ps", bank), ("rstd", tt)], writes=[("f32tmp", fi)])
                P.op(DVE, lambda e: e.tensor_tensor(out=self.xT[:, n, sl], in0=tmp[:], in1=self.xT[:, n, sl], op=ALU.add),
                     reads=[("f32tmp", fi), ("x", n, tt)], writes=[("x", n, tt)])
            else:
                P.op(DVE, lambda e: e.tensor_tensor(out=self.xT[:, n, sl], in0=self.ps[bank][:], in1=self.xT[:, n, sl], op=ALU.add),
                     reads=[("ps", bank), ("x", n, tt)], writes=[("x", n, tt)])
            if stats:
                self.stat_update(n, tt, n == 0, n == NCH - 1)
            if next_gcol is not None:
                self.scale_chunk(next_gcol, n, tt)
        return ep

    def mlp(self, i):
        P = self.P
        P.op(POOL, lambda e: e.dma_start(out=self.pT, in_=self.p_in[i].rearrange("(k p) t -> p k t", p=128)),
             writes=[("r3", q) for q in range(8, 12)], kind="d")
        P.op(POOL, lambda e: e.dma_start(out=self.wpT, in_=self.w[f"ple_w_proj{i}"].rearrange("(k p) n -> p k n", p=128)),
             writes=[("r3", q) for q in range(0, 8)], kind="d")
        self.rmsnorm(NORM_COL[f"mlp{i}"], mode="deferred", power=-1.0)
        W1 = self.w[f"mlp_w1{i}"]
        W2 = self.w[f"mlp_w2{i}"]
        for fb in range(4):
            def ep1(n, tt, bank, fb=fb):
                nl = n - fb * 16
                ri = self.ring("sq", 4)
                rt = self.sq[ri]
                P.op(ACT, lambda e: e.activation(out=rt[:], in_=self.ps[bank][:], func=AF.Relu),
                     reads=[("ps", bank)], writes=[("sq", ri)])
                P.op(DVE, lambda e: e.scalar_tensor_tensor(
                    out=self.R2v[:, nl, tt * 512:(tt + 1) * 512], in0=self.ps[bank][:], scalar=0.0, in1=rt[:],
                    op0=ALU.max, op1=ALU.mult),
                    reads=[("ps", bank), ("sq", ri)], writes=[("r2", nl, tt)])
            self.dense(W1, 16, [(fb * 2048 + b * 512, 512) for b in range(4)], self.rhs_r1, ep1)
            self.dense(W2[fb * 2048:(fb + 1) * 2048, :], 16, [(b * 512, 512) for b in range(4)], self.rhs_a,
                       self.make_ep_resid(stats=(fb == 3), scaled=True,
                                          next_gcol=(NORM_COL[f"ple{i}"] if fb == 3 else None)))
        self.stats_ready = True

    def ple(self, i):
        P = self.P
        self.rmsnorm(NORM_COL[f"ple{i}"], mode="deferred", prescaled=True)

        def ep(n, tt, bank):
            sl = slice(tt * 512, (tt + 1) * 512)
            bank2 = self.next_bank()
            for kc in range(2):
                P.op(PE, (lambda kc: lambda e: e.matmul(self.ps[bank2][:], lhsT=self.wpT[:, kc, n * 128:(n + 1) * 128],
                                                        rhs=self.pT[:, kc, sl], start=(kc == 0), stop=(kc == 1)))(kc),
                     reads=[("r3", q) for q in range(12)], writes=[("ps", bank2)])
            fi = self.ring("f32tmp", 3)
            sg = self.f32tmp[fi]
            P.op(DVE, lambda e: e.tensor_tensor(out=sg[:], in0=self.ps[bank][:], in1=self.rstd[:, sl], op=ALU.mult),
                 reads=[("ps", bank), ("rstd", tt)], writes=[("f32tmp", fi)])
            P.op(ACT, lambda e: e.activation(out=sg[:], in_=sg[:], func=AF.Sigmoid),
                 reads=[("f32tmp", fi)], writes=[("f32tmp", fi)])
            P.op(DVE, lambda e: e.tensor_tensor(out=sg[:], in0=sg[:], in1=self.ps[bank2][:], op=ALU.mult),
                 reads=[("ps", bank2), ("f32tmp", fi)], writes=[("f32tmp", fi)])
            P.op(DVE, lambda e: e.tensor_tensor(out=self.xT[:, n, sl], in0=sg[:], in1=self.xT[:, n, sl], op=ALU.add),
                 reads=[("f32tmp", fi), ("x", n, tt)], writes=[("x", n, tt)])
            self.stat_update(n, tt, n == 0, n == NCH - 1)
        self.dense(self.w[f"ple_w_gate{i}"], 16, [(b * 512, 512) for b in range(4)], self.rhs_r1, ep)
        self.stats_ready = True

    def even_mixer(self, i):
        P = self.P
        j = i // 2
        vecs = self.vecs
        R2v, R3b, R3f, R1 = self.R2v, self.R3b, self.R3f, self.R1
        P.op(POOL, lambda e: e.dma_start(out=self.poolw[:], in_=self.w[f"ev_pool_w{j}"].rearrange("(k p) n -> p k n", p=128)),
             writes=["poolw"], kind="d")
        self.rmsnorm(NORM_COL[f"ev{j}"], mode="deferred")

        def ep_in(n, tt, bank):
            sl = slice(tt * 512, (tt + 1) * 512)
            rs = self.rstd[:, sl]
            if n < 16:
                fi = self.ring("f32tmp", 3)
                tmp = self.f32tmp[fi]
                P.op(DVE, lambda e: e.tensor_tensor(out=tmp[:], in0=self.ps[bank][:], in1=rs, op=ALU.mult),
                     reads=[("ps", bank), ("rstd", tt)], writes=[("f32tmp", fi)])
            if 8 <= n < 16:
                c = n - 8
                P.op(ACT, lambda e: e.activation(out=R3b[:, c, sl], in_=tmp[:], func=AF.Sigmoid),
                     reads=[("f32tmp", fi)], writes=[("r3", 2 * c + tt)])
            elif n < 8:
                c = n
                a0 = 16 + tt * 512
                P.op(DVE, lambda e: e.tensor_tensor(out=R2v[:, c, a0:a0 + 512], in0=tmp[:], in1=R3b[:, c, sl], op=ALU.mult),
                     reads=[("f32tmp", fi), ("r3", 2 * c + tt)], writes=self.r2keys(c, a0, a0 + 512))
            else:
                c = n - 16
                a0 = 8 + tt * 512
                P.op(DVE, lambda e: e.tensor_tensor(out=R2v[:, 8 + c, a0:a0 + 512], in0=self.ps[bank][:], in1=rs, op=ALU.mult),
                     reads=[("ps", bank), ("rstd", tt)], writes=self.r2keys(8 + c, a0, a0 + 512))
        blocks = [(1024, 512), (1536, 512), (2048, 512), (2560, 512), (0, 512), (512, 512)]
        self.dense(self.w[f"ev_w_in{j}"], 16, blocks, self.rhs_r1, ep_in)

        if DEBUG_STOP == "win":
            return
        mL = self.masks[:, 0:1]
        mR = self.masks[:, 1:2]
        h4 = []
        for k, (r0, c_first, c_last) in enumerate(((8, 8, 1016), (0, 16, 1024))):
            hin = self.halo_in[k].ap().rearrange("p (q s t) -> p q s t", q=8, s=2)
            rows = [("r2", r, h) for r in range(r0, r0 + 8) for h in range(2)]
            P.op(SP, lambda e, hin=hin, r0=r0, c_first=c_first: e.dma_start(out=hin[:, :, 0, :], in_=R2v[:, r0:r0 + 8, c_first:c_first + 16]),
                 reads=rows, writes=[("halo_in", k, 0)], kind="d")
            P.op(SP, lambda e, hin=hin, r0=r0, c_last=c_last: e.dma_start(out=hin[:, :, 1, :], in_=R2v[:, r0:r0 + 8, c_last:c_last + 16]),
                 reads=rows, writes=[("halo_in", k, 1)], kind="d")
            P.op(POOL, lambda e, k=k: e.collective_compute("AllGather", ALU.bypass, replica_groups=[[0, 1], [2, 3], [4, 5], [6, 7]],
                                                           ins=[self.halo_in[k].ap().opt()], outs=[self.halo_out[k].ap().opt()]),
                 reads=[("halo_in", k, 0), ("halo_in", k, 1)], writes=[("halo_out", k)], kind="cc")
            P.op(SP, lambda e, k=k: e.dma_start(out=self.haloT[k][:], in_=self.halo_out[k].ap().rearrange("(r p) n -> p r n", p=128)),
                 reads=[("halo_out", k)], writes=[("haloT", k)], kind="d")
            h4.append(self.haloT[k][:].rearrange("p r (q s t) -> p r q s t", q=8, s=2))
        hu, hv = h4
        P.op(DVE, lambda e: e.tensor_scalar(out=R2v[:, 8:16, 0:8], in0=hu[:, 0, :, 1, 8:16], scalar1=mL, scalar2=None, op0=ALU.mult),
             reads=[("haloT", 0), "masks"], writes=[("r2", r, 0) for r in range(8, 16)])
        P.op(DVE, lambda e: e.tensor_scalar(out=R2v[:, 8:16, 1032:1040], in0=hu[:, 1, :, 0, 0:8], scalar1=mR, scalar2=None, op0=ALU.mult),
             reads=[("haloT", 0), "masks"], writes=[("r2", r, 1) for r in range(8, 16)])
        self._conv_ln_tile(j, 0)
        P.op(DVE, lambda e: e.tensor_scalar(out=R2v[:, 0:8, 0:16], in0=hv[:, 0, :, 1, :], scalar1=mL, scalar2=None, op0=ALU.mult),
             reads=[("haloT", 1), "masks"], writes=[("r2", r, 0) for r in range(8)])
        P.op(DVE, lambda e: e.tensor_scalar(out=R2v[:, 0:8, 1040:1056], in0=hv[:, 1, :, 0, :], scalar1=mR, scalar2=None, op0=ALU.mult),
             reads=[("haloT", 1), "masks"], writes=[("r2", r, 1) for r in range(8)])
        self._conv_ln_tile(j, 1)

        if DEBUG_STOP == "halo":
            return
        self._pool_done = [False] * 8
        if DEBUG_STOP == "pool":
            return
        if DEBUG_STOP == "poolw":
            return
        if DEBUG_STOP == "conv":
            return
        for g in range(4):
            for dc in range(2):
                for tt in range(NT):
                    sl = slice(tt * 512, (tt + 1) * 512)
                    bank = self.next_bank()
                    for kc in range(2):
                        P.op(PE, lambda e, g=g, dc=dc, kc=kc, bank=bank, sl=sl: e.matmul(
                            self.ps[bank][:], lhsT=self.poolw[:, 2 * g + kc, dc * 128:(dc + 1) * 128],
                            rhs=R2v[:, 8 + 2 * g + kc, sl], start=(kc == 0), stop=(kc == 1)),
                            reads=["poolw", ("r2", 8 + 2 * g + kc, tt)], writes=[("ps", bank)])
                    oc = 2 * g + dc
                    col = PSC_COL + j * 8 + oc
                    P.op(DVE, lambda e, oc=oc, col=col, bank=bank, sl=sl: e.tensor_scalar(
                        out=R1[:, 8 + oc, sl], in0=self.ps[bank][:], scalar1=vecs[:, col:col + 1], scalar2=None, op0=ALU.mult),
                        reads=[("ps", bank), "vecs"], writes=[("r1", 8 + oc, tt)])
        self.dense(self.w[f"ev_w_out{j}"], 16, [(b * 512, 512) for b in range(4)], self.rhs_r1, self.make_ep_resid(stats=True))
        self.stats_ready = True

    def _pool_chunk(self, c):
        P = self.P
        R2v = self.R2v
        tA, tB = self.tA, self.tB
        TAK = [("r1", r, tt) for r in range(0, 3) for tt in range(NT)]
        TBK = [("r1", r, tt) for r in range(3, 6) for tt in range(NT)]
        g = c // 2
        w = WINDOWS[g]
        row = 8 + c
        rk = [("r2", row, 0), ("r2", row, 1)]
        u = R2v[:, row, :]
        P.op(DVE, lambda e, u=u: e.tensor_tensor(out=tA[:, 1:1040], in0=u[:, 0:1039], in1=u[:, 1:1040], op=ALU.add),
             reads=rk, writes=TAK)
        cur, curk = tA, TAK
        if w >= 4:
            P.op(DVE, lambda e: e.tensor_tensor(out=tB[:, 2:1038], in0=tA[:, 1:1037], in1=tA[:, 3:1039], op=ALU.add),
                 reads=TAK, writes=TBK)
            cur, curk = tB, TBK
        if w >= 8:
            P.op(DVE, lambda e: e.tensor_tensor(out=tA[:, 4:1036], in0=tB[:, 2:1034], in1=tB[:, 6:1038], op=ALU.add),
                 reads=TBK, writes=TAK)
            cur, curk = tA, TAK
        if w >= 16:
            P.op(DVE, lambda e: e.tensor_tensor(out=tB[:, 8:1032], in0=tA[:, 4:1028], in1=tA[:, 12:1036], op=ALU.add),
                 reads=TAK, writes=TBK)
            cur, curk = tB, TBK
        for ei, (ecol, tcol) in enumerate(((0, 0), (8, 1016))):
            et = self.edgeres[ei]
            P.op(DVE, lambda e, cur=cur, g=g, ecol=ecol, tcol=tcol: e.tensor_tensor(
                out=self.edget[:], in0=cur[:, 8 + tcol:16 + tcol], in1=self.edge[:, g, ecol:ecol + 8], op=ALU.mult),
                reads=curk + ["edge"], writes=["edget"])
            P.op(DVE, lambda e, u=u, et=et, tcol=tcol: e.tensor_tensor(
                out=et[:], in0=self.edget[:], in1=u[:, 8 + tcol:16 + tcol], op=ALU.subtract),
                reads=["edget"] + rk, writes=[("edgeres", ei)])
        P.op(DVE, lambda e, cur=cur, u=u, w=w: e.scalar_tensor_tensor(
            out=u[:, 0:1024], in0=cur[:, 8:1032], scalar=1.0 / w, in1=u[:, 8:1032], op0=ALU.mult, op1=ALU.subtract),
            reads=curk + rk, writes=rk)
        for ei, tcol in enumerate((0, 1016)):
            et = self.edgeres[ei]
            P.op(DVE, lambda e, u=u, et=et, tcol=tcol: e.tensor_copy(out=u[:, tcol:tcol + 8], in_=et[:]),
                 reads=[("edgeres", ei)], writes=rk)


    def _build_dset(self, j, c):
        P = self.P
        vecs, R1 = self.vecs, self.R1
        db = self.ring("dset", 2)
        dk = [("r1", r, h) for r in range(8 + 4 * db, 12 + 4 * db) for h in range(NT)]
        dset = R1[:, 8 + 4 * db:12 + 4 * db, :].rearrange("p c t -> p (c t)")[:, 0:31 * 128].rearrange("p (a m) -> p a m", m=128)
        w0 = (j * 31) * 8 + c
        wv = self.convw_bf[:, w0:w0 + 30 * 8 + 1:8].unsqueeze(2).broadcast_to([128, 31, 128])
        idb = self.ident[:].unsqueeze(1).broadcast_to([128, 31, 128])
        P.op(DVE, lambda e: e.tensor_tensor(out=dset, in0=idb, in1=wv, op=ALU.mult),
             reads=["ident", "convw_bf"], writes=dk)
        return dset, dk

    def _conv_ln_tile(self, j, tt):
        P = self.P
        vecs = self.vecs
        R2v, R3b, R3f, R1 = self.R2v, self.R3b, self.R3f, self.R1
        if True:
            def vwin(c, tap):
                if tt == 0:
                    return R2v[:, c, 257 + tap:769 + tap]
                return R2v[:, c, 1 + tap:1025 + tap].rearrange("p (s t) -> p s t", s=4)[:, 0:4:3, :]

            def catwin(c):
                if tt == 0:
                    return R1[:, c, 256:768]
                return R1[:, c, :].rearrange("p (s t) -> p s t", s=4)[:, 0:4:3, :]

            def cwin(c):
                if tt == 0:
                    return R3f[:, c, :]
                return R3f[:, c, :].rearrange("p (s t) -> p s t", s=2)
            bank_m = 6
            bank_s = 7
            pending = None

            def stats(c, cb, bi, sqb, qi):
                P.op(PE, lambda e: e.matmul(self.ps[bank_m][:], lhsT=self.ones_ln[:], rhs=cb[:], start=(c == 0), stop=(c == 7)),
                     reads=[("sq", bi), "ones_ln"], writes=[("ps", bank_m)])
                P.op(PE, lambda e: e.matmul(self.ps[bank_s][:], lhsT=self.ones_ln[:], rhs=sqb[:], start=(c == 0), stop=(c == 7)),
                     reads=[("sq", qi), "ones_ln"], writes=[("ps", bank_s)])
            for c in range(8):
                vk = [("r2", c, 0), ("r2", c, 1)]
                ck = [("r3", 2 * c), ("r3", 2 * c + 1)]
                bc = CONVB_COL + j * 8 + c
                bank_c = self.ring("convbank", 6)
                if self.prebuilt is None:
                    self.prebuilt = self._build_dset(j, c)
                dset, dk = self.prebuilt
                self.prebuilt = None
                if c < 7:
                    self.prebuilt = self._build_dset(j, c + 1)
                elif tt == 0:
                    self.prebuilt = self._build_dset(j, 0)
                if tt == 0 and DEBUG_STOP != "halo":
                    self._pool_chunk(c)
                for tap in range(31):
                    P.op(PE, lambda e, dset=dset, c=c, tap=tap, bank_c=bank_c: e.matmul(
                        self.ps[bank_c][:], lhsT=dset[:, tap, :], rhs=vwin(c, tap),
                        start=(tap == 0), stop=(tap == 30)),
                        reads=vk + dk, writes=[("ps", bank_c)])
                P.op(ACT, lambda e, c=c, bc=bc, bank_c=bank_c: e.activation(
                    out=R3f[:, c, :], in_=self.ps[bank_c][:], func=AF.Identity, bias=vecs[:, bc:bc + 1], scale=1.0),
                    reads=[("ps", bank_c), "vecs"], writes=ck)
                bi = self.ring("sq", 4)
                cb = self.sq[bi]
                P.op(ACT, lambda e, c=c, cb=cb: e.activation(out=cb[:], in_=R3f[:, c, :], func=AF.Copy),
                     reads=ck, writes=[("sq", bi)])
                qi = self.ring("sq", 4)
                sqb = self.sq[qi]
                P.op(ACT, lambda e, c=c, sqb=sqb: e.activation(out=sqb[:], in_=R3f[:, c, :], func=AF.Square),
                     reads=ck, writes=[("sq", qi)])
                if pending is not None:
                    stats(*pending)
                pending = (c, cb, bi, sqb, qi)
            stats(*pending)
            msq, nmean, var = self.f32tmp[0], self.f32tmp[1], self.f32tmp[2]
            P.op(ACT, lambda e: e.activation(out=msq[:], in_=self.ps[bank_m][:], func=AF.Square),
                 reads=[("ps", bank_m)], writes=[("f32tmp", 0)])
            P.op(ACT, lambda e: e.activation(out=nmean[:], in_=self.ps[bank_m][:], func=AF.Copy, scale=-1.0),
                 reads=[("ps", bank_m)], writes=[("f32tmp", 1)])
            P.op(DVE, lambda e: e.tensor_tensor(out=var[:], in0=self.ps[bank_s][:], in1=msq[:], op=ALU.subtract),
                 reads=[("ps", bank_s), ("f32tmp", 0)], writes=[("f32tmp", 2)])
            P.op(ACT, lambda e: e.activation(out=self.lnt[:], in_=var[:], func=AF.Ln, bias=self.eps[:], scale=1.0),
                 reads=[("f32tmp", 2), "eps"], writes=["lnt"])
            P.op(ACT, lambda e: e.activation(out=var[:], in_=self.lnt[:], func=AF.Exp, scale=-0.5),
                 reads=["lnt"], writes=[("f32tmp", 2)])
            for c in range(8):
                ck = [("r3", 2 * c), ("r3", 2 * c + 1)]
                gcol = LNG_COL + j * 8 + c
                bcol = LNB_COL + j * 8 + c
                P.op(DVE, lambda e, c=c: e.tensor_tensor(out=R3f[:, c, :], in0=R3f[:, c, :], in1=nmean[:], op=ALU.add),
                     reads=ck + [("f32tmp", 1)], writes=ck)
                P.op(DVE, lambda e, c=c, gcol=gcol: e.scalar_tensor_tensor(
                    out=R3f[:, c, :], in0=R3f[:, c, :], scalar=vecs[:, gcol:gcol + 1], in1=var[:], op0=ALU.mult, op1=ALU.mult),
                    reads=ck + [("f32tmp", 2), "vecs"], writes=ck)
                P.op(ACT, lambda e, c=c, bcol=bcol: e.activation(out=catwin(c), in_=cwin(c), func=AF.Silu,
                                                                 bias=vecs[:, bcol:bcol + 1], scale=1.0),
                     reads=ck + ["vecs"], writes=[("r1", c, 0), ("r1", c, 1)])

    def odd_mixer(self, i):
        P = self.P
        j = i // 2
        R1, R2v, R3h = self.R1, self.R2v, self.R3h
        self.rmsnorm(NORM_COL[f"od{j}"], mode="direct")
        if DEBUG_STOP == "rms":
            return
        for q in range(8):
            hs = self.hsend[q].ap().rearrange("p (c t) -> p c t", c=2)
            P.op(SP, lambda e, q=q, hs=hs: e.dma_start(out=hs, in_=R1[:, 2 * q:2 * q + 2, :]),
                 reads=[("r1", c, tt) for c in range(2 * q, 2 * q + 2) for tt in range(NT)], writes=[("hsend", q)], kind="d")
            P.op(POOL, lambda e, q=q: e.collective_compute("AllGather", ALU.bypass, replica_groups=[[0, 1], [2, 3], [4, 5], [6, 7]],
                                                           ins=[self.hsend[q].ap().opt()], outs=[self.hfull[q].ap().opt()]),
                 reads=[("hsend", q)], writes=[("hfull", q)], kind="cc")
        hfs = [self.hfull[q].ap().rearrange("(r p) (c t) -> p c r t", r=2, c=2) for q in range(8)]
        if DEBUG_STOP == "gather":
            return
        tslots = []
        for e_ in range(2):
            for tab in (self.cs_in, self.nss_in):
                sl_ = self.next_slot()
                tslots.append(sl_)
                P.op(SP, lambda e, sl_=sl_, tab=tab, e_=e_: e.dma_start(
                    out=self.wring[sl_][:], in_=tab[e_].rearrange("p (s k) -> p s k", k=512)),
                    writes=[("w", sl_)], kind="d")
        for g in range(8):
            b = g % 2
            hk = [("r3", 8 * b + q) for q in range(8)]
            for r in range(2):
                P.op(SP, lambda e, g=g, b=b, r=r: e.dma_start(out=R3h[:, b, :, r, :], in_=hfs[g][:, :, r, :]),
                     reads=[("hfull", g)], writes=[("r3", 8 * b + 4 * k + 2 * r + h) for k in range(2) for h in range(2)], kind="d")
            for st8 in range(8):
                off = st8 * 128
                bankL = self.next_bank()
                bankH = self.next_bank()
                for r, bank in ((0, bankL), (1, bankH)):
                    for kc in range(2):
                        P.op(PE, lambda e, b=b, kc=kc, r=r, off=off, bank=bank: e.matmul(
                            self.ps[bank][:], lhsT=R3h[:, b, kc, r, off:off + 128], rhs=self.ccsc[:, kc, :],
                            start=(kc == 0), stop=(kc == 1)), reads=hk + ["ccsc"], writes=[("ps", bank)])
                fi = self.ring("f32tmp", 3)
                tmp = self.f32tmp[fi]
                P.op(ACT, lambda e, tmp=tmp, bankL=bankL: e.activation(out=tmp[:], in_=self.ps[bankL][:], func=AF.Copy),
                     reads=[("ps", bankL)], writes=[("f32tmp", fi)])
                P.op(DVE, lambda e, b=b, st8=st8, tmp=tmp, bankH=bankH: e.tensor_tensor(
                    out=R2v[:, st8, b * 512:(b + 1) * 512], in0=tmp[:], in1=self.ps[bankH][:], op=ALU.add),
                    reads=[("f32tmp", fi), ("ps", bankH)], writes=[("r2", st8, b)])
                P.op(DVE, lambda e, b=b, st8=st8, tmp=tmp, bankH=bankH: e.tensor_tensor(
                    out=R2v[:, 8 + st8, b * 512:(b + 1) * 512], in0=tmp[:], in1=self.ps[bankH][:], op=ALU.subtract),
                    reads=[("f32tmp", fi), ("ps", bankH)], writes=[("r2", 8 + st8, b)])
            if DEBUG_STOP == "chdft":
                continue
            for e_ in range(2):
                slc, sls = tslots[2 * e_], tslots[2 * e_ + 1]
                for mc in range(2):
                    bank = self.next_bank()
                    for s8 in range(8):
                        stl = e_ * 8 + s8
                        P.op(PE, lambda e, b=b, stl=stl, s8=s8, mc=mc, slc=slc, bank=bank: e.matmul(
                            self.ps[bank][:], lhsT=R2v[:, stl, b * 512 + mc * 128:b * 512 + (mc + 1) * 128],
                            rhs=self.wring[slc][:, s8, :], start=(s8 == 0), stop=False),
                            reads=[("r2", stl, b), ("w", slc)], writes=[("ps", bank)])
                        P.op(PE, lambda e, b=b, stl=stl, s8=s8, mc=mc, sls=sls, bank=bank: e.matmul(
                            self.ps[bank][:], lhsT=R2v[:, stl, b * 512 + 256 + mc * 128:b * 512 + 256 + (mc + 1) * 128],
                            rhs=self.wring[sls][:, s8, :], start=False, stop=(s8 == 7)),
                            reads=[("r2", stl, b), ("w", sls)], writes=[("ps", bank)])
                    row = 2 * g + mc
                    if mc == 0:
                        P.op(ACT, lambda e, row=row, e_=e_, bank=bank: e.activation(out=R1[:, row, e_::2], in_=self.ps[bank][:], func=AF.Copy),
                             reads=[("ps", bank)], writes=[("r1", row, 0), ("r1", row, 1)])
                    else:
                        P.op(DVE, lambda e, row=row, e_=e_, bank=bank: e.tensor_copy(out=R1[:, row, e_::2], in_=self.ps[bank][:]),
                             reads=[("ps", bank)], writes=[("r1", row, 0), ("r1", row, 1)])
        if DEBUG_STOP in ("chdft", "seqdft"):
            return
        self.dense(self.w[f"od_w_out{j}"], 16, [(b * 512, 512) for b in range(4)], self.rhs_r1, self.make_ep_resid(stats=True))
        self.stats_ready = True


_CACHE = {}


def _consts():
    if "c" in _CACHE:
        return _CACHE["c"]
    c = np.arange(256)
    ang = 2.0 * np.pi * ((c[:, None] * c[None, :]) % 256) / 256.0
    ccsc = np.concatenate([np.cos(ang), np.sin(ang)], axis=1) / 16.0
    s = np.arange(S)
    per_half = []
    for half in range(2):
        s_ = np.arange(T)

        def tabs(fn):
            out = np.zeros((2, 128, 4096), np.float64)
            for e_ in range(2):
                k = half * T + 2 * np.arange(512) + e_
                a = 2.0 * np.pi * ((s_[:, None] * k[None, :]) % S) / S
                m = fn(a) / np.sqrt(S)
                out[e_] = m.reshape(8, 128, 512).transpose(1, 0, 2).reshape(128, 4096)
            return np.ascontiguousarray(out.astype(ml_dtypes.bfloat16))
        cs_t = tabs(np.cos)
        nss_t = tabs(lambda a: -np.sin(a))
        edge = np.zeros((4, 16), np.float64)
        for gi, win in enumerate(WINDOWS):
            for q in range(16):
                tl = q if q < 8 else T - 16 + q
                tg = half * T + tl
                lo = min(max(tg - win // 2, 0), S - 1)
                hi = min(max(tg + win // 2 - 1, 0), S - 1) + 1
                edge[gi, q] = 1.0 / (hi - lo)
        masks = np.zeros((128, 2), np.float32)
        masks[:, 0] = 1.0 if half == 1 else 0.0
        masks[:, 1] = 1.0 if half == 0 else 0.0
        per_half.append(dict(cs=cs_t, nss=nss_t,
                             edge=np.ascontiguousarray(np.broadcast_to(edge.reshape(1, 64), (128, 64)).astype(np.float32)),
                             masks=masks))
    _CACHE["c"] = (ccsc.astype(np.float32), per_half)
    return _CACHE["c"]


def _pm(v):
    v = np.asarray(v, np.float32)
    return v.reshape(-1, 128).T


def _vecs(inp):
    vecs = np.zeros((128, NV), np.float32)
    names = [("ev0", inp["ev_norm"][0]), ("ev1", inp["ev_norm"][1]), ("od0", inp["od_norm"][0]), ("od1", inp["od_norm"][1])]
    names += [(f"mlp{i}", inp["mlp_norm"][i]) for i in range(4)]
    names += [(f"ple{i}", inp["ple_norm"][i]) for i in range(4)]
    names += [("fin", inp["final_norm"])]
    for n, v in names:
        vecs[:, NORM_COL[n]:NORM_COL[n] + 16] = _pm(v)
    for j in range(2):
        for tap in range(31):
            c0 = CONVW_COL + (j * 31 + tap) * 8
            vecs[:, c0:c0 + 8] = _pm(inp["ev_conv_w"][j, tap])
        vecs[:, CONVB_COL + j * 8:CONVB_COL + j * 8 + 8] = _pm(inp["ev_conv_b"][j])
        vecs[:, LNG_COL + j * 8:LNG_COL + j * 8 + 8] = _pm(inp["ev_ln_g"][j])
        vecs[:, LNB_COL + j * 8:LNB_COL + j * 8 + 8] = _pm(inp["ev_ln_b"][j])
        vecs[:, PSC_COL + j * 8:PSC_COL + j * 8 + 8] = _pm(inp["ev_pool_scale"][j])
    return vecs


def _get_nc(layers, do_final):
    key = (tuple(layers), do_final)
    if key not in _CACHE:
        _CACHE[key] = Builder(layers, do_final).build()
    return _CACHE[key]


def _run(xT_cores, inp, layers, do_final):
    ccsc, per_half = _consts()
    vecs = _vecs(inp)
    nc = _get_nc(layers, do_final)
    p = np.asarray(inp["p"], np.float32)
    in_maps = []
    for core in range(8):
        b, half = core // 2, core % 2
        ph = per_half[half]
        m = {
            "xT": xT_cores[core],
            "pT": np.ascontiguousarray(p[:, b, half * T:(half + 1) * T, :].transpose(0, 2, 1)),
            "vecs": vecs, "ident": np.eye(128, dtype=np.float32), "masks": ph["masks"], "edge": ph["edge"], "ccsc": ccsc,
            "cs": ph["cs"], "nss": ph["nss"],
        }
        for i in layers:
            j = i // 2
            if i % 2 == 0:
                m[f"ev_w_in{j}"] = np.asarray(inp["ev_w_in"][j], np.float32)
                m[f"ev_pool_w{j}"] = np.asarray(inp["ev_pool_w"][j], np.float32).reshape(1024, 256)
                m[f"ev_w_out{j}"] = np.asarray(inp["ev_w_out"][j], np.float32)
            else:
                m[f"od_w_out{j}"] = np.asarray(inp["od_w_out"][j], np.float32)
            m[f"mlp_w1{i}"] = np.asarray(inp["mlp_w1"][i], np.float32)
            m[f"mlp_w2{i}"] = np.asarray(inp["mlp_w2"][i], np.float32)
            m[f"ple_w_gate{i}"] = np.asarray(inp["ple_w_gate"][i], np.float32)
            m[f"ple_w_proj{i}"] = np.asarray(inp["ple_w_proj"][i], np.float32)
        in_maps.append(m)
    res = run_bass_kernel_spmd(nc, in_maps, core_ids=list(range(8)))
    return [np.asarray(res.results[c]["outT"]) for c in range(8)]


LAUNCH_GROUPS = [[0, 1, 2, 3]]
DEBUG_STOP = None


def kernel(**inputs):
    inp = {k: np.asarray(v) for k, v in inputs.items()}
    x = np.asarray(inp["x"], np.float32)
    xT = [np.ascontiguousarray(x[c // 2, (c % 2) * T:(c % 2 + 1) * T, :].T) for c in range(8)]
    for gi, grp in enumerate(LAUNCH_GROUPS):
        xT = _run(xT, inp, grp, do_final=(gi == len(LAUNCH_GROUPS) - 1))
    out = np.empty((4, S, D), np.float32)
    for c in range(8):
        out[c // 2, (c % 2) * T:(c % 2 + 1) * T, :] = xT[c].T
    return out
```
